# Optimizing a Trainium2 kernel written in Bass

```python
import math
import jax
import jax.numpy as jnp
from jax import lax
import numpy as np

D_MODEL = 1024
BATCH = 8
SEQ = 4096
DEPTH = 1

HYENA_WIDTH = D_MODEL // 2
ATTN_WIDTH = D_MODEL - HYENA_WIDTH
ATTN_HEAD_DIM = 64
ATTN_HEADS = ATTN_WIDTH // (2 * ATTN_HEAD_DIM)
IN_WIDTH = 3 * HYENA_WIDTH + 3 * ATTN_WIDTH
FILTER_EMB_DIM = 33
FILTER_ORDER = 64
FILTER_TARGET = 1e-2
FILTER_FAST_DECAY_PCT = 0.3
FILTER_SLOW_DECAY_PCT = 1.5
D_FF = 4 * D_MODEL
ROPE_THETA = 10000.0
Q_BLOCK = 128
NORM_EPS = 1e-6
SUBLN_EPS = 1e-5

kernel_name = 'hybrid_hyena_diffattn_encoder_block'

F32 = jnp.float32


def rms_norm(x, gain, eps=NORM_EPS):
    xf = x.astype(F32)
    y = xf * lax.rsqrt(jnp.mean(xf * xf, axis=-1, keepdims=True) + eps)
    return (y * gain.astype(F32)).astype(x.dtype)


def short_conv_centred(u, w, b):
    L = u.shape[1]
    up = jnp.pad(u, ((0, 0), (1, 1), (0, 0)))
    return up[:, 0:L] * w[0] + up[:, 1:L + 1] * w[1] + up[:, 2:L + 2] * w[2] + b


def hyena_positional_features(L):
    bands = (FILTER_EMB_DIM - 1) // 2
    t = jnp.linspace(0.0, 1.0, L, dtype=F32)[:, None]
    w = (2.0 * math.pi / L) * jnp.arange(L, dtype=F32)[:, None]
    f = jnp.linspace(1e-4, bands - 1, bands, dtype=F32)[None, :]
    fw = f * w
    return jnp.concatenate([t, jnp.cos(fw), -jnp.sin(fw)], axis=-1)


def hyena_decay_window(L):
    t = jnp.linspace(0.0, 1.0, L, dtype=F32)[:, None]
    max_decay = math.log(FILTER_TARGET) / FILTER_FAST_DECAY_PCT
    min_decay = math.log(FILTER_TARGET) / FILTER_SLOW_DECAY_PCT
    deltas = jnp.linspace(min_decay, max_decay, HYENA_WIDTH, dtype=F32)[None, :]
    return jnp.exp(-t * jnp.abs(deltas))


def implicit_filters(z, decay, w1, b1, w2, b2, w3, b3, w4, freq):
    freq = freq.astype(F32)
    h = jnp.sin(freq * (z @ w1.astype(F32) + b1.astype(F32)))
    h = jnp.sin(freq * (h @ w2.astype(F32) + b2.astype(F32)))
    h = jnp.sin(freq * (h @ w3.astype(F32) + b3.astype(F32)))
    h = (h @ w4.astype(F32)).reshape(z.shape[0], 2, HYENA_WIDTH)
    h = h * decay[:, None, :]
    return h[:, 0], h[:, 1]


def fft_conv(u, h):
    L = u.shape[1]
    n = 2 * L
    uf = jnp.fft.rfft(u, n=n, axis=1)
    hf = jnp.fft.rfft(h, n=n, axis=0)
    return jnp.fft.irfft(uf * hf[None], n=n, axis=1)[:, :L]


def bidirectional_long_conv(u, h_fwd, h_bwd, bias):
    u32 = u.astype(F32)
    y_fwd = fft_conv(u32, h_fwd)
    y_bwd = jnp.flip(fft_conv(jnp.flip(u32, axis=1), h_bwd), axis=1)
    return y_fwd + y_bwd + u32 * bias.astype(F32)


def hyena_mixer(u, conv_w, conv_b, h_fwd, h_bwd, filt_bias):
    u = short_conv_centred(u, conv_w, conv_b)
    x0, x1, v = jnp.split(u, 3, axis=-1)
    y = bidirectional_long_conv(v * x1, h_fwd, h_bwd, filt_bias)
    return (y * x0.astype(F32)).astype(u.dtype)


def rope_tables(L):
    inv = ROPE_THETA ** (-jnp.arange(0, ATTN_HEAD_DIM, 2, dtype=F32) / ATTN_HEAD_DIM)
    ang = jnp.arange(L, dtype=F32)[:, None] * inv[None, :]
    ang = jnp.concatenate([ang, ang], axis=-1)
    return jnp.cos(ang), jnp.sin(ang)


def apply_rope(x, cos, sin):
    x = x.astype(F32)
    x1, x2 = jnp.split(x, 2, axis=-1)
    rot = jnp.concatenate([-x2, x1], axis=-1)
    return x * cos[None, :, None, None, :] + rot * sin[None, :, None, None, :]


def differential_attention(q, k, v, lam):
    B, S = q.shape[0], q.shape[1]
    nb = S // Q_BLOCK
    qb = q.reshape(B, nb, Q_BLOCK, ATTN_HEADS, 2, ATTN_HEAD_DIM).transpose(1, 0, 3, 4, 2, 5)
    kt = k.transpose(0, 2, 3, 1, 4)
    vt = v.astype(F32).transpose(0, 2, 1, 3)

    def block(q_blk):
        s = jnp.einsum('bhcqd,bhcsd->bhcqs', q_blk, kt)
        p = jax.nn.softmax(s, axis=-1)
        w = p[:, :, 0] - lam * p[:, :, 1]
        return jnp.einsum('bhqs,bhse->bhqe', w, vt)

    o = lax.map(block, qb)
    return o.transpose(1, 0, 3, 2, 4).reshape(B, S, ATTN_HEADS, 2 * ATTN_HEAD_DIM)


def setup_inputs(seed: int = 0) -> dict:
    key = jax.random.key(seed)
    ks = jax.random.split(key, 32)

    def nrm(k, shape, std):
        return std * jax.random.normal(k, shape, F32)

    def gain(k, n):
        return 1.0 + nrm(k, (DEPTH, n), 0.01)

    return {
        'x': jax.random.normal(ks[0], (BATCH, SEQ, D_MODEL), F32),
        'attn_pre_gain': gain(ks[1], D_MODEL),
        'attn_post_gain': gain(ks[2], D_MODEL),
        'w_in': nrm(ks[3], (DEPTH, D_MODEL, IN_WIDTH), D_MODEL ** -0.5),
        'conv_w': nrm(ks[4], (DEPTH, 3, 3 * HYENA_WIDTH), 3 ** -0.5),
        'conv_b': nrm(ks[5], (DEPTH, 3 * HYENA_WIDTH), 0.02),
        'filt_w1': nrm(ks[6], (DEPTH, FILTER_EMB_DIM, FILTER_ORDER), FILTER_EMB_DIM ** -0.5),
        'filt_b1': nrm(ks[7], (DEPTH, FILTER_ORDER), 0.1),
        'filt_w2': nrm(ks[8], (DEPTH, FILTER_ORDER, FILTER_ORDER), FILTER_ORDER ** -0.5),
        'filt_b2': nrm(ks[9], (DEPTH, FILTER_ORDER), 0.1),
        'filt_w3': nrm(ks[10], (DEPTH, FILTER_ORDER, FILTER_ORDER), FILTER_ORDER ** -0.5),
        'filt_b3': nrm(ks[11], (DEPTH, FILTER_ORDER), 0.1),
        'filt_w4': nrm(ks[12], (DEPTH, FILTER_ORDER, 2 * HYENA_WIDTH), 0.02),
        'filt_freq': 1.0 + nrm(ks[13], (DEPTH, FILTER_ORDER), 0.01),
        'filt_bias': nrm(ks[14], (DEPTH, HYENA_WIDTH), 1.0),
        'lam_q1': nrm(ks[15], (DEPTH, ATTN_HEAD_DIM), 0.1),
        'lam_k1': nrm(ks[16], (DEPTH, ATTN_HEAD_DIM), 0.1),
        'lam_q2': nrm(ks[17], (DEPTH, ATTN_HEAD_DIM), 0.1),
        'lam_k2': nrm(ks[18], (DEPTH, ATTN_HEAD_DIM), 0.1),
        'subln_gain': gain(ks[19], 2 * ATTN_HEAD_DIM),
        'w_out': nrm(ks[20], (DEPTH, HYENA_WIDTH + ATTN_WIDTH, D_MODEL), (HYENA_WIDTH + ATTN_WIDTH) ** -0.5),
        'mlp_pre_gain': gain(ks[21], D_MODEL),
        'mlp_post_gain': gain(ks[22], D_MODEL),
        'w_up': nrm(ks[23], (DEPTH, D_MODEL, D_FF), D_MODEL ** -0.5),
        'w_down': nrm(ks[24], (DEPTH, D_FF, D_MODEL), D_FF ** -0.5),
    }


def reference(x, attn_pre_gain, attn_post_gain, w_in, conv_w, conv_b, filt_w1, filt_b1, filt_w2, filt_b2,
              filt_w3, filt_b3, filt_w4, filt_freq, filt_bias, lam_q1, lam_k1, lam_q2, lam_k2, subln_gain,
              w_out, mlp_pre_gain, mlp_post_gain, w_up, w_down):
    B, S, _ = x.shape
    z = hyena_positional_features(S)
    decay = hyena_decay_window(S)
    cos, sin = rope_tables(S)
    scale = ATTN_HEAD_DIM ** -0.5
    for l in range(DEPTH):
        h = rms_norm(x, attn_pre_gain[l])
        proj = h @ w_in[l]
        u_hy = proj[..., :3 * HYENA_WIDTH]
        q, k, v = jnp.split(proj[..., 3 * HYENA_WIDTH:], 3, axis=-1)

        h_fwd, h_bwd = implicit_filters(z, decay, filt_w1[l], filt_b1[l], filt_w2[l], filt_b2[l],
                                        filt_w3[l], filt_b3[l], filt_w4[l], filt_freq[l])
        y_hy = hyena_mixer(u_hy, conv_w[l], conv_b[l], h_fwd, h_bwd, filt_bias[l])

        q = apply_rope(q.reshape(B, S, ATTN_HEADS, 2, ATTN_HEAD_DIM), cos, sin) * scale
        k = apply_rope(k.reshape(B, S, ATTN_HEADS, 2, ATTN_HEAD_DIM), cos, sin)
        v = v.reshape(B, S, ATTN_HEADS, 2 * ATTN_HEAD_DIM)
        lam_init = 0.8 - 0.6 * math.exp(-0.3 * l)
        lam = (jnp.exp(jnp.sum(lam_q1[l].astype(F32) * lam_k1[l].astype(F32)))
               - jnp.exp(jnp.sum(lam_q2[l].astype(F32) * lam_k2[l].astype(F32))) + lam_init)
        o = differential_attention(q, k, v, lam)
        o = rms_norm(o, subln_gain[l], SUBLN_EPS) * (1.0 - lam_init)
        y_attn = o.reshape(B, S, ATTN_WIDTH).astype(x.dtype)

        mix = jnp.concatenate([y_hy, y_attn], axis=-1) @ w_out[l]
        x = x + rms_norm(mix, attn_post_gain[l])

        h = rms_norm(x, mlp_pre_gain[l])
        y = jnp.square(jax.nn.relu(h @ w_up[l])) @ w_down[l]
        x = x + rms_norm(y, mlp_post_gain[l])
    return x
```

```python
import numpy as np
import concourse.bass as bass
import concourse.mybir as mybir
from concourse.bass_utils import run_bass_kernel_spmd
from contextlib import ExitStack

F32 = mybir.dt.float32
BF16 = mybir.dt.bfloat16
ALU = mybir.AluOpType
ACT = mybir.ActivationFunctionType
AX = mybir.AxisListType

ENGS = ('tensor', 'vector', 'scalar', 'gpsimd', 'sync')
SEM_ROT = 16000


class Buf:
    __slots__ = ('name', 'w', 'r', 'dsem', 'excl')

    def __init__(self, name, excl=False):
        self.name = name
        self.excl = excl
        self.w = None
        self.r = {}
        self.dsem = None


class Prog:
    def __init__(self, nc, sems):
        self.nc = nc
        self.sems = sems
        self.nsem = 0
        self.q = {e: [] for e in ENGS}
        self.cnt = {e: 0 for e in ENGS}
        self.esem = {}
        self.owned = {e: set() for e in ENGS}
        self.seen = {e: {} for e in ENGS}
        self.dcnt = {}
        self.final = {}
        for e in ENGS:
            self._new_esem(e)

    def alloc_sem(self):
        i = self.nsem
        self.nsem += 1
        assert i < len(self.sems), "out of semaphores"
        return i

    def _new_esem(self, e):
        i = self.alloc_sem()
        self.esem[e] = i
        self.owned[e].add(i)
        self.cnt[e] = 0

    def _deps(self, eng, reads, writes):
        need = {}

        def add(t, raw=False):
            if t is None:
                return
            s, v = t
            if s in self.owned[eng] and not (raw and eng != 'tensor'):
                return
            if need.get(s, 0) < v:
                need[s] = v
        for b in reads:
            add(b.w, raw=True)
        for b in writes:
            add(b.w)
            for s, v in b.r.items():
                add((s, v))
        out = []
        seen = self.seen[eng]
        for s, v in need.items():
            if seen.get(s, 0) < v:
                seen[s] = v
                out.append((s, v))
        return out

    def op(self, eng, fn, reads=(), writes=()):
        if any(b.excl for b in reads):
            writes = list(writes) + [b for b in reads if b.excl]
            reads = [b for b in reads if not b.excl]
        waits = self._deps(eng, reads, writes)
        if self.cnt[eng] >= SEM_ROT:
            self._new_esem(eng)
        self.cnt[eng] += 1
        s, v = self.esem[eng], self.cnt[eng]
        for b in reads:
            if b.r.get(s, 0) < v:
                b.r[s] = v
        for b in writes:
            b.w = (s, v)
            b.r = {}
        self.q[eng].append((waits, fn, s, 1))

    def dma(self, qeng, fn, reads=(), writes=(), sembuf=None):
        waits = self._deps(qeng, reads, writes)
        sb = sembuf if sembuf is not None else (writes[0] if writes else reads[0])
        if sb.dsem is None or self.dcnt[sb.dsem] >= SEM_ROT:
            sb.dsem = self.alloc_sem()
            self.dcnt[sb.dsem] = 0
        s = sb.dsem
        self.dcnt[s] += 16
        v = self.dcnt[s]
        for b in reads:
            if b.r.get(s, 0) < v:
                b.r[s] = v
        for b in writes:
            b.w = (s, v)
            b.r = {}
        self.q[qeng].append((waits, fn, s, 16))
        return (s, v)

    def wait_all(self, eng, ticks):
        need = {}
        for s, v in ticks:
            if need.get(s, 0) < v:
                need[s] = v
        self.q[eng].append((list(need.items()), None, None, 0))

    def barrier(self):
        ticks = [(self.esem[o], self.cnt[o]) for o in ENGS if self.cnt[o] > 0]
        ticks += [(s, v) for s, v in self.dcnt.items()]
        for e in ENGS:
            need = []
            for s, v in ticks:
                if s in self.owned[e]:
                    continue
                if self.seen[e].get(s, 0) < v:
                    self.seen[e][s] = v
                    need.append((s, v))
            if need:
                self.q[e].append((need, None, None, 0))

    def replay(self, engobj, eng):
        sems = self.sems
        for waits, fn, s, inc in self.q[eng]:
            for ws, wv in waits:
                engobj.wait_ge(sems[ws], wv)
            if fn is not None:
                inst = fn(engobj)
                inst.then_inc(sems[s], inc)

    def emit(self, block):
        @block.tensor
        def _(e):
            self.replay(e, 'tensor')

        @block.vector
        def _(e):
            self.replay(e, 'vector')

        @block.scalar
        def _(e):
            self.replay(e, 'scalar')

        @block.gpsimd
        def _(e):
            self.replay(e, 'gpsimd')

        @block.sync
        def _(e):
            self.replay(e, 'sync')


def AP(base, off, dims):
    return bass.AP(base.tensor, off, [list(d) for d in dims])


S = 4096
D = 1024
NFFT = 8192
PI = float(np.pi)


def host_tables():
    import ml_dtypes
    bf = ml_dtypes.bfloat16
    T = {}
    T['ident_bf'] = np.eye(128, dtype=np.float32).astype(bf)
    T['ident_f'] = np.eye(128, dtype=np.float32)
    p = np.arange(128, dtype=np.float64)
    k1 = np.arange(128, dtype=np.float64)
    j = np.arange(32, dtype=np.float64)
    k2 = np.arange(32, dtype=np.float64)
    ang = 2 * np.pi * (k1[None, :] + 0.5) * p[:, None] / 256.0
    T['F1'] = np.concatenate([np.cos(ang), -np.sin(ang)], axis=1).astype(np.float32).astype(bf)
    jj = np.repeat(j, 4)
    a1 = 2 * np.pi * (k1[None, :] + 0.5) * jj[:, None] / NFFT
    T['Tw1'] = np.stack([np.cos(a1), -np.sin(a1)], axis=1).astype(np.float32)
    th = 2 * np.pi * np.outer(j, k2) / 32.0
    I4 = np.eye(4)
    C = np.kron(np.cos(th), I4)
    Sm = np.kron(np.sin(th), I4)
    T['CS4'] = np.stack([C, -C, Sm, -Sm], axis=1).astype(np.float32).astype(bf)
    W1 = np.concatenate([C, Sm], axis=1)
    W2 = np.concatenate([-C, -Sm], axis=1)
    W34 = np.concatenate([-Sm, C], axis=1)
    T['W3'] = np.stack([W1, W2, W34], axis=1).astype(np.float32).astype(bf)
    a2 = 2 * np.pi * (k1[:, None] + 0.5) * j[None, :] / NFFT
    T['Tw2'] = np.stack([np.cos(a2), -np.sin(a2)], axis=1).astype(np.float32)
    ph = 2 * np.pi * (k1[:, None] + 0.5) * p[None, :] / 256.0
    cphi = (2.0 / NFFT) * np.cos(ph)
    sphi = (2.0 / NFFT) * np.sin(ph)
    T['PHI'] = np.stack([cphi, -sphi, sphi], axis=1).astype(np.float32).astype(bf)
    bands = 16
    t = np.linspace(0.0, 1.0, S, dtype=np.float32)[:, None]
    w = (np.float32(2.0 * np.pi / S) * np.arange(S, dtype=np.float32))[:, None]
    f = np.linspace(1e-4, bands - 1, bands, dtype=np.float32)[None, :]
    fw_ = (f * w).astype(np.float32)
    z = np.concatenate([t, np.cos(fw_), -np.sin(fw_)], axis=-1).astype(np.float32)
    T['zT'] = np.ascontiguousarray(z.T)
    tt = np.linspace(0.0, 1.0, S, dtype=np.float32).reshape(128, 32)
    T['negt'] = (-tt).astype(np.float32)
    import math
    max_decay = math.log(1e-2) / 0.3
    min_decay = math.log(1e-2) / 1.5
    deltas = np.abs(np.linspace(min_decay, max_decay, 512, dtype=np.float32))
    T['absdelta'] = np.ascontiguousarray(np.broadcast_to(deltas[None, :], (128, 512))).astype(np.float32)
    inv = (10000.0 ** (-np.arange(0, 64, 2, dtype=np.float32) / 64)).astype(np.float32)
    angr = (np.arange(S, dtype=np.float32)[:, None] * inv[None, :]).astype(np.float32)
    angr = np.concatenate([angr, angr], axis=-1)
    cosr = np.cos(angr).astype(np.float32)
    sinr = np.sin(angr).astype(np.float32)
    sgn = np.concatenate([-np.ones(32, np.float32), np.ones(32, np.float32)])
    sins = sinr * sgn[None, :]
    T['ropec'] = np.ascontiguousarray(np.concatenate([cosr.T, cosr.T], axis=0))
    T['ropes'] = np.ascontiguousarray(np.concatenate([sins.T, sins.T], axis=0))
    return T


KB = 1024
CONST0 = 0
HT0 = 14 * KB
MIX0 = 78 * KB
WORK0 = 142 * KB
HW0 = 118 * KB
ARENA_F32 = 53100


class Ctx:
    pass


def V(region, off, dims, p0=0, np_=128):
    pstep = region.ap[0][0]
    return bass.AP(region.tensor, region.offset + p0 * pstep + off,
                   [[pstep, np_]] + [list(d) for d in dims])


class K:
  def __init__(self, dumps=(), small_out=False):
    nc = bass.Bass("TRN2", target_bir_lowering=False)

    def din(name, shape, dt=F32):
        return nc.dram_tensor(name, list(shape), dt, kind="ExternalInput").ap()

    x = din('x', [S, D])
    w_in = din('w_in', [D, 3072])
    w_qkp = din('w_qkp', [D, 1024])
    cw = din('cw', [128, 48])
    fw1 = din('fw1', [33, 64])
    fw2 = din('fw2', [64, 64])
    fw3 = din('fw3', [64, 64])
    fw4 = din('fw4', [64, 1024])
    fcol = din('fcol', [64, 4])
    fbias = din('fbias', [128, 4])
    lamv = din('lamv', [1, 256])
    subg = din('subg', [128, 1])
    w_out = din('w_out', [D, D])
    w_up = din('w_up', [D, 4096])
    w_down = din('w_down', [4096, D])
    gcols = din('gcols', [128, 16])
    gpost = din('gpost', [2, 1024])
    t_ident_bf = din('ident_bf', [128, 128], BF16)
    t_ident_f = din('ident_f', [128, 128])
    t_F1 = din('F1', [128, 256], BF16)
    t_Tw1 = din('Tw1', [128, 256])
    t_CS4 = din('CS4', [128, 512], BF16)
    t_W3 = din('W3', [128, 768], BF16)
    t_Tw2 = din('Tw2', [128, 64])
    t_PHI = din('PHI', [128, 384], BF16)
    t_zT = din('zT', [33, S])
    t_negt = din('negt', [128, 32])
    t_absd = din('absdelta', [128, 512])
    t_ropec = din('ropec', [128, S])
    t_ropes = din('ropes', [128, S])
    out = nc.dram_tensor('out', [128 if small_out else S, D], F32, kind='ExternalOutput').ap()
    xa_scr = nc.dram_tensor('xa_scr', [S, D], F32, kind='Internal').ap()
    dump_aps = {}
    for (nm, shape, dt) in dumps:
        dump_aps[nm] = nc.dram_tensor('dbg_' + nm, list(shape), dt, kind='ExternalOutput').ap()

    es = ExitStack()
    sems = [es.enter_context(nc.semaphore(f"s{i}")) for i in range(88)]
    arena_t = es.enter_context(nc.sbuf_tensor("arena", [128, ARENA_F32], F32))
    pst = [es.enter_context(nc.psum_tensor(f"ps{i}", [128, 512], F32)) for i in range(8)]
    ps = [t[:] for t in pst]
    psb = [t[:].bitcast(BF16) for t in pst]
    AF = arena_t[:]
    AB = arena_t[:].bitcast(BF16)

    def f32r(boff, n):
        assert boff % 4 == 0 and boff + 4 * n <= ARENA_F32 * 4, (boff, n)
        return AF[:, boff // 4: boff // 4 + n]

    def bfr(boff, n):
        assert boff % 4 == 0 and boff + 2 * n <= ARENA_F32 * 4, (boff, n)
        return AB[:, boff // 2: boff // 2 + n]

    P = Prog(nc, sems)
    bufs = {}

    def B(name):
        b = bufs.get(name)
        if b is None:
            b = Buf(name, excl=name.startswith('ps'))
            bufs[name] = b
        return b

    def mm(out, lhsT, rhs, start=True, stop=True, reads=(), writes=(), **kw):
        P.op('tensor', lambda e: e.matmul(out, lhsT=lhsT, rhs=rhs, start=start, stop=stop, **kw), reads, writes)

    def tr(out, in_, ident, reads=(), writes=()):
        P.op('tensor', lambda e: e.transpose(out=out, in_=in_, identity=ident), reads, writes)

    def act(out, in_, func, reads=(), writes=(), **kw):
        P.op('scalar', lambda e: e.activation(out=out, in_=in_, func=func, **kw), reads, writes)

    def tt(eng, out, in0, in1, op, reads=(), writes=()):
        P.op(eng, lambda e: e.tensor_tensor(out=out, in0=in0, in1=in1, op=op), reads, writes)

    def ts(eng, out, in0, s1, s2, op0, op1=None, reads=(), writes=(), **kw):
        if op1 is None:
            P.op(eng, lambda e: e.tensor_scalar(out=out, in0=in0, scalar1=s1, scalar2=None, op0=op0, **kw), reads, writes)
        else:
            P.op(eng, lambda e: e.tensor_scalar(out=out, in0=in0, scalar1=s1, scalar2=s2, op0=op0, op1=op1, **kw), reads, writes)

    def stt(eng, out, in0, scalar, in1, op0, op1, reads=(), writes=(), **kw):
        P.op(eng, lambda e: e.scalar_tensor_tensor(out=out, in0=in0, scalar=scalar, in1=in1, op0=op0, op1=op1, **kw), reads, writes)

    def cp(eng, out, in_, reads=(), writes=()):
        P.op(eng, lambda e: e.tensor_copy(out=out, in_=in_), reads, writes)

    def mset(eng, ap, val, writes=()):
        P.op(eng, lambda e: e.memset(ap, val), (), writes)

    def dma(q, out, in_, reads=(), writes=(), sembuf=None):
        return P.dma(q, lambda e: e.dma_start(out=out, in_=in_), reads, writes, sembuf)

    def barrier():
        P.barrier()

    cpos = [CONST0]

    def calloc(nbytes):
        o = cpos[0]
        cpos[0] += (nbytes + 63) // 64 * 64
        assert cpos[0] <= HT0, cpos[0]
        return o

    c_ident_bf = bfr(calloc(256), 128)
    c_ident_f = f32r(calloc(512), 128)
    c_F1 = bfr(calloc(512), 256)
    c_Tw1 = f32r(calloc(1024), 256)
    c_CS4 = bfr(calloc(1024), 512)
    c_W3 = bfr(calloc(1536), 768)
    c_Tw2 = f32r(calloc(256), 64)
    c_PHI = bfr(calloc(768), 384)
    c_gcols = f32r(calloc(64), 16)
    c_cw = f32r(calloc(192), 48)
    c_fbias = f32r(calloc(16), 4)
    c_subg = f32r(calloc(4), 1)
    c_lamv = f32r(calloc(1024), 256)
    c_negt = f32r(calloc(128), 32)
    c_absd = f32r(calloc(2048), 512)
    c_fw1 = f32r(calloc(256), 64)
    c_fw2 = f32r(calloc(256), 64)
    c_fw3 = f32r(calloc(256), 64)
    c_fcol = f32r(calloc(16), 4)
    c_fw4 = bfr(calloc(2048), 1024)
    c_cols = f32r(calloc(1024), 256)
    c_lam = f32r(calloc(64), 16)
    KC = B('consts')
    for dst, src in [(c_ident_f, t_ident_f), (c_Tw1, t_Tw1), (c_Tw2, t_Tw2), (c_gcols, gcols), (c_cw, cw),
                     (c_fbias, fbias), (c_subg, subg), (c_negt, t_negt), (c_absd, t_absd), (c_fcol[0:64, :], fcol),
                     (c_fw1[0:33, :], fw1), (c_fw2[0:64, :], fw2), (c_fw3[0:64, :], fw3),
                     (c_ident_bf, t_ident_bf), (c_F1, t_F1), (c_CS4, t_CS4), (c_W3, t_W3), (c_PHI, t_PHI)]:
        dma('sync', dst, src, writes=[KC])
    dma('sync', c_lamv, bass.AP(lamv.tensor, 0, [[0, 128], [1, 256]]), writes=[KC])
    dma('gpsimd', c_fw4[0:64, :], fw4, writes=[KC])

    d = dict(locals())
    d.pop('self')
    self.__dict__.update(d)


def rstd_from_ss(k, ss, ms, ln, rstd, n, eps, bufs):
    k.ts('vector', ms, ss, 1.0 / n, eps, ALU.mult, ALU.add, reads=bufs, writes=bufs)
    k.act(ln, ms, ACT.Ln, reads=bufs, writes=bufs)
    k.act(rstd, ln, ACT.Exp, reads=bufs, writes=bufs, scale=-0.5)


def phase0(k):
    B = k.B
    hT = k.bfr(HT0, 8 * S).rearrange("p (c t) -> p c t", c=8)
    k.hT = hT
    XT = [k.f32r(WORK0 + i * 4096, 1024) for i in range(3)]
    XN = [k.bfr(WORK0 + 12288 + i * 2048, 1024) for i in range(2)]
    JK = k.bfr(WORK0 + 16384, 1024)
    gb = V(k.c_gcols, 0, [[1, 8], [0, 128]])
    for i in range(getattr(k, 'dev_ntiles', 32)):
        s3, s2 = i % 3, i % 2
        xt, xn = XT[s3], XN[s2]
        bx, bn, bc = B(f'p0x{s3}'), B(f'p0n{s2}'), B(f'p0c{s2}')
        cols = k.c_cols[:, s2 * 8: s2 * 8 + 8]
        ss, ms, ln, rs = cols[:, 0:1], cols[:, 1:2], cols[:, 2:3], cols[:, 3:4]
        k.dma('sync', xt, k.x[i * 128:(i + 1) * 128, :], writes=[bx])
        k.stt('vector', JK, xt, 1.0, xt, ALU.mult, ALU.mult, reads=[bx], writes=[B('p0jk'), bc], accum_out=ss)
        rstd_from_ss(k, ss, ms, ln, rs, 1024.0, 1e-6, [bc])
        k.act(xn, xt, ACT.Identity, reads=[bx, bc], writes=[bn], scale=rs)
        bank = s2
        bp = B(f'ps{bank}')
        for c in range(8):
            k.tr(k.psb[bank][:, c * 128:(c + 1) * 128], xn[:, c * 128:(c + 1) * 128], k.c_ident_bf,
                 reads=[bn, k.KC], writes=[bp])
        pin = k.psb[bank][:, 0:1024].rearrange("p (c t) -> p c t", c=8)
        k.tt('vector', hT[:, :, i * 128:(i + 1) * 128], pin, gb, ALU.mult, reads=[bp, k.KC], writes=[B(f'hT{i}')])


H3T0 = MIX0 + 32 * KB


def phaseF(k):
    B = k.B
    ZT = k.f32r(HW0, S)
    ARG = k.f32r(HW0 + 16 * KB, S)
    WT = k.f32r(HW0 + 32 * KB, S)
    HA = k.f32r(HW0 + 48 * KB, S)
    HB = k.f32r(HW0 + 64 * KB, S)
    h3T = k.bfr(H3T0, S)
    k.h3T = h3T
    bz, ba, bw = B('fz'), B('farg'), B('fwt')
    k.dma('sync', ZT[0:33, :], k.t_zT, writes=[bz])
    freq = k.c_fcol[0:64, 3:4]
    layers = [(k.c_fw1[0:33, :], ZT[0:33, :], HA, bz, B('fha')),
              (k.c_fw2[0:64, :], HA[0:64, :], HB, B('fha'), B('fhb')),
              (k.c_fw3[0:64, :], HB[0:64, :], h3T, B('fhb'), B('h3T'))]
    for L, (w, inp, outp, bin_, bout) in enumerate(layers):
        fb = k.c_cols[0:64, 32 + L: 33 + L]
        bfb = B(f'ffb{L}')
        k.tt('vector', fb, k.c_fcol[0:64, L:L + 1], freq, ALU.mult, reads=[k.KC], writes=[bfb])
        for ch in range(8):
            bank = ch % 2
            bp = B(f'ps{bank}')
            k.mm(k.ps[bank][0:64, :], w, inp[:, ch * 512:(ch + 1) * 512], reads=[k.KC, bin_], writes=[bp])
            k.ts('vector', ARG[0:64, ch * 512:(ch + 1) * 512], k.ps[bank][0:64, :], freq, fb, ALU.mult, ALU.add,
                 reads=[bp, k.KC, bfb], writes=[ba])
        a = ARG[0:64, :]
        wt = WT[0:64, :]
        k.ts('vector', wt, a, PI, -2 * PI, ALU.is_gt, ALU.mult, reads=[ba], writes=[bw])
        k.tt('vector', a, a, wt, ALU.add, reads=[ba, bw], writes=[ba])
        k.ts('vector', wt, a, -PI, 2 * PI, ALU.is_lt, ALU.mult, reads=[ba], writes=[bw])
        k.tt('vector', a, a, wt, ALU.add, reads=[ba, bw], writes=[ba])
        k.act(outp[0:64, :], a, ACT.Sin, reads=[ba], writes=[bout])


def load_w(k, dst3, src_cols, bw):
    k.dma('gpsimd', dst3, src_cols.rearrange("(c p) n -> p c n", p=128), writes=[bw])


def proj_fm(k, wt3, col0, ncols, dst_fn, bw, tag, banks=(6, 7)):
    B = k.B
    for ch in range(8):
        bank = banks[ch % len(banks)]
        bp = B(f'ps{bank}')
        hb = [B(f'hT{i}') for i in range(ch * 4, ch * 4 + 4)]
        for c in range(8):
            k.mm(k.ps[bank][:, :], wt3[:, c, col0:col0 + ncols], k.hT[:, c, ch * 512:(ch + 1) * 512],
                 start=(c == 0), stop=(c == 7), reads=[bw] + hb, writes=[bp])
        dst_fn(ch, k.ps[bank][:, :], bp)


def short_conv(k, raw, t1, dst, ti, braw, bt1, bdst, eng='vector'):
    w0, w1, w2, bb = [k.c_cw[:, ti * 4 + i: ti * 4 + i + 1] for i in range(4)]
    k.ts(eng, t1, raw[:, 1:S + 1], w1, bb, ALU.mult, ALU.add, reads=[braw, k.KC], writes=[bt1])
    k.stt(eng, t1, raw[:, 0:S], w0, t1, ALU.mult, ALU.add, reads=[braw, bt1, k.KC], writes=[bt1])
    k.stt(eng, dst, raw[:, 2:S + 2], w2, t1, ALU.mult, ALU.add, reads=[braw, bt1, k.KC], writes=[bdst])


def fft_fwd_batch(k, d0_groups, pslots, a_slots, tag):
    P_reg, bP = pslots
    for gi, (lh, bd) in enumerate(d0_groups):
        pa, ba = a_slots[gi]
        k.mm(pa, lh, k.c_F1, reads=[bd, k.KC], writes=[ba])
    for gi, (lh, bd) in enumerate(d0_groups):
        pa, ba = a_slots[gi]
        o = V(P_reg, gi * 128, [[2 * 512, 2], [512, 2], [1, 128]])
        i0 = bass.AP(pa.tensor, pa.offset, [list(pa.ap[0]), [0, 2], [128, 2], [1, 128]])
        i1 = V(k.c_Tw1, 0, [[128, 2], [0, 2], [1, 128]])
        k.tt('vector', o, i0, i1, ALU.mult, reads=[ba, k.KC], writes=[bP])


def phaseH(k):
    B = k.B
    mixT = k.bfr(MIX0, 8 * S).rearrange("p (c t) -> p c t", c=8)
    k.mixT = mixT
    U = k.bfr(HW0, S)
    G = k.bfr(HW0 + 8 * KB, 2 * 32 * 128)
    R = k.bfr(HW0 + 24 * KB, 4 * 32 * 128)
    D0F = k.bfr(HW0 + 24 * KB, 2 * 32 * 128)
    PF = [k.bfr(HW0 + 40 * KB + i * 4 * KB, 2048) for i in range(4)]
    X0 = HW0 + 56 * KB
    RAW = k.f32r(X0, S + 4)
    T1 = k.f32r(X0 + 16400 + 16, S)
    D0 = k.bfr(X0, 32 * 128)
    PU = [k.bfr(X0 + 8 * KB + i * 4 * KB, 2048) for i in range(2)]
    QS = [k.bfr(X0 + 16 * KB + i * 4 * KB, 2048) for i in range(2)]
    DEC = [k.f32r(X0 + i * 512, 128) for i in range(2)]
    WSL = [k.bfr(HW0 + 8 * KB + i * 2 * KB, 1024).rearrange("p (c n) -> p c n", c=8) for i in range(3)]
    X0C = k.bfr(HW0 + 8 * KB + 2 * KB, S)
    TMP = [k.f32r(HW0 + 8 * KB + 10 * KB + i * 2 * KB, 512) for i in range(2)]
    hT = k.hT
    ASL = [(k.ps[b][:, h * 256:(h + 1) * 256], B(f'ps{b}')) for b in range(2) for h in range(2)]
    CSm = lambda i: k.c_CS4[:, i * 128:(i + 1) * 128]
    W3m = lambda i: k.c_W3[:, i * 256:(i + 1) * 256]
    PHIm = lambda i: k.c_PHI[:, i * 128:(i + 1) * 128]
    XRE = [0, 2, 2, 1]
    XIM = [3, 0, 0, 2]
    XIMN = [2, 1, 1, 3]
    QW = [0, 2, 2, 1]
    RW = [0, 1, 2, 0]
    braw, bt1, bu, bx0c = B('hraw'), B('ht1'), B('hu'), B('hx0c')

    for ct in getattr(k, 'dev_cts', range(4)):
        k.mset('vector', RAW[:, 0:1], 0.0, writes=[braw])
        k.mset('vector', RAW[:, S + 1:S + 2], 0.0, writes=[braw])
        for wi, col in enumerate((512 + ct * 128, 1024 + ct * 128)):
            bw = B(f'hw{wi}')
            load_w(k, WSL[wi], k.w_in[:, col:col + 128], bw)

            def evac(ch, pap, bp):
                k.act(RAW[:, 1 + ch * 512: 1 + (ch + 1) * 512], pap, ACT.Copy, reads=[bp], writes=[braw])
            proj_fm(k, WSL[wi], 0, 128, evac, bw, f'h{wi}')
            ti = col // 128
            if wi == 0:
                short_conv(k, RAW, T1, U, ti, braw, bt1, bu)
            else:
                short_conv(k, RAW, T1, T1, ti, braw, bt1, bt1)
                k.tt('gpsimd', U, T1, U, ALU.mult, reads=[bt1, bu], writes=[bu])
        k.barrier()
        bd0f = B('hd0f')
        bG = B('hG')
        for j in range(32):
            bank = 6 + j % 2
            bp = B(f'ps{bank}')
            s2 = j % 2
            bdec = B(f'hdec{s2}')
            lh = V(k.h3T, j, [[32, 128]], p0=0, np_=64)
            rh = V(k.c_fw4, ct * 128, [[512, 2], [1, 128]], p0=0, np_=64)
            k.mm(k.ps[bank][:, 0:256], lh, rh, reads=[B('h3T'), k.KC], writes=[bp])
            k.act(DEC[s2], k.c_absd[:, ct * 128:(ct + 1) * 128], ACT.Exp, reads=[k.KC], writes=[bdec],
                  scale=k.c_negt[:, j:j + 1])
            o = V(D0F, j * 4, [[32 * 128, 2], [128, 32], [1, 4]])
            i0 = bass.AP(k.ps[bank].tensor, k.ps[bank].offset, [list(k.ps[bank].ap[0]), [128, 2], [4, 32], [1, 4]])
            i1 = V(DEC[s2], 0, [[0, 2], [4, 32], [1, 4]])
            k.tt('vector', o, i0, i1, ALU.mult, reads=[bp, bdec], writes=[bd0f])
        for bt in range(8):
            pf, pb_ = (PF[(bt % 2) * 2], B(f'hpf{bt % 2}')), (PF[(bt % 2) * 2 + 1], B(f'hpb{bt % 2}'))
            for di, pslot in enumerate((pf, pb_)):
                groups = [(V(D0F, di * 4096 + (bt * 4 + gi) * 128, [[1, 128]]), bd0f) for gi in range(4)]
                fft_fwd_batch(k, groups, pslot, ASL, 'f')
            bre, bim = 2 + (bt % 2) * 2, 3 + (bt % 2) * 2
            bpr, bpi = B(f'ps{bre}'), B(f'ps{bim}')
            for (bank, bp, cf, cb) in ((bre, bpr, XRE, XRE), (bim, bpi, XIM, XIMN)):
                n = 0
                for (preg, bP), coef in ((pf, cf), (pb_, cb)):
                    for comp in range(4):
                        k.mm(k.ps[bank][:, :], CSm(coef[comp]), preg[:, comp * 512:(comp + 1) * 512],
                             start=(n == 0), stop=(n == 7), reads=[bP, k.KC], writes=[bp])
                        n += 1
            for ri, (bank, bp) in enumerate(((bre, bpr), (bim, bpi))):
                k.act(G[:, ri * 4096 + bt * 512: ri * 4096 + (bt + 1) * 512], k.ps[bank][:, :], ACT.Copy,
                      reads=[bp], writes=[bG])
        k.barrier()
        bd0 = B('hd0')
        bR = B('hR')
        for jb in range(4):
            bank = 6 + jb % 2
            bp = B(f'ps{bank}')
            for jj in range(8):
                j = jb * 8 + jj
                k.tr(k.psb[bank][:, jj * 128:(jj + 1) * 128], V(U, j, [[32, 128]]), k.c_ident_bf,
                     reads=[bu, k.KC], writes=[bp])
            o = V(D0, jb * 8 * 4, [[4, 8], [128, 32], [1, 4]])
            i0 = bass.AP(k.psb[bank].tensor, k.psb[bank].offset, [list(k.psb[bank].ap[0]), [128, 8], [4, 32], [1, 4]])
            k.cp('vector', o, i0, reads=[bp], writes=[bd0])
        for bt in range(8):
            pu = (PU[bt % 2], B(f'hpu{bt % 2}'))
            groups = [(V(D0, (bt * 4 + gi) * 128, [[1, 128]]), bd0) for gi in range(4)]
            fft_fwd_batch(k, groups, pu, ASL, 'u')
            bre, bim = 2 + (bt % 2) * 2, 3 + (bt % 2) * 2
            bpr, bpi = B(f'ps{bre}'), B(f'ps{bim}')
            for (bank, bp, cf) in ((bre, bpr, XRE), (bim, bpi, XIM)):
                for comp in range(4):
                    k.mm(k.ps[bank][:, :], CSm(cf[comp]), pu[0][:, comp * 512:(comp + 1) * 512],
                         start=(comp == 0), stop=(comp == 3), reads=[pu[1], k.KC], writes=[bp])
            qreg, bq = QS[bt % 2], B(f'hq{bt % 2}')
            gsl = V(G, bt * 512, [[4096, 2], [1, 512]])
            for half, (bank, bp) in enumerate(((bre, bpr), (bim, bpi))):
                if half == 0:
                    o = V(qreg, 0, [[512, 2], [1, 512]])
                else:
                    o = V(qreg, 1024, [[512, 2], [1, 512]])
                i0 = bass.AP(k.ps[bank].tensor, k.ps[bank].offset, [list(k.ps[bank].ap[0]), [0, 2], [1, 512]])
                k.tt('vector', o, i0, gsl, ALU.mult, reads=[bp, bG], writes=[bq])
            for gi in range(4):
                pa, ba = ASL[gi]
                for comp in range(4):
                    k.mm(pa, qreg[:, comp * 512 + gi * 128: comp * 512 + (gi + 1) * 128], W3m(QW[comp]),
                         start=(comp == 0), stop=(comp == 3), reads=[bq, k.KC], writes=[ba], skip_group_check=True)
            for gi in range(4):
                pa, ba = ASL[gi]
                g = bt * 4 + gi
                for a in range(2):
                    o = V(R, (a * 2) * 4096 + g * 4, [[4096, 2], [128, 32], [1, 4]])
                    i0 = bass.AP(pa.tensor, pa.offset, [list(pa.ap[0]), [128, 2], [4, 32], [1, 4]])
                    i1 = V(k.c_Tw2, a * 32, [[0, 2], [1, 32], [0, 4]])
                    k.tt('vector', o, i0, i1, ALU.mult, reads=[ba, k.KC], writes=[bR])
        k.barrier()
        bw = B('hw2')
        col = ct * 128
        k.mset('vector', RAW[:, 0:1], 0.0, writes=[braw])
        k.mset('vector', RAW[:, S + 1:S + 2], 0.0, writes=[braw])
        load_w(k, WSL[0], k.w_in[:, col:col + 128], bw)

        def evac2(ch, pap, bp):
            k.act(RAW[:, 1 + ch * 512: 1 + (ch + 1) * 512], pap, ACT.Copy, reads=[bp], writes=[braw])
        proj_fm(k, WSL[0], 0, 128, evac2, bw, 'hx0')
        short_conv(k, RAW, T1, X0C, ct, braw, bt1, bx0c)
        bmix = B(f'mix{ct}')
        for jb in range(8):
            bank = 4 + jb % 2
            bp = B(f'ps{bank}')
            for jj in range(4):
                j = jb * 4 + jj
                for comp in range(4):
                    k.mm(k.ps[bank][:, jj * 128:(jj + 1) * 128], R[:, comp * 4096 + j * 128: comp * 4096 + (j + 1) * 128],
                         PHIm(RW[comp]), start=(comp == 0), stop=(comp == 3), reads=[bR, k.KC], writes=[bp],
                         skip_group_check=True)
            tmp, btmp = TMP[jb % 2], B(f'htmp{jb % 2}')
            uv = V(U, jb * 4, [[32, 128], [1, 4]])
            pv = bass.AP(k.ps[bank].tensor, k.ps[bank].offset, [list(k.ps[bank].ap[0]), [1, 128], [128, 4]])
            tv = V(tmp, 0, [[4, 128], [1, 4]])
            k.stt('vector', tv, uv, k.c_fbias[:, ct:ct + 1], pv, ALU.mult, ALU.add, reads=[bu, bp, k.KC], writes=[btmp])
            xv = V(X0C, jb * 4, [[32, 128], [1, 4]])
            mv = bass.AP(mixT.tensor, mixT.offset + ct * S + jb * 4, [list(mixT.ap[0]), [32, 128], [1, 4]])
            k.tt('gpsimd', mv, tv, xv, ALU.mult, reads=[btmp, bx0c], writes=[bmix])
        k.barrier()


def phaseA(k):
    B = k.B
    hT, mixT = k.hT, k.mixT
    W = WORK0
    QT = k.bfr(W, S)
    KT = k.bfr(W + 8 * KB, S)
    VA = k.bfr(W + 16 * KB, 32 * 130).rearrange("p (s e) -> p s e", s=32)
    RC = [k.f32r(W + 25 * KB + i * 2 * KB, 512) for i in range(2)]
    RS = [k.f32r(W + 29 * KB + i * 2 * KB, 512) for i in range(2)]
    WSL = [k.bfr(W + 33 * KB + i * 2 * KB, 1024).rearrange("p (c n) -> p c n", c=8) for i in range(5)]
    PT = [k.bfr(W + 43 * KB + i * KB, 512) for i in range(4)]
    TA = [k.f32r(W + 47 * KB + i * 2 * KB, 512) for i in range(4)]
    OSB = [k.f32r(W + 55 * KB + i * 512, 128) for i in range(2)]
    YN = [k.bfr(W + 56 * KB + i * 256, 128) for i in range(2)]
    JK = k.bfr(W + 57 * KB, 128)
    lam = k.c_lam
    bl = B('lam')
    lv = k.c_lamv
    k.stt('vector', JK[:, 0:64], lv[:, 0:64], 1.0, lv[:, 64:128], ALU.mult, ALU.mult, reads=[k.KC], writes=[bl, B('ajk')],
          accum_out=lam[:, 0:1])
    k.stt('vector', JK[:, 0:64], lv[:, 128:192], 1.0, lv[:, 192:256], ALU.mult, ALU.mult, reads=[k.KC], writes=[bl, B('ajk')],
          accum_out=lam[:, 1:2])
    k.act(lam[:, 2:4], lam[:, 0:2], ACT.Exp, reads=[bl], writes=[bl])
    k.tt('vector', lam[:, 4:5], lam[:, 2:3], lam[:, 3:4], ALU.subtract, reads=[bl], writes=[bl])
    k.ts('vector', lam[:, 5:6], lam[:, 4:5], 0.2, -1.0, ALU.add, ALU.mult, reads=[bl], writes=[bl])
    neglam = lam[:, 5:6]
    bQ, bK, bV = B('aQT'), B('aKT'), B('aVA')

    def oreg(r):
        return k.ps[4 + r // 3][:, (r % 3) * 129:(r % 3) * 129 + 129], B(f'ps{4 + r // 3}')

    for hd in getattr(k, 'dev_heads', range(4)):
        srcs = [k.w_in[:, 1536 + hd * 128: 1536 + (hd + 1) * 128], k.w_qkp[:, hd * 128:(hd + 1) * 128],
                k.w_in[:, 2048 + hd * 128: 2048 + (hd + 1) * 128], k.w_qkp[:, 512 + hd * 128: 512 + (hd + 1) * 128],
                k.w_in[:, 2560 + hd * 128: 2560 + (hd + 1) * 128]]
        bws = [B(f'aw{i}') for i in range(5)]
        for i in range(5):
            load_w(k, WSL[i], srcs[i], bws[i])
        k.mset('vector', VA[:, :, 128:129], 1.0, writes=[bV])
        for sb in range(8):
            bank = 6 + sb % 2
            bp = B(f'ps{bank}')
            for s4 in range(4):
                st = sb * 4 + s4
                for c in range(8):
                    k.mm(k.ps[bank][:, s4 * 128:(s4 + 1) * 128], hT[:, c, st * 128:(st + 1) * 128], WSL[4][:, c, :],
                         start=(c == 0), stop=(c == 7), reads=[bws[4], B(f'hT{st}')], writes=[bp], skip_group_check=True)
            pin = k.ps[bank][:, :].rearrange("p (s e) -> p s e", s=4)
            k.act(VA[:, sb * 4:(sb + 1) * 4, 0:128], pin, ACT.Copy, reads=[bp], writes=[bV])
        for ch in range(8):
            s2 = ch % 2
            brc, brs = B(f'arc{s2}'), B(f'ars{s2}')
            k.dma('sync', RC[s2], k.t_ropec[:, ch * 512:(ch + 1) * 512], writes=[brc])
            k.dma('sync', RS[s2], k.t_ropes[:, ch * 512:(ch + 1) * 512], writes=[brs])
            hb = [B(f'hT{i}') for i in range(ch * 4, ch * 4 + 4)]
            for wi in range(4):
                bank = 2 + wi
                for c in range(8):
                    k.mm(k.ps[bank][:, :], WSL[wi][:, c, :], hT[:, c, ch * 512:(ch + 1) * 512],
                         start=(c == 0), stop=(c == 7), reads=[bws[wi]] + hb, writes=[B(f'ps{bank}')])
            for qi, (dst, bdst, sc) in enumerate(((QT, bQ, 0.125), (KT, bK, 1.0))):
                ta, tb = TA[qi * 2], TA[qi * 2 + 1]
                bta, btb = B(f'ata{qi}'), B(f'atb{qi}')
                k.stt('vector', ta, k.ps[2 + qi * 2][:, :], sc, RC[s2], ALU.mult, ALU.mult,
                      reads=[B(f'ps{2 + qi * 2}'), brc], writes=[bta])
                k.stt('vector', tb, k.ps[3 + qi * 2][:, :], sc, RS[s2], ALU.mult, ALU.mult,
                      reads=[B(f'ps{3 + qi * 2}'), brs], writes=[btb])
                k.tt('gpsimd', dst[:, ch * 512:(ch + 1) * 512], ta, tb, ALU.add, reads=[bta, btb], writes=[bdst])
        for qc in range(8):
            for st in range(32):
                par = st % 2
                for c in range(2):
                    bank = par * 2 + c
                    k.mm(k.ps[bank][:, :], KT[c * 64:(c + 1) * 64, st * 128:(st + 1) * 128],
                         QT[c * 64:(c + 1) * 64, qc * 512:(qc + 1) * 512], reads=[bK, bQ], writes=[B(f'ps{bank}')])
                for c in range(2):
                    bank = par * 2 + c
                    k.act(PT[par * 2 + c], k.ps[bank][:, :], ACT.Exp, reads=[B(f'ps{bank}')], writes=[B(f'apt{par * 2 + c}')])
                for c in range(2):
                    for qs in range(4):
                        r = c * 4 + qs
                        oap, bo = oreg(r)
                        k.mm(oap, PT[par * 2 + c][:, qs * 128:(qs + 1) * 128], VA[:, st, 0:129],
                             start=(st == 0 and r % 3 == 0), stop=(st == 31),
                             reads=[B(f'apt{par * 2 + c}'), bV], writes=[bo], skip_group_check=True)
            for qs in range(4):
                o0, bo0 = oreg(qs)
                o1, bo1 = oreg(4 + qs)
                s2 = qs % 2
                cols = k.c_cols[:, 64 + s2 * 16: 64 + s2 * 16 + 16]
                bc = B(f'acol{s2}')
                osb, bos = OSB[s2], B(f'aosb{s2}')
                yn, byn = YN[s2], B(f'ayn{s2}')
                P_ = k.P
                P_.op('vector', lambda e, a=cols[:, 0:1], b=o0[:, 128:129]: e.reciprocal(out=a, in_=b), [bo0], [bc])
                P_.op('vector', lambda e, a=cols[:, 1:2], b=o1[:, 128:129]: e.reciprocal(out=a, in_=b), [bo1], [bc])
                k.tt('vector', cols[:, 2:3], cols[:, 1:2], neglam, ALU.mult, reads=[bc, bl], writes=[bc])
                k.ts('vector', osb, o0[:, 0:128], cols[:, 0:1], None, ALU.mult, reads=[bo0, bc], writes=[bos])
                k.stt('vector', osb, o1[:, 0:128], cols[:, 2:3], osb, ALU.mult, ALU.add, reads=[bo1, bc, bos], writes=[bos])
                k.stt('vector', JK, osb, 1.0, osb, ALU.mult, ALU.mult, reads=[bos], writes=[bc, B('ajk')], accum_out=cols[:, 3:4])
                rstd_from_ss(k, cols[:, 3:4], cols[:, 4:5], cols[:, 5:6], cols[:, 6:7], 128.0, 1e-5, [bc])
                k.ts('vector', yn, osb, cols[:, 6:7], 0.8, ALU.mult, ALU.mult, reads=[bos, bc], writes=[byn])
                k.tr(k.psb[7][:, 0:128], yn, k.c_ident_bf, reads=[byn, k.KC], writes=[B('ps7')])
                q0 = qc * 512 + qs * 128
                k.act(mixT[:, 4 + hd, q0:q0 + 128], k.psb[7][:, 0:128], ACT.Identity, reads=[B('ps7'), k.KC],
                      writes=[B(f'mix{4 + hd}')], scale=k.c_subg[:, 0:1])


def load_mlp_w(k):
    B = k.B
    WUP = k.bfr(HT0, 8 * 4096).rearrange("p (c n) -> p c n", c=8)
    k.WUP = WUP
    for c in range(8):
        k.dma('gpsimd', WUP[:, c, :], k.w_up[c * 128:(c + 1) * 128, :], writes=[B('wup')])


def phaseO(k):
    B = k.B
    mixT = k.mixT
    W = WORK0
    load_mlp_w(k)
    WO = k.bfr(W, 8 * 1024).rearrange("p (c n) -> p c n", c=8)
    GP = k.f32r(W + 16 * KB, 1024)
    XT = [k.f32r(W + 20 * KB + i * 4 * KB, 1024) for i in range(3)]
    MS = [k.f32r(W + 32 * KB + i * 4 * KB, 1024) for i in range(2)]
    TMP = [k.f32r(W + 40 * KB + i * 4 * KB, 1024) for i in range(2)]
    JK = k.bfr(W + 48 * KB, 1024)
    bwo, bgp = B('owo'), B('ogp')
    for half in range(2):
        k.dma('gpsimd', WO[:, :, half * 512:(half + 1) * 512],
              k.w_out[:, half * 512:(half + 1) * 512].rearrange("(c p) n -> p c n", p=128), writes=[bwo])
    k.dma('sync', GP, bass.AP(k.gpost.tensor, 0, [[0, 128], [1, 1024]]), writes=[bgp])
    mixb = [B(f'mix{c}') for c in range(8)]
    for i in range(32):
        s3, s2 = i % 3, i % 2
        xt, bx = XT[s3], B(f'ox{s3}')
        ms, bms = MS[s2], B(f'oms{s2}')
        tmp, btmp = TMP[s2], B(f'otmp{s2}')
        cols = k.c_cols[:, 96 + s2 * 8: 96 + s2 * 8 + 8]
        bc = B(f'ocol{s2}')
        k.dma('sync', xt, k.x[i * 128:(i + 1) * 128, :], writes=[bx])
        for half in range(2):
            bank = 2 * s2 + half
            bp = B(f'ps{bank}')
            for c in range(8):
                k.mm(k.ps[bank][:, :], mixT[:, c, i * 128:(i + 1) * 128], WO[:, c, half * 512:(half + 1) * 512],
                     start=(c == 0), stop=(c == 7), reads=[bwo] + mixb, writes=[bp])
            k.act(ms[:, half * 512:(half + 1) * 512], k.ps[bank][:, :], ACT.Copy, reads=[bp], writes=[bms])
        k.stt('vector', JK, ms, 1.0, ms, ALU.mult, ALU.mult, reads=[bms], writes=[bc, B('ojk')], accum_out=cols[:, 0:1])
        rstd_from_ss(k, cols[:, 0:1], cols[:, 1:2], cols[:, 2:3], cols[:, 3:4], 1024.0, 1e-6, [bc])
        k.stt('vector', tmp, ms, cols[:, 3:4], GP, ALU.mult, ALU.mult, reads=[bms, bc, bgp], writes=[btmp])
        k.tt('gpsimd', tmp, tmp, xt, ALU.add, reads=[btmp, bx], writes=[btmp])
        k.dma('sync', k.xa_scr[i * 128:(i + 1) * 128, :], tmp, reads=[btmp], writes=[B(f'xa{i}')], sembuf=btmp)


def phaseM(k):
    B = k.B
    WUP = k.WUP
    WDN = k.bfr(MIX0, 32 * 1024).rearrange("p (f n) -> p f n", f=32)
    for f0 in range(0, 32, 4):
        k.dma('gpsimd', WDN[:, f0:f0 + 4, :],
              k.w_down[f0 * 128:(f0 + 4) * 128, :].rearrange("(f p) n -> p f n", p=128), writes=[B('wdn')])
    W = WORK0
    GP = k.f32r(W, 1024)
    XA = [[k.f32r(W + 4 * KB + (a * 2 + b) * 4 * KB, 1024) for b in range(2)] for a in range(2)]
    XN = [k.bfr(W + 20 * KB + i * 2 * KB, 1024) for i in range(2)]
    H2T = [k.bfr(W + 24 * KB + i * 4 * KB, 8 * 256).rearrange("p (c t) -> p c t", c=8) for i in range(2)]
    UPT = k.bfr(W + 32 * KB, 32 * 256).rearrange("p (f t) -> p f t", f=32)
    RL = [k.f32r(W + 48 * KB + i * KB, 256) for i in range(2)]
    MS = [k.f32r(W + 50 * KB + i * 4 * KB, 1024) for i in range(2)]
    JK = k.bfr(W + 58 * KB, 1024)
    bgp = B('mgp')
    k.dma('sync', GP, bass.AP(k.gpost.tensor, 1024, [[0, 128], [1, 1024]]), writes=[bgp])
    gb = V(k.c_gcols, 8, [[1, 8], [0, 128]])
    bwup, bwdn, bupt = B('wup'), B('wdn'), B('mupt')
    k.out_ticks = []
    for ck in range(getattr(k, 'dev_nck', 16)):
        a = ck % 2
        bh2 = B(f'mh2{a}')
        for tl in range(2):
            i = ck * 2 + tl
            xa, bxa = XA[a][tl], B(f'mxa{a}{tl}')
            xn, bxn = XN[tl], B(f'mxn{tl}')
            cols = k.c_cols[:, 128 + tl * 8: 128 + tl * 8 + 8]
            bc = B(f'mcol{tl}')
            k.dma('sync', xa, k.xa_scr[i * 128:(i + 1) * 128, :], reads=[B(f'xa{i}')], writes=[bxa])
            k.stt('vector', JK, xa, 1.0, xa, ALU.mult, ALU.mult, reads=[bxa], writes=[bc, B('mjk')], accum_out=cols[:, 0:1])
            rstd_from_ss(k, cols[:, 0:1], cols[:, 1:2], cols[:, 2:3], cols[:, 3:4], 1024.0, 1e-6, [bc])
            k.act(xn, xa, ACT.Identity, reads=[bxa, bc], writes=[bxn], scale=cols[:, 3:4])
            bank = tl
            bp = B(f'ps{bank}')
            for c in range(8):
                k.tr(k.psb[bank][:, c * 128:(c + 1) * 128], xn[:, c * 128:(c + 1) * 128], k.c_ident_bf,
                     reads=[bxn, k.KC], writes=[bp])
            pin = k.psb[bank][:, 0:1024].rearrange("p (c t) -> p c t", c=8)
            k.tt('vector', H2T[a][:, :, tl * 128:(tl + 1) * 128], pin, gb, ALU.mult, reads=[bp, k.KC], writes=[bh2])
        for f in range(32):
            bank = 2 + f % 2
            bp = B(f'ps{bank}')
            rl, brl = RL[f % 2], B(f'mrl{f % 2}')
            for c in range(8):
                k.mm(k.ps[bank][:, 0:256], WUP[:, c, f * 128:(f + 1) * 128], H2T[a][:, c, :],
                     start=(c == 0), stop=(c == 7), reads=[bwup, bh2], writes=[bp])
            k.act(rl, k.ps[bank][:, 0:256], ACT.Relu, reads=[bp], writes=[brl])
            k.tt('gpsimd', UPT[:, f, :], rl, rl, ALU.mult, reads=[brl], writes=[bupt])
        for tl in range(2):
            i = ck * 2 + tl
            xa, bxa = XA[a][tl], B(f'mxa{a}{tl}')
            ms, bms = MS[tl], B(f'mms{tl}')
            cols = k.c_cols[:, 144 + tl * 8: 144 + tl * 8 + 8]
            bc = B(f'mcol2{tl}')
            for half in range(2):
                bank = 4 + tl * 2 + half
                bp = B(f'ps{bank}')
                for f in range(32):
                    k.mm(k.ps[bank][:, :], UPT[:, f, tl * 128:(tl + 1) * 128], WDN[:, f, half * 512:(half + 1) * 512],
                         start=(f == 0), stop=(f == 31), reads=[bupt, bwdn], writes=[bp])
                k.act(ms[:, half * 512:(half + 1) * 512], k.ps[bank][:, :], ACT.Copy, reads=[bp], writes=[bms])
            k.stt('vector', JK, ms, 1.0, ms, ALU.mult, ALU.mult, reads=[bms], writes=[bc, B('mjk')], accum_out=cols[:, 0:1])
            rstd_from_ss(k, cols[:, 0:1], cols[:, 1:2], cols[:, 2:3], cols[:, 3:4], 1024.0, 1e-6, [bc])
            k.stt('vector', ms, ms, cols[:, 3:4], GP, ALU.mult, ALU.mult, reads=[bms, bc, bgp], writes=[bms])
            k.tt('gpsimd', ms, ms, xa, ALU.add, reads=[bms, bxa], writes=[bms])
            k.out_ticks.append(k.dma('sync', k.out[i * 128:(i + 1) * 128, :], ms, reads=[bms], sembuf=bms))


def host_inputs(inputs):
    T = host_tables()

    def g(k):
        return np.asarray(inputs[k], dtype=np.float32)
    w_in = np.ascontiguousarray(g('w_in')[0])
    qk = w_in[:, 1536:2560]
    idx = np.arange(1024)
    d = idx % 64
    perm = (idx - d) + (d + 32) % 64
    sh = {}
    sh['w_in'] = w_in
    sh['w_qkp'] = np.ascontiguousarray(qk[:, perm])
    cwb = np.concatenate([g('conv_w')[0], g('conv_b')[0][None, :]], axis=0)
    sh['cw'] = np.ascontiguousarray(cwb.reshape(4, 12, 128).transpose(2, 1, 0).reshape(128, 48))
    sh['fw1'] = np.ascontiguousarray(g('filt_w1')[0])
    sh['fw2'] = np.ascontiguousarray(g('filt_w2')[0])
    sh['fw3'] = np.ascontiguousarray(g('filt_w3')[0])
    sh['fw4'] = np.ascontiguousarray(g('filt_w4')[0])
    sh['fcol'] = np.ascontiguousarray(np.stack([g('filt_b1')[0], g('filt_b2')[0], g('filt_b3')[0], g('filt_freq')[0]], axis=1))
    sh['fbias'] = np.ascontiguousarray(g('filt_bias')[0].reshape(4, 128).T)
    sh['lamv'] = np.concatenate([g('lam_q1')[0], g('lam_k1')[0], g('lam_q2')[0], g('lam_k2')[0]])[None, :].copy()
    sh['subg'] = np.ascontiguousarray(g('subln_gain')[0][:, None])
    sh['w_out'] = np.ascontiguousarray(g('w_out')[0])
    sh['w_up'] = np.ascontiguousarray(g('w_up')[0])
    sh['w_down'] = np.ascontiguousarray(g('w_down')[0])
    sh['gcols'] = np.ascontiguousarray(np.stack([g('attn_pre_gain')[0].reshape(8, 128).T,
                                                 g('mlp_pre_gain')[0].reshape(8, 128).T], axis=1).reshape(128, 16))
    sh['gpost'] = np.ascontiguousarray(np.stack([g('attn_post_gain')[0], g('mlp_post_gain')[0]], axis=0))
    sh['ident_bf'] = T['ident_bf']
    sh['ident_f'] = T['ident_f']
    sh['F1'] = T['F1']
    sh['Tw1'] = T['Tw1'].reshape(128, 256)
    sh['CS4'] = T['CS4'].reshape(128, 512)
    sh['W3'] = T['W3'].reshape(128, 768)
    sh['Tw2'] = T['Tw2'].reshape(128, 64)
    sh['PHI'] = T['PHI'].reshape(128, 384)
    sh['zT'] = T['zT']
    sh['negt'] = T['negt']
    sh['absdelta'] = T['absdelta']
    sh['ropec'] = T['ropec']
    sh['ropes'] = T['ropes']
    return sh


PHASES = ['p0', 'pF', 'pH', 'pA', 'pO', 'pM']


def make_program(upto='pM', dumps=(), dump_fn=None, small_out=False, **attrs):
    k = K(dumps=dumps, small_out=small_out)
    k.__dict__.update(attrs)
    fns = {'p0': phase0, 'pF': phaseF, 'pH': phaseH, 'pA': phaseA, 'pO': phaseO, 'pM': phaseM}
    for ph in PHASES:
        fns[ph](k)
        k.barrier()
        if ph == upto:
            break
    ticks = []
    if dump_fn is not None:
        ticks += dump_fn(k)
    ticks += getattr(k, 'out_ticks', [])
    k.P.wait_all('sync', ticks)
    with k.nc.Block() as block:
        k.P.emit(block)
    k.es.close()
    return k.nc


def kernel(**inputs):
    sh = host_inputs(inputs)
    x = np.asarray(inputs['x'], dtype=np.float32)
    nc = make_program()
    in_maps = []
    for c in range(8):
        m = dict(sh)
        m['x'] = np.ascontiguousarray(x[c])
        in_maps.append(m)
    res = run_bass_kernel_spmd(nc, in_maps, core_ids=list(range(8)))
    return np.stack([np.asarray(r['out'], dtype=np.float32) for r in res.results], axis=0)
```

```python
import numpy as np
import concourse.bass as bass
import concourse.mybir as mybir
from concourse.bass_utils import run_bass_kernel_spmd
from contextlib import ExitStack

F32 = mybir.dt.float32
BF16 = mybir.dt.bfloat16
ALU = mybir.AluOpType
ACT = mybir.ActivationFunctionType
AX = mybir.AxisListType

ENGS = ('tensor', 'vector', 'scalar', 'gpsimd', 'sync')
SEM_ROT = 16000


class Buf:
    __slots__ = ('name', 'w', 'r', 'dsem', 'excl')

    def __init__(self, name, excl=False):
        self.name = name
        self.excl = excl
        self.w = None
        self.r = {}
        self.dsem = None


class Prog:
    def __init__(self, nc, sems):
        self.nc = nc
        self.sems = sems
        self.nsem = 0
        self.q = {e: [] for e in ENGS}
        self.cnt = {e: 0 for e in ENGS}
        self.esem = {}
        self.owned = {e: set() for e in ENGS}
        self.seen = {e: {} for e in ENGS}
        self.dcnt = {}
        self.final = {}
        for e in ENGS:
            self._new_esem(e)

    def alloc_sem(self):
        i = self.nsem
        self.nsem += 1
        assert i < len(self.sems), "out of semaphores"
        return i

    def _new_esem(self, e):
        i = self.alloc_sem()
        self.esem[e] = i
        self.owned[e].add(i)
        self.cnt[e] = 0

    def _deps(self, eng, reads, writes):
        need = {}

        def add(t, raw=False):
            if t is None:
                return
            s, v = t
            if s in self.owned[eng] and not (raw and eng != 'tensor'):
                return
            if need.get(s, 0) < v:
                need[s] = v
        for b in reads:
            add(b.w, raw=True)
        for b in writes:
            add(b.w)
            for s, v in b.r.items():
                add((s, v))
        out = []
        seen = self.seen[eng]
        for s, v in need.items():
            if seen.get(s, 0) < v:
                seen[s] = v
                out.append((s, v))
        return out

    def op(self, eng, fn, reads=(), writes=()):
        if any(b.excl for b in reads):
            writes = list(writes) + [b for b in reads if b.excl]
            reads = [b for b in reads if not b.excl]
        waits = self._deps(eng, reads, writes)
        if self.cnt[eng] >= SEM_ROT:
            self._new_esem(eng)
        self.cnt[eng] += 1
        s, v = self.esem[eng], self.cnt[eng]
        for b in reads:
            if b.r.get(s, 0) < v:
                b.r[s] = v
        for b in writes:
            b.w = (s, v)
            b.r = {}
        self.q[eng].append((waits, fn, s, 1))

    def dma(self, qeng, fn, reads=(), writes=(), sembuf=None):
        waits = self._deps(qeng, reads, writes)
        sb = sembuf if sembuf is not None else (writes[0] if writes else reads[0])
        if sb.dsem is None or self.dcnt[sb.dsem] >= SEM_ROT:
            sb.dsem = self.alloc_sem()
            self.dcnt[sb.dsem] = 0
        s = sb.dsem
        self.dcnt[s] += 16
        v = self.dcnt[s]
        for b in reads:
            if b.r.get(s, 0) < v:
                b.r[s] = v
        for b in writes:
            b.w = (s, v)
            b.r = {}
        self.q[qeng].append((waits, fn, s, 16))
        return (s, v)

    def wait_all(self, eng, ticks):
        need = {}
        for s, v in ticks:
            if need.get(s, 0) < v:
                need[s] = v
        self.q[eng].append((list(need.items()), None, None, 0))

    def barrier(self):
        ticks = [(self.esem[o], self.cnt[o]) for o in ENGS if self.cnt[o] > 0]
        ticks += [(s, v) for s, v in self.dcnt.items()]
        for e in ENGS:
            need = []
            for s, v in ticks:
                if s in self.owned[e]:
                    continue
                if self.seen[e].get(s, 0) < v:
                    self.seen[e][s] = v
                    need.append((s, v))
            if need:
                self.q[e].append((need, None, None, 0))

    def replay(self, engobj, eng):
        sems = self.sems
        for waits, fn, s, inc in self.q[eng]:
            for ws, wv in waits:
                engobj.wait_ge(sems[ws], wv)
            if fn is not None:
                inst = fn(engobj)
                inst.then_inc(sems[s], inc)

    def emit(self, block):
        @block.tensor
        def _(e):
            self.replay(e, 'tensor')

        @block.vector
        def _(e):
            self.replay(e, 'vector')

        @block.scalar
        def _(e):
            self.replay(e, 'scalar')

        @block.gpsimd
        def _(e):
            self.replay(e, 'gpsimd')

        @block.sync
        def _(e):
            self.replay(e, 'sync')


def AP(base, off, dims):
    return bass.AP(base.tensor, off, [list(d) for d in dims])


S = 4096
D = 1024
NFFT = 8192
PI = float(np.pi)


def host_tables():
    import ml_dtypes
    bf = ml_dtypes.bfloat16
    T = {}
    T['ident_bf'] = np.eye(128, dtype=np.float32).astype(bf)
    T['ident_f'] = np.eye(128, dtype=np.float32)
    p = np.arange(128, dtype=np.float64)
    k1 = np.arange(128, dtype=np.float64)
    j = np.arange(32, dtype=np.float64)
    k2 = np.arange(32, dtype=np.float64)
    ang = 2 * np.pi * (k1[None, :] + 0.5) * p[:, None] / 256.0
    T['F1'] = np.concatenate([np.cos(ang), -np.sin(ang)], axis=1).astype(np.float32).astype(bf)
    jj = np.repeat(j, 4)
    a1 = 2 * np.pi * (k1[None, :] + 0.5) * jj[:, None] / NFFT
    T['Tw1'] = np.stack([np.cos(a1), -np.sin(a1)], axis=1).astype(np.float32)
    th = 2 * np.pi * np.outer(j, k2) / 32.0
    I4 = np.eye(4)
    C = np.kron(np.cos(th), I4)
    Sm = np.kron(np.sin(th), I4)
    T['CS4'] = np.stack([C, -C, Sm, -Sm], axis=1).astype(np.float32).astype(bf)
    W1 = np.concatenate([C, Sm], axis=1)
    W2 = np.concatenate([-C, -Sm], axis=1)
    W34 = np.concatenate([-Sm, C], axis=1)
    T['W3'] = np.stack([W1, W2, W34], axis=1).astype(np.float32).astype(bf)
    a2 = 2 * np.pi * (k1[:, None] + 0.5) * j[None, :] / NFFT
    T['Tw2'] = np.stack([np.cos(a2), -np.sin(a2)], axis=1).astype(np.float32)
    ph = 2 * np.pi * (k1[:, None] + 0.5) * p[None, :] / 256.0
    cphi = (2.0 / NFFT) * np.cos(ph)
    sphi = (2.0 / NFFT) * np.sin(ph)
    T['PHI'] = np.stack([cphi, -sphi, sphi], axis=1).astype(np.float32).astype(bf)
    bands = 16
    t = np.linspace(0.0, 1.0, S, dtype=np.float32)[:, None]
    w = (np.float32(2.0 * np.pi / S) * np.arange(S, dtype=np.float32))[:, None]
    f = np.linspace(1e-4, bands - 1, bands, dtype=np.float32)[None, :]
    fw_ = (f * w).astype(np.float32)
    z = np.concatenate([t, np.cos(fw_), -np.sin(fw_)], axis=-1).astype(np.float32)
    T['zT'] = np.ascontiguousarray(z.T)
    tt = np.linspace(0.0, 1.0, S, dtype=np.float32).reshape(128, 32)
    T['negt'] = (-tt).astype(np.float32)
    import math
    max_decay = math.log(1e-2) / 0.3
    min_decay = math.log(1e-2) / 1.5
    deltas = np.abs(np.linspace(min_decay, max_decay, 512, dtype=np.float32))
    T['absdelta'] = np.ascontiguousarray(np.broadcast_to(deltas[None, :], (128, 512))).astype(np.float32)
    inv = (10000.0 ** (-np.arange(0, 64, 2, dtype=np.float32) / 64)).astype(np.float32)
    angr = (np.arange(S, dtype=np.float32)[:, None] * inv[None, :]).astype(np.float32)
    angr = np.concatenate([angr, angr], axis=-1)
    cosr = np.cos(angr).astype(np.float32)
    sinr = np.sin(angr).astype(np.float32)
    sgn = np.concatenate([-np.ones(32, np.float32), np.ones(32, np.float32)])
    sins = sinr * sgn[None, :]
    T['ropec'] = np.ascontiguousarray(np.concatenate([cosr.T, cosr.T], axis=0))
    T['ropes'] = np.ascontiguousarray(np.concatenate([sins.T, sins.T], axis=0))
    return T


KB = 1024
CONST0 = 0
HT0 = 14 * KB
MIX0 = 78 * KB
WORK0 = 142 * KB
HW0 = 118 * KB
ARENA_F32 = 53100


class Ctx:
    pass


def V(region, off, dims, p0=0, np_=128):
    pstep = region.ap[0][0]
    return bass.AP(region.tensor, region.offset + p0 * pstep + off,
                   [[pstep, np_]] + [list(d) for d in dims])


class K:
  def __init__(self, dumps=(), small_out=False):
    nc = bass.Bass("TRN2", target_bir_lowering=False)

    def din(name, shape, dt=F32):
        return nc.dram_tensor(name, list(shape), dt, kind="ExternalInput").ap()

    x = din('x', [S, D])
    w_in = din('w_in', [D, 3072])
    w_qkp = din('w_qkp', [D, 1024])
    cw = din('cw', [128, 48])
    fw1 = din('fw1', [33, 64])
    fw2 = din('fw2', [64, 64])
    fw3 = din('fw3', [64, 64])
    fw4 = din('fw4', [64, 1024])
    fcol = din('fcol', [64, 4])
    fbias = din('fbias', [128, 4])
    lamv = din('lamv', [1, 256])
    subg = din('subg', [128, 1])
    w_out = din('w_out', [D, D])
    w_up = din('w_up', [D, 4096])
    w_down = din('w_down', [4096, D])
    gcols = din('gcols', [128, 16])
    gpost = din('gpost', [2, 1024])
    t_ident_bf = din('ident_bf', [128, 128], BF16)
    t_ident_f = din('ident_f', [128, 128])
    t_F1 = din('F1', [128, 256], BF16)
    t_Tw1 = din('Tw1', [128, 256])
    t_CS4 = din('CS4', [128, 512], BF16)
    t_W3 = din('W3', [128, 768], BF16)
    t_Tw2 = din('Tw2', [128, 64])
    t_PHI = din('PHI', [128, 384], BF16)
    t_zT = din('zT', [33, S])
    t_negt = din('negt', [128, 32])
    t_absd = din('absdelta', [128, 512])
    t_ropec = din('ropec', [128, S])
    t_ropes = din('ropes', [128, S])
    out = nc.dram_tensor('out', [128 if small_out else S, D], F32, kind='ExternalOutput').ap()
    xa_scr = nc.dram_tensor('xa_scr', [S, D], F32, kind='Internal').ap()
    dump_aps = {}
    for (nm, shape, dt) in dumps:
        dump_aps[nm] = nc.dram_tensor('dbg_' + nm, list(shape), dt, kind='ExternalOutput').ap()

    es = ExitStack()
    sems = [es.enter_context(nc.semaphore(f"s{i}")) for i in range(88)]
    arena_t = es.enter_context(nc.sbuf_tensor("arena", [128, ARENA_F32], F32))
    pst = [es.enter_context(nc.psum_tensor(f"ps{i}", [128, 512], F32)) for i in range(8)]
    ps = [t[:] for t in pst]
    psb = [t[:].bitcast(BF16) for t in pst]
    AF = arena_t[:]
    AB = arena_t[:].bitcast(BF16)

    def f32r(boff, n):
        assert boff % 4 == 0 and boff + 4 * n <= ARENA_F32 * 4, (boff, n)
        return AF[:, boff // 4: boff // 4 + n]

    def bfr(boff, n):
        assert boff % 4 == 0 and boff + 2 * n <= ARENA_F32 * 4, (boff, n)
        return AB[:, boff // 2: boff // 2 + n]

    P = Prog(nc, sems)
    bufs = {}

    def B(name):
        b = bufs.get(name)
        if b is None:
            b = Buf(name, excl=name.startswith('ps'))
            bufs[name] = b
        return b

    def mm(out, lhsT, rhs, start=True, stop=True, reads=(), writes=(), **kw):
        P.op('tensor', lambda e: e.matmul(out, lhsT=lhsT, rhs=rhs, start=start, stop=stop, **kw), reads, writes)

    def tr(out, in_, ident, reads=(), writes=()):
        P.op('tensor', lambda e: e.transpose(out=out, in_=in_, identity=ident), reads, writes)

    def act(out, in_, func, reads=(), writes=(), **kw):
        P.op('scalar', lambda e: e.activation(out=out, in_=in_, func=func, **kw), reads, writes)

    def tt(eng, out, in0, in1, op, reads=(), writes=()):
        P.op(eng, lambda e: e.tensor_tensor(out=out, in0=in0, in1=in1, op=op), reads, writes)

    def ts(eng, out, in0, s1, s2, op0, op1=None, reads=(), writes=(), **kw):
        if op1 is None:
            P.op(eng, lambda e: e.tensor_scalar(out=out, in0=in0, scalar1=s1, scalar2=None, op0=op0, **kw), reads, writes)
        else:
            P.op(eng, lambda e: e.tensor_scalar(out=out, in0=in0, scalar1=s1, scalar2=s2, op0=op0, op1=op1, **kw), reads, writes)

    def stt(eng, out, in0, scalar, in1, op0, op1, reads=(), writes=(), **kw):
        P.op(eng, lambda e: e.scalar_tensor_tensor(out=out, in0=in0, scalar=scalar, in1=in1, op0=op0, op1=op1, **kw), reads, writes)

    def cp(eng, out, in_, reads=(), writes=()):
        P.op(eng, lambda e: e.tensor_copy(out=out, in_=in_), reads, writes)

    def mset(eng, ap, val, writes=()):
        P.op(eng, lambda e: e.memset(ap, val), (), writes)

    def dma(q, out, in_, reads=(), writes=(), sembuf=None):
        return P.dma(q, lambda e: e.dma_start(out=out, in_=in_), reads, writes, sembuf)

    def barrier():
        P.barrier()

    cpos = [CONST0]

    def calloc(nbytes):
        o = cpos[0]
        cpos[0] += (nbytes + 63) // 64 * 64
        assert cpos[0] <= HT0, cpos[0]
        return o

    c_ident_bf = bfr(calloc(256), 128)
    c_ident_f = f32r(calloc(512), 128)
    c_F1 = bfr(calloc(512), 256)
    c_Tw1 = f32r(calloc(1024), 256)
    c_CS4 = bfr(calloc(1024), 512)
    c_W3 = bfr(calloc(1536), 768)
    c_Tw2 = f32r(calloc(256), 64)
    c_PHI = bfr(calloc(768), 384)
    c_gcols = f32r(calloc(64), 16)
    c_cw = f32r(calloc(192), 48)
    c_fbias = f32r(calloc(16), 4)
    c_subg = f32r(calloc(4), 1)
    c_lamv = f32r(calloc(1024), 256)
    c_negt = f32r(calloc(128), 32)
    c_absd = f32r(calloc(2048), 512)
    c_fw1 = f32r(calloc(256), 64)
    c_fw2 = f32r(calloc(256), 64)
    c_fw3 = f32r(calloc(256), 64)
    c_fcol = f32r(calloc(16), 4)
    c_fw4 = bfr(calloc(2048), 1024)
    c_cols = f32r(calloc(1024), 256)
    c_lam = f32r(calloc(64), 16)
    KC = B('consts')
    for dst, src in [(c_ident_f, t_ident_f), (c_Tw1, t_Tw1), (c_Tw2, t_Tw2), (c_gcols, gcols), (c_cw, cw),
                     (c_fbias, fbias), (c_subg, subg), (c_negt, t_negt), (c_absd, t_absd), (c_fcol[0:64, :], fcol),
                     (c_fw1[0:33, :], fw1), (c_fw2[0:64, :], fw2), (c_fw3[0:64, :], fw3),
                     (c_ident_bf, t_ident_bf), (c_F1, t_F1), (c_CS4, t_CS4), (c_W3, t_W3), (c_PHI, t_PHI)]:
        dma('sync', dst, src, writes=[KC])
    dma('sync', c_lamv, bass.AP(lamv.tensor, 0, [[0, 128], [1, 256]]), writes=[KC])
    dma('gpsimd', c_fw4[0:64, :], fw4, writes=[KC])

    d = dict(locals())
    d.pop('self')
    self.__dict__.update(d)


def rstd_from_ss(k, ss, ms, ln, rstd, n, eps, bufs):
    k.ts('vector', ms, ss, 1.0 / n, eps, ALU.mult, ALU.add, reads=bufs, writes=bufs)
    k.act(ln, ms, ACT.Ln, reads=bufs, writes=bufs)
    k.act(rstd, ln, ACT.Exp, reads=bufs, writes=bufs, scale=-0.5)


def phase0(k):
    B = k.B
    hT = k.bfr(HT0, 8 * S).rearrange("p (c t) -> p c t", c=8)
    k.hT = hT
    XT = [k.f32r(WORK0 + i * 4096, 1024) for i in range(3)]
    XN = [k.bfr(WORK0 + 12288 + i * 2048, 1024) for i in range(2)]
    JK = k.bfr(WORK0 + 16384, 1024)
    gb = V(k.c_gcols, 0, [[1, 8], [0, 128]])
    for i in range(getattr(k, 'dev_ntiles', 32)):
        s3, s2 = i % 3, i % 2
        xt, xn = XT[s3], XN[s2]
        bx, bn, bc = B(f'p0x{s3}'), B(f'p0n{s2}'), B(f'p0c{s2}')
        cols = k.c_cols[:, s2 * 8: s2 * 8 + 8]
        ss, ms, ln, rs = cols[:, 0:1], cols[:, 1:2], cols[:, 2:3], cols[:, 3:4]
        k.dma('sync', xt, k.x[i * 128:(i + 1) * 128, :], writes=[bx])
        k.stt('vector', JK, xt, 1.0, xt, ALU.mult, ALU.mult, reads=[bx], writes=[B('p0jk'), bc], accum_out=ss)
        rstd_from_ss(k, ss, ms, ln, rs, 1024.0, 1e-6, [bc])
        k.act(xn, xt, ACT.Identity, reads=[bx, bc], writes=[bn], scale=rs)
        bank = s2
        bp = B(f'ps{bank}')
        for c in range(8):
            k.tr(k.psb[bank][:, c * 128:(c + 1) * 128], xn[:, c * 128:(c + 1) * 128], k.c_ident_bf,
                 reads=[bn, k.KC], writes=[bp])
        pin = k.psb[bank][:, 0:1024].rearrange("p (c t) -> p c t", c=8)
        k.tt('vector', hT[:, :, i * 128:(i + 1) * 128], pin, gb, ALU.mult, reads=[bp, k.KC], writes=[B(f'hT{i}')])


H3T0 = MIX0 + 32 * KB


def phaseF(k):
    B = k.B
    ZT = k.f32r(HW0, S)
    ARG = k.f32r(HW0 + 16 * KB, S)
    WT = k.f32r(HW0 + 32 * KB, S)
    HA = k.f32r(HW0 + 48 * KB, S)
    HB = k.f32r(HW0 + 64 * KB, S)
    h3T = k.bfr(H3T0, S)
    k.h3T = h3T
    bz, ba, bw = B('fz'), B('farg'), B('fwt')
    k.dma('sync', ZT[0:33, :], k.t_zT, writes=[bz])
    freq = k.c_fcol[0:64, 3:4]
    layers = [(k.c_fw1[0:33, :], ZT[0:33, :], HA, bz, B('fha')),
              (k.c_fw2[0:64, :], HA[0:64, :], HB, B('fha'), B('fhb')),
              (k.c_fw3[0:64, :], HB[0:64, :], h3T, B('fhb'), B('h3T'))]
    for L, (w, inp, outp, bin_, bout) in enumerate(layers):
        fb = k.c_cols[0:64, 32 + L: 33 + L]
        bfb = B(f'ffb{L}')
        k.tt('vector', fb, k.c_fcol[0:64, L:L + 1], freq, ALU.mult, reads=[k.KC], writes=[bfb])
        for ch in range(8):
            bank = ch % 2
            bp = B(f'ps{bank}')
            k.mm(k.ps[bank][0:64, :], w, inp[:, ch * 512:(ch + 1) * 512], reads=[k.KC, bin_], writes=[bp])
            k.ts('vector', ARG[0:64, ch * 512:(ch + 1) * 512], k.ps[bank][0:64, :], freq, fb, ALU.mult, ALU.add,
                 reads=[bp, k.KC, bfb], writes=[ba])
        a = ARG[0:64, :]
        wt = WT[0:64, :]
        k.ts('vector', wt, a, PI, -2 * PI, ALU.is_gt, ALU.mult, reads=[ba], writes=[bw])
        k.tt('vector', a, a, wt, ALU.add, reads=[ba, bw], writes=[ba])
        k.ts('vector', wt, a, -PI, 2 * PI, ALU.is_lt, ALU.mult, reads=[ba], writes=[bw])
        k.tt('vector', a, a, wt, ALU.add, reads=[ba, bw], writes=[ba])
        k.act(outp[0:64, :], a, ACT.Sin, reads=[ba], writes=[bout])


def load_w(k, dst3, src_cols, bw):
    k.dma('gpsimd', dst3, src_cols.rearrange("(c p) n -> p c n", p=128), writes=[bw])


def proj_fm(k, wt3, col0, ncols, dst_fn, bw, tag, banks=(6, 7)):
    B = k.B
    for ch in range(8):
        bank = banks[ch % len(banks)]
        bp = B(f'ps{bank}')
        hb = [B(f'hT{i}') for i in range(ch * 4, ch * 4 + 4)]
        for c in range(8):
            k.mm(k.ps[bank][:, :], wt3[:, c, col0:col0 + ncols], k.hT[:, c, ch * 512:(ch + 1) * 512],
                 start=(c == 0), stop=(c == 7), reads=[bw] + hb, writes=[bp])
        dst_fn(ch, k.ps[bank][:, :], bp)


def short_conv(k, raw, t1, dst, ti, braw, bt1, bdst, eng='vector'):
    w0, w1, w2, bb = [k.c_cw[:, ti * 4 + i: ti * 4 + i + 1] for i in range(4)]
    k.ts(eng, t1, raw[:, 1:S + 1], w1, bb, ALU.mult, ALU.add, reads=[braw, k.KC], writes=[bt1])
    k.stt(eng, t1, raw[:, 0:S], w0, t1, ALU.mult, ALU.add, reads=[braw, bt1, k.KC], writes=[bt1])
    k.stt(eng, dst, raw[:, 2:S + 2], w2, t1, ALU.mult, ALU.add, reads=[braw, bt1, k.KC], writes=[bdst])


def fft_fwd_batch(k, d0_groups, pslots, a_slots, tag):
    P_reg, bP = pslots
    for gi, (lh, bd) in enumerate(d0_groups):
        pa, ba = a_slots[gi]
        k.mm(pa, lh, k.c_F1, reads=[bd, k.KC], writes=[ba])
    for gi, (lh, bd) in enumerate(d0_groups):
        pa, ba = a_slots[gi]
        o = V(P_reg, gi * 128, [[2 * 512, 2], [512, 2], [1, 128]])
        i0 = bass.AP(pa.tensor, pa.offset, [list(pa.ap[0]), [0, 2], [128, 2], [1, 128]])
        i1 = V(k.c_Tw1, 0, [[128, 2], [0, 2], [1, 128]])
        k.tt('vector', o, i0, i1, ALU.mult, reads=[ba, k.KC], writes=[bP])


def phaseH(k):
    B = k.B
    mixT = k.bfr(MIX0, 8 * S).rearrange("p (c t) -> p c t", c=8)
    k.mixT = mixT
    U = k.bfr(HW0, S)
    G = k.bfr(HW0 + 8 * KB, 2 * 32 * 128)
    R = k.bfr(HW0 + 24 * KB, 4 * 32 * 128)
    D0F = k.bfr(HW0 + 24 * KB, 2 * 32 * 128)
    PF = [k.bfr(HW0 + 40 * KB + i * 4 * KB, 2048) for i in range(4)]
    X0 = HW0 + 56 * KB
    RAW = k.f32r(X0, S + 4)
    T1 = k.f32r(X0 + 16400 + 16, S)
    D0 = k.bfr(X0, 32 * 128)
    PU = [k.bfr(X0 + 8 * KB + i * 4 * KB, 2048) for i in range(2)]
    QS = [k.bfr(X0 + 16 * KB + i * 4 * KB, 2048) for i in range(2)]
    DEC = [k.f32r(X0 + i * 512, 128) for i in range(2)]
    WSL = [k.bfr(HW0 + 8 * KB + i * 2 * KB, 1024).rearrange("p (c n) -> p c n", c=8) for i in range(3)]
    X0C = k.bfr(HW0 + 8 * KB + 2 * KB, S)
    TMP = [k.f32r(HW0 + 8 * KB + 10 * KB + i * 2 * KB, 512) for i in range(2)]
    hT = k.hT
    ASL = [(k.ps[b][:, h * 256:(h + 1) * 256], B(f'ps{b}')) for b in range(2) for h in range(2)]
    CSm = lambda i: k.c_CS4[:, i * 128:(i + 1) * 128]
    W3m = lambda i: k.c_W3[:, i * 256:(i + 1) * 256]
    PHIm = lambda i: k.c_PHI[:, i * 128:(i + 1) * 128]
    XRE = [0, 2, 2, 1]
    XIM = [3, 0, 0, 2]
    XIMN = [2, 1, 1, 3]
    QW = [0, 2, 2, 1]
    RW = [0, 1, 2, 0]
    braw, bt1, bu, bx0c = B('hraw'), B('ht1'), B('hu'), B('hx0c')

    for ct in getattr(k, 'dev_cts', range(4)):
        k.mset('vector', RAW[:, 0:1], 0.0, writes=[braw])
        k.mset('vector', RAW[:, S + 1:S + 2], 0.0, writes=[braw])
        for wi, col in enumerate((512 + ct * 128, 1024 + ct * 128)):
            bw = B(f'hw{wi}')
            load_w(k, WSL[wi], k.w_in[:, col:col + 128], bw)

            def evac(ch, pap, bp):
                k.act(RAW[:, 1 + ch * 512: 1 + (ch + 1) * 512], pap, ACT.Copy, reads=[bp], writes=[braw])
            proj_fm(k, WSL[wi], 0, 128, evac, bw, f'h{wi}')
            ti = col // 128
            if wi == 0:
                short_conv(k, RAW, T1, U, ti, braw, bt1, bu)
            else:
                short_conv(k, RAW, T1, T1, ti, braw, bt1, bt1)
                k.tt('gpsimd', U, T1, U, ALU.mult, reads=[bt1, bu], writes=[bu])
        k.barrier()
        bd0f = B('hd0f')
        bG = B('hG')
        for j in range(32):
            bank = 6 + j % 2
            bp = B(f'ps{bank}')
            s2 = j % 2
            bdec = B(f'hdec{s2}')
            lh = V(k.h3T, j, [[32, 128]], p0=0, np_=64)
            rh = V(k.c_fw4, ct * 128, [[512, 2], [1, 128]], p0=0, np_=64)
            k.mm(k.ps[bank][:, 0:256], lh, rh, reads=[B('h3T'), k.KC], writes=[bp])
            k.act(DEC[s2], k.c_absd[:, ct * 128:(ct + 1) * 128], ACT.Exp, reads=[k.KC], writes=[bdec],
                  scale=k.c_negt[:, j:j + 1])
            o = V(D0F, j * 4, [[32 * 128, 2], [128, 32], [1, 4]])
            i0 = bass.AP(k.ps[bank].tensor, k.ps[bank].offset, [list(k.ps[bank].ap[0]), [128, 2], [4, 32], [1, 4]])
            i1 = V(DEC[s2], 0, [[0, 2], [4, 32], [1, 4]])
            k.tt('vector', o, i0, i1, ALU.mult, reads=[bp, bdec], writes=[bd0f])
        for bt in range(8):
            pf, pb_ = (PF[(bt % 2) * 2], B(f'hpf{bt % 2}')), (PF[(bt % 2) * 2 + 1], B(f'hpb{bt % 2}'))
            for di, pslot in enumerate((pf, pb_)):
                groups = [(V(D0F, di * 4096 + (bt * 4 + gi) * 128, [[1, 128]]), bd0f) for gi in range(4)]
                fft_fwd_batch(k, groups, pslot, ASL, 'f')
            bre, bim = 2 + (bt % 2) * 2, 3 + (bt % 2) * 2
            bpr, bpi = B(f'ps{bre}'), B(f'ps{bim}')
            for (bank, bp, cf, cb) in ((bre, bpr, XRE, XRE), (bim, bpi, XIM, XIMN)):
                n = 0
                for (preg, bP), coef in ((pf, cf), (pb_, cb)):
                    for comp in range(4):
                        k.mm(k.ps[bank][:, :], CSm(coef[comp]), preg[:, comp * 512:(comp + 1) * 512],
                             start=(n == 0), stop=(n == 7), reads=[bP, k.KC], writes=[bp])
                        n += 1
            for ri, (bank, bp) in enumerate(((bre, bpr), (bim, bpi))):
                k.act(G[:, ri * 4096 + bt * 512: ri * 4096 + (bt + 1) * 512], k.ps[bank][:, :], ACT.Copy,
                      reads=[bp], writes=[bG])
        k.barrier()
        bd0 = B('hd0')
        bR = B('hR')
        for jb in range(4):
            bank = 6 + jb % 2
            bp = B(f'ps{bank}')
            for jj in range(8):
                j = jb * 8 + jj
                k.tr(k.psb[bank][:, jj * 128:(jj + 1) * 128], V(U, j, [[32, 128]]), k.c_ident_bf,
                     reads=[bu, k.KC], writes=[bp])
            o = V(D0, jb * 8 * 4, [[4, 8], [128, 32], [1, 4]])
            i0 = bass.AP(k.psb[bank].tensor, k.psb[bank].offset, [list(k.psb[bank].ap[0]), [128, 8], [4, 32], [1, 4]])
            k.cp('vector', o, i0, reads=[bp], writes=[bd0])
        for bt in range(8):
            pu = (PU[bt % 2], B(f'hpu{bt % 2}'))
            groups = [(V(D0, (bt * 4 + gi) * 128, [[1, 128]]), bd0) for gi in range(4)]
            fft_fwd_batch(k, groups, pu, ASL, 'u')
            bre, bim = 2 + (bt % 2) * 2, 3 + (bt % 2) * 2
            bpr, bpi = B(f'ps{bre}'), B(f'ps{bim}')
            for (bank, bp, cf) in ((bre, bpr, XRE), (bim, bpi, XIM)):
                for comp in range(4):
                    k.mm(k.ps[bank][:, :], CSm(cf[comp]), pu[0][:, comp * 512:(comp + 1) * 512],
                         start=(comp == 0), stop=(comp == 3), reads=[pu[1], k.KC], writes=[bp])
            qreg, bq = QS[bt % 2], B(f'hq{bt % 2}')
            gsl = V(G, bt * 512, [[4096, 2], [1, 512]])
            for half, (bank, bp) in enumerate(((bre, bpr), (bim, bpi))):
                if half == 0:
                    o = V(qreg, 0, [[512, 2], [1, 512]])
                else:
                    o = V(qreg, 1024, [[512, 2], [1, 512]])
                i0 = bass.AP(k.ps[bank].tensor, k.ps[bank].offset, [list(k.ps[bank].ap[0]), [0, 2], [1, 512]])
                k.tt('vector', o, i0, gsl, ALU.mult, reads=[bp, bG], writes=[bq])
            for gi in range(4):
                pa, ba = ASL[gi]
                for comp in range(4):
                    k.mm(pa, qreg[:, comp * 512 + gi * 128: comp * 512 + (gi + 1) * 128], W3m(QW[comp]),
                         start=(comp == 0), stop=(comp == 3), reads=[bq, k.KC], writes=[ba], skip_group_check=True)
            for gi in range(4):
                pa, ba = ASL[gi]
                g = bt * 4 + gi
                for a in range(2):
                    o = V(R, (a * 2) * 4096 + g * 4, [[4096, 2], [128, 32], [1, 4]])
                    i0 = bass.AP(pa.tensor, pa.offset, [list(pa.ap[0]), [128, 2], [4, 32], [1, 4]])
                    i1 = V(k.c_Tw2, a * 32, [[0, 2], [1, 32], [0, 4]])
                    k.tt('vector', o, i0, i1, ALU.mult, reads=[ba, k.KC], writes=[bR])
        k.barrier()
        bw = B('hw2')
        col = ct * 128
        k.mset('vector', RAW[:, 0:1], 0.0, writes=[braw])
        k.mset('vector', RAW[:, S + 1:S + 2], 0.0, writes=[braw])
        load_w(k, WSL[0], k.w_in[:, col:col + 128], bw)

        def evac2(ch, pap, bp):
            k.act(RAW[:, 1 + ch * 512: 1 + (ch + 1) * 512], pap, ACT.Copy, reads=[bp], writes=[braw])
        proj_fm(k, WSL[0], 0, 128, evac2, bw, 'hx0')
        short_conv(k, RAW, T1, X0C, ct, braw, bt1, bx0c)
        bmix = B(f'mix{ct}')
        for jb in range(8):
            bank = 4 + jb % 2
            bp = B(f'ps{bank}')
            for jj in range(4):
                j = jb * 4 + jj
                for comp in range(4):
                    k.mm(k.ps[bank][:, jj * 128:(jj + 1) * 128], R[:, comp * 4096 + j * 128: comp * 4096 + (j + 1) * 128],
                         PHIm(RW[comp]), start=(comp == 0), stop=(comp == 3), reads=[bR, k.KC], writes=[bp],
                         skip_group_check=True)
            tmp, btmp = TMP[jb % 2], B(f'htmp{jb % 2}')
            uv = V(U, jb * 4, [[32, 128], [1, 4]])
            pv = bass.AP(k.ps[bank].tensor, k.ps[bank].offset, [list(k.ps[bank].ap[0]), [1, 128], [128, 4]])
            tv = V(tmp, 0, [[4, 128], [1, 4]])
            k.stt('vector', tv, uv, k.c_fbias[:, ct:ct + 1], pv, ALU.mult, ALU.add, reads=[bu, bp, k.KC], writes=[btmp])
            xv = V(X0C, jb * 4, [[32, 128], [1, 4]])
            mv = bass.AP(mixT.tensor, mixT.offset + ct * S + jb * 4, [list(mixT.ap[0]), [32, 128], [1, 4]])
            k.tt('gpsimd', mv, tv, xv, ALU.mult, reads=[btmp, bx0c], writes=[bmix])
        k.barrier()


def phaseA(k):
    B = k.B
    hT, mixT = k.hT, k.mixT
    W = WORK0
    QT = k.bfr(W, S)
    KT = k.bfr(W + 8 * KB, S)
    VA = k.bfr(W + 16 * KB, 32 * 130).rearrange("p (s e) -> p s e", s=32)
    RC = [k.f32r(W + 25 * KB + i * 2 * KB, 512) for i in range(2)]
    RS = [k.f32r(W + 29 * KB + i * 2 * KB, 512) for i in range(2)]
    WSL = [k.bfr(W + 33 * KB + i * 2 * KB, 1024).rearrange("p (c n) -> p c n", c=8) for i in range(5)]
    PT = [k.bfr(W + 43 * KB + i * KB, 512) for i in range(4)]
    TA = [k.f32r(W + 47 * KB + i * 2 * KB, 512) for i in range(4)]
    OSB = [k.f32r(W + 55 * KB + i * 512, 128) for i in range(2)]
    YN = [k.bfr(W + 56 * KB + i * 256, 128) for i in range(2)]
    JK = k.bfr(W + 57 * KB, 128)
    lam = k.c_lam
    bl = B('lam')
    lv = k.c_lamv
    k.stt('vector', JK[:, 0:64], lv[:, 0:64], 1.0, lv[:, 64:128], ALU.mult, ALU.mult, reads=[k.KC], writes=[bl, B('ajk')],
          accum_out=lam[:, 0:1])
    k.stt('vector', JK[:, 0:64], lv[:, 128:192], 1.0, lv[:, 192:256], ALU.mult, ALU.mult, reads=[k.KC], writes=[bl, B('ajk')],
          accum_out=lam[:, 1:2])
    k.act(lam[:, 2:4], lam[:, 0:2], ACT.Exp, reads=[bl], writes=[bl])
    k.tt('vector', lam[:, 4:5], lam[:, 2:3], lam[:, 3:4], ALU.subtract, reads=[bl], writes=[bl])
    k.ts('vector', lam[:, 5:6], lam[:, 4:5], 0.2, -1.0, ALU.add, ALU.mult, reads=[bl], writes=[bl])
    neglam = lam[:, 5:6]
    bQ, bK, bV = B('aQT'), B('aKT'), B('aVA')

    def oreg(r):
        return k.ps[4 + r // 3][:, (r % 3) * 129:(r % 3) * 129 + 129], B(f'ps{4 + r // 3}')

    for hd in getattr(k, 'dev_heads', range(4)):
        srcs = [k.w_in[:, 1536 + hd * 128: 1536 + (hd + 1) * 128], k.w_qkp[:, hd * 128:(hd + 1) * 128],
                k.w_in[:, 2048 + hd * 128: 2048 + (hd + 1) * 128], k.w_qkp[:, 512 + hd * 128: 512 + (hd + 1) * 128],
                k.w_in[:, 2560 + hd * 128: 2560 + (hd + 1) * 128]]
        bws = [B(f'aw{i}') for i in range(5)]
        for i in range(5):
            load_w(k, WSL[i], srcs[i], bws[i])
        k.mset('vector', VA[:, :, 128:129], 1.0, writes=[bV])
        for sb in range(8):
            bank = 6 + sb % 2
            bp = B(f'ps{bank}')
            for s4 in range(4):
                st = sb * 4 + s4
                for c in range(8):
                    k.mm(k.ps[bank][:, s4 * 128:(s4 + 1) * 128], hT[:, c, st * 128:(st + 1) * 128], WSL[4][:, c, :],
                         start=(c == 0), stop=(c == 7), reads=[bws[4], B(f'hT{st}')], writes=[bp], skip_group_check=True)
            pin = k.ps[bank][:, :].rearrange("p (s e) -> p s e", s=4)
            k.act(VA[:, sb * 4:(sb + 1) * 4, 0:128], pin, ACT.Copy, reads=[bp], writes=[bV])
        for ch in range(8):
            s2 = ch % 2
            brc, brs = B(f'arc{s2}'), B(f'ars{s2}')
            k.dma('sync', RC[s2], k.t_ropec[:, ch * 512:(ch + 1) * 512], writes=[brc])
            k.dma('sync', RS[s2], k.t_ropes[:, ch * 512:(ch + 1) * 512], writes=[brs])
            hb = [B(f'hT{i}') for i in range(ch * 4, ch * 4 + 4)]
            for wi in range(4):
                bank = 2 + wi
                for c in range(8):
                    k.mm(k.ps[bank][:, :], WSL[wi][:, c, :], hT[:, c, ch * 512:(ch + 1) * 512],
                         start=(c == 0), stop=(c == 7), reads=[bws[wi]] + hb, writes=[B(f'ps{bank}')])
            for qi, (dst, bdst, sc) in enumerate(((QT, bQ, 0.125), (KT, bK, 1.0))):
                ta, tb = TA[qi * 2], TA[qi * 2 + 1]
                bta, btb = B(f'ata{qi}'), B(f'atb{qi}')
                k.stt('vector', ta, k.ps[2 + qi * 2][:, :], sc, RC[s2], ALU.mult, ALU.mult,
                      reads=[B(f'ps{2 + qi * 2}'), brc], writes=[bta])
                k.stt('vector', tb, k.ps[3 + qi * 2][:, :], sc, RS[s2], ALU.mult, ALU.mult,
                      reads=[B(f'ps{3 + qi * 2}'), brs], writes=[btb])
                k.tt('gpsimd', dst[:, ch * 512:(ch + 1) * 512], ta, tb, ALU.add, reads=[bta, btb], writes=[bdst])
        def qk_exp(qc, st):
            par = st % 2
            for c in range(2):
                bank = par * 2 + c
                k.mm(k.ps[bank][:, :], KT[c * 64:(c + 1) * 64, st * 128:(st + 1) * 128],
                     QT[c * 64:(c + 1) * 64, qc * 512:(qc + 1) * 512], reads=[bK, bQ], writes=[B(f'ps{bank}')])
            for c in range(2):
                bank = par * 2 + c
                k.act(PT[par * 2 + c], k.ps[bank][:, :], ACT.Exp, reads=[B(f'ps{bank}')], writes=[B(f'apt{par * 2 + c}')])

        for qc in range(8):
            qk_exp(qc, 0)
            for st in range(32):
                par = st % 2
                if st + 1 < 32:
                    qk_exp(qc, st + 1)
                for c in range(2):
                    for qs in range(4):
                        r = c * 4 + qs
                        oap, bo = oreg(r)
                        k.mm(oap, PT[par * 2 + c][:, qs * 128:(qs + 1) * 128], VA[:, st, 0:129],
                             start=(st == 0 and r % 3 == 0), stop=(st == 31),
                             reads=[B(f'apt{par * 2 + c}'), bV], writes=[bo], skip_group_check=True)
            for qs in range(4):
                o0, bo0 = oreg(qs)
                o1, bo1 = oreg(4 + qs)
                s2 = qs % 2
                cols = k.c_cols[:, 64 + s2 * 16: 64 + s2 * 16 + 16]
                bc = B(f'acol{s2}')
                osb, bos = OSB[s2], B(f'aosb{s2}')
                yn, byn = YN[s2], B(f'ayn{s2}')
                P_ = k.P
                P_.op('vector', lambda e, a=cols[:, 0:1], b=o0[:, 128:129]: e.reciprocal(out=a, in_=b), [bo0], [bc])
                P_.op('vector', lambda e, a=cols[:, 1:2], b=o1[:, 128:129]: e.reciprocal(out=a, in_=b), [bo1], [bc])
                k.tt('vector', cols[:, 2:3], cols[:, 1:2], neglam, ALU.mult, reads=[bc, bl], writes=[bc])
                k.ts('vector', osb, o0[:, 0:128], cols[:, 0:1], None, ALU.mult, reads=[bo0, bc], writes=[bos])
                k.stt('vector', osb, o1[:, 0:128], cols[:, 2:3], osb, ALU.mult, ALU.add, reads=[bo1, bc, bos], writes=[bos])
                k.stt('vector', JK, osb, 1.0, osb, ALU.mult, ALU.mult, reads=[bos], writes=[bc, B('ajk')], accum_out=cols[:, 3:4])
                rstd_from_ss(k, cols[:, 3:4], cols[:, 4:5], cols[:, 5:6], cols[:, 6:7], 128.0, 1e-5, [bc])
                k.ts('vector', yn, osb, cols[:, 6:7], 0.8, ALU.mult, ALU.mult, reads=[bos, bc], writes=[byn])
                k.tr(k.psb[7][:, 0:128], yn, k.c_ident_bf, reads=[byn, k.KC], writes=[B('ps7')])
                q0 = qc * 512 + qs * 128
                k.act(mixT[:, 4 + hd, q0:q0 + 128], k.psb[7][:, 0:128], ACT.Identity, reads=[B('ps7'), k.KC],
                      writes=[B(f'mix{4 + hd}')], scale=k.c_subg[:, 0:1])


def load_mlp_w(k):
    B = k.B
    WUP = k.bfr(HT0, 8 * 4096).rearrange("p (c n) -> p c n", c=8)
    k.WUP = WUP
    for c in range(8):
        k.dma('gpsimd', WUP[:, c, :], k.w_up[c * 128:(c + 1) * 128, :], writes=[B('wup')])


def phaseO(k):
    B = k.B
    mixT = k.mixT
    W = WORK0
    load_mlp_w(k)
    WO = k.bfr(W, 8 * 1024).rearrange("p (c n) -> p c n", c=8)
    GP = k.f32r(W + 16 * KB, 1024)
    XT = [k.f32r(W + 20 * KB + i * 4 * KB, 1024) for i in range(3)]
    MS = [k.f32r(W + 32 * KB + i * 4 * KB, 1024) for i in range(2)]
    TMP = [k.f32r(W + 40 * KB + i * 4 * KB, 1024) for i in range(2)]
    JK = k.bfr(W + 48 * KB, 1024)
    bwo, bgp = B('owo'), B('ogp')
    for half in range(2):
        k.dma('gpsimd', WO[:, :, half * 512:(half + 1) * 512],
              k.w_out[:, half * 512:(half + 1) * 512].rearrange("(c p) n -> p c n", p=128), writes=[bwo])
    k.dma('sync', GP, bass.AP(k.gpost.tensor, 0, [[0, 128], [1, 1024]]), writes=[bgp])
    mixb = [B(f'mix{c}') for c in range(8)]
    for i in range(32):
        s3, s2 = i % 3, i % 2
        xt, bx = XT[s3], B(f'ox{s3}')
        ms, bms = MS[s2], B(f'oms{s2}')
        tmp, btmp = TMP[s2], B(f'otmp{s2}')
        cols = k.c_cols[:, 96 + s2 * 8: 96 + s2 * 8 + 8]
        bc = B(f'ocol{s2}')
        k.dma('sync', xt, k.x[i * 128:(i + 1) * 128, :], writes=[bx])
        for half in range(2):
            bank = 2 * s2 + half
            bp = B(f'ps{bank}')
            for c in range(8):
                k.mm(k.ps[bank][:, :], mixT[:, c, i * 128:(i + 1) * 128], WO[:, c, half * 512:(half + 1) * 512],
                     start=(c == 0), stop=(c == 7), reads=[bwo] + mixb, writes=[bp])
            k.act(ms[:, half * 512:(half + 1) * 512], k.ps[bank][:, :], ACT.Copy, reads=[bp], writes=[bms])
        k.stt('vector', JK, ms, 1.0, ms, ALU.mult, ALU.mult, reads=[bms], writes=[bc, B('ojk')], accum_out=cols[:, 0:1])
        rstd_from_ss(k, cols[:, 0:1], cols[:, 1:2], cols[:, 2:3], cols[:, 3:4], 1024.0, 1e-6, [bc])
        k.stt('vector', tmp, ms, cols[:, 3:4], GP, ALU.mult, ALU.mult, reads=[bms, bc, bgp], writes=[btmp])
        k.tt('gpsimd', tmp, tmp, xt, ALU.add, reads=[btmp, bx], writes=[btmp])
        k.dma('sync', k.xa_scr[i * 128:(i + 1) * 128, :], tmp, reads=[btmp], writes=[B(f'xa{i}')], sembuf=btmp)


def phaseM(k):
    B = k.B
    WUP = k.WUP
    WDN = k.bfr(MIX0, 32 * 1024).rearrange("p (f n) -> p f n", f=32)
    for f0 in range(0, 32, 4):
        k.dma('gpsimd', WDN[:, f0:f0 + 4, :],
              k.w_down[f0 * 128:(f0 + 4) * 128, :].rearrange("(f p) n -> p f n", p=128), writes=[B('wdn')])
    W = WORK0
    GP = k.f32r(W, 1024)
    XA = [[k.f32r(W + 4 * KB + (a * 2 + b) * 4 * KB, 1024) for b in range(2)] for a in range(2)]
    XN = [k.bfr(W + 20 * KB + i * 2 * KB, 1024) for i in range(2)]
    H2T = [k.bfr(W + 24 * KB + i * 4 * KB, 8 * 256).rearrange("p (c t) -> p c t", c=8) for i in range(2)]
    UPT = k.bfr(W + 32 * KB, 32 * 256).rearrange("p (f t) -> p f t", f=32)
    RL = [k.f32r(W + 48 * KB + i * KB, 256) for i in range(2)]
    MS = [k.f32r(W + 50 * KB + i * 4 * KB, 1024) for i in range(2)]
    JK = k.bfr(W + 58 * KB, 1024)
    bgp = B('mgp')
    k.dma('sync', GP, bass.AP(k.gpost.tensor, 1024, [[0, 128], [1, 1024]]), writes=[bgp])
    gb = V(k.c_gcols, 8, [[1, 8], [0, 128]])
    bwup, bwdn, bupt = B('wup'), B('wdn'), B('mupt')
    k.out_ticks = []
    nck = getattr(k, 'dev_nck', 16)

    def prologue(ck):
        a = ck % 2
        bh2 = B(f'mh2{a}')
        for tl in range(2):
            i = ck * 2 + tl
            xa, bxa = XA[a][tl], B(f'mxa{a}{tl}')
            xn, bxn = XN[tl], B(f'mxn{tl}')
            cols = k.c_cols[:, 128 + tl * 8: 128 + tl * 8 + 8]
            bc = B(f'mcol{tl}')
            k.dma('sync', xa, k.xa_scr[i * 128:(i + 1) * 128, :], reads=[B(f'xa{i}')], writes=[bxa])
            k.stt('vector', JK, xa, 1.0, xa, ALU.mult, ALU.mult, reads=[bxa], writes=[bc, B('mjk')], accum_out=cols[:, 0:1])
            rstd_from_ss(k, cols[:, 0:1], cols[:, 1:2], cols[:, 2:3], cols[:, 3:4], 1024.0, 1e-6, [bc])
            k.act(xn, xa, ACT.Identity, reads=[bxa, bc], writes=[bxn], scale=cols[:, 3:4])
            bank = tl
            bp = B(f'ps{bank}')
            for c in range(8):
                k.tr(k.psb[bank][:, c * 128:(c + 1) * 128], xn[:, c * 128:(c + 1) * 128], k.c_ident_bf,
                     reads=[bxn, k.KC], writes=[bp])
            pin = k.psb[bank][:, 0:1024].rearrange("p (c t) -> p c t", c=8)
            k.tt('vector', H2T[a][:, :, tl * 128:(tl + 1) * 128], pin, gb, ALU.mult, reads=[bp, k.KC], writes=[bh2])

    prologue(0)
    for ck in range(nck):
        a = ck % 2
        bh2 = B(f'mh2{a}')
        for f in range(32):
            bank = 2 + f % 2
            bp = B(f'ps{bank}')
            rl, brl = RL[f % 2], B(f'mrl{f % 2}')
            for c in range(8):
                k.mm(k.ps[bank][:, 0:256], WUP[:, c, f * 128:(f + 1) * 128], H2T[a][:, c, :],
                     start=(c == 0), stop=(c == 7), reads=[bwup, bh2], writes=[bp])
            k.act(rl, k.ps[bank][:, 0:256], ACT.Relu, reads=[bp], writes=[brl])
            k.tt('gpsimd', UPT[:, f, :], rl, rl, ALU.mult, reads=[brl], writes=[bupt])
        if ck + 1 < nck:
            prologue(ck + 1)
        for tl in range(2):
            i = ck * 2 + tl
            xa, bxa = XA[a][tl], B(f'mxa{a}{tl}')
            ms, bms = MS[tl], B(f'mms{tl}')
            cols = k.c_cols[:, 144 + tl * 8: 144 + tl * 8 + 8]
            bc = B(f'mcol2{tl}')
            for half in range(2):
                bank = 4 + tl * 2 + half
                bp = B(f'ps{bank}')
                for f in range(32):
                    k.mm(k.ps[bank][:, :], UPT[:, f, tl * 128:(tl + 1) * 128], WDN[:, f, half * 512:(half + 1) * 512],
                         start=(f == 0), stop=(f == 31), reads=[bupt, bwdn], writes=[bp])
                k.act(ms[:, half * 512:(half + 1) * 512], k.ps[bank][:, :], ACT.Copy, reads=[bp], writes=[bms])
            k.stt('vector', JK, ms, 1.0, ms, ALU.mult, ALU.mult, reads=[bms], writes=[bc, B('mjk')], accum_out=cols[:, 0:1])
            rstd_from_ss(k, cols[:, 0:1], cols[:, 1:2], cols[:, 2:3], cols[:, 3:4], 1024.0, 1e-6, [bc])
            k.stt('vector', ms, ms, cols[:, 3:4], GP, ALU.mult, ALU.mult, reads=[bms, bc, bgp], writes=[bms])
            k.tt('gpsimd', ms, ms, xa, ALU.add, reads=[bms, bxa], writes=[bms])
            k.out_ticks.append(k.dma('sync', k.out[i * 128:(i + 1) * 128, :], ms, reads=[bms], sembuf=bms))


def host_inputs(inputs):
    T = host_tables()

    def g(k):
        return np.asarray(inputs[k], dtype=np.float32)
    w_in = np.ascontiguousarray(g('w_in')[0])
    qk = w_in[:, 1536:2560]
    idx = np.arange(1024)
    d = idx % 64
    perm = (idx - d) + (d + 32) % 64
    sh = {}
    sh['w_in'] = w_in
    sh['w_qkp'] = np.ascontiguousarray(qk[:, perm])
    cwb = np.concatenate([g('conv_w')[0], g('conv_b')[0][None, :]], axis=0)
    sh['cw'] = np.ascontiguousarray(cwb.reshape(4, 12, 128).transpose(2, 1, 0).reshape(128, 48))
    sh['fw1'] = np.ascontiguousarray(g('filt_w1')[0])
    sh['fw2'] = np.ascontiguousarray(g('filt_w2')[0])
    sh['fw3'] = np.ascontiguousarray(g('filt_w3')[0])
    sh['fw4'] = np.ascontiguousarray(g('filt_w4')[0])
    sh['fcol'] = np.ascontiguousarray(np.stack([g('filt_b1')[0], g('filt_b2')[0], g('filt_b3')[0], g('filt_freq')[0]], axis=1))
    sh['fbias'] = np.ascontiguousarray(g('filt_bias')[0].reshape(4, 128).T)
    sh['lamv'] = np.concatenate([g('lam_q1')[0], g('lam_k1')[0], g('lam_q2')[0], g('lam_k2')[0]])[None, :].copy()
    sh['subg'] = np.ascontiguousarray(g('subln_gain')[0][:, None])
    sh['w_out'] = np.ascontiguousarray(g('w_out')[0])
    sh['w_up'] = np.ascontiguousarray(g('w_up')[0])
    sh['w_down'] = np.ascontiguousarray(g('w_down')[0])
    sh['gcols'] = np.ascontiguousarray(np.stack([g('attn_pre_gain')[0].reshape(8, 128).T,
                                                 g('mlp_pre_gain')[0].reshape(8, 128).T], axis=1).reshape(128, 16))
    sh['gpost'] = np.ascontiguousarray(np.stack([g('attn_post_gain')[0], g('mlp_post_gain')[0]], axis=0))
    sh['ident_bf'] = T['ident_bf']
    sh['ident_f'] = T['ident_f']
    sh['F1'] = T['F1']
    sh['Tw1'] = T['Tw1'].reshape(128, 256)
    sh['CS4'] = T['CS4'].reshape(128, 512)
    sh['W3'] = T['W3'].reshape(128, 768)
    sh['Tw2'] = T['Tw2'].reshape(128, 64)
    sh['PHI'] = T['PHI'].reshape(128, 384)
    sh['zT'] = T['zT']
    sh['negt'] = T['negt']
    sh['absdelta'] = T['absdelta']
    sh['ropec'] = T['ropec']
    sh['ropes'] = T['ropes']
    return sh


PHASES = ['p0', 'pF', 'pH', 'pA', 'pO', 'pM']


def make_program(upto='pM', dumps=(), dump_fn=None, small_out=False, **attrs):
    k = K(dumps=dumps, small_out=small_out)
    k.__dict__.update(attrs)
    fns = {'p0': phase0, 'pF': phaseF, 'pH': phaseH, 'pA': phaseA, 'pO': phaseO, 'pM': phaseM}
    for ph in PHASES:
        fns[ph](k)
        k.barrier()
        if ph == upto:
            break
    ticks = []
    if dump_fn is not None:
        ticks += dump_fn(k)
    ticks += getattr(k, 'out_ticks', [])
    k.P.wait_all('sync', ticks)
    with k.nc.Block() as block:
        k.P.emit(block)
    k.es.close()
    return k.nc


def kernel(**inputs):
    sh = host_inputs(inputs)
    x = np.asarray(inputs['x'], dtype=np.float32)
    nc = make_program()
    in_maps = []
    for c in range(8):
        m = dict(sh)
        m['x'] = np.ascontiguousarray(x[c])
        in_maps.append(m)
    res = run_bass_kernel_spmd(nc, in_maps, core_ids=list(range(8)))
    return np.stack([np.asarray(r['out'], dtype=np.float32) for r in res.results], axis=0)
```

```python
import numpy as np
import concourse.bass as bass
import concourse.mybir as mybir
from concourse.bass_utils import run_bass_kernel_spmd
from contextlib import ExitStack

F32 = mybir.dt.float32
BF16 = mybir.dt.bfloat16
ALU = mybir.AluOpType
ACT = mybir.ActivationFunctionType
AX = mybir.AxisListType

ENGS = ('tensor', 'vector', 'scalar', 'gpsimd', 'sync')
SEM_ROT = 16000


class Buf:
    __slots__ = ('name', 'w', 'r', 'dsem', 'excl')

    def __init__(self, name, excl=False):
        self.name = name
        self.excl = excl
        self.w = None
        self.r = {}
        self.dsem = None


class Prog:
    def __init__(self, nc, sems):
        self.nc = nc
        self.sems = sems
        self.nsem = 0
        self.q = {e: [] for e in ENGS}
        self.cnt = {e: 0 for e in ENGS}
        self.esem = {}
        self.owned = {e: set() for e in ENGS}
        self.seen = {e: {} for e in ENGS}
        self.dcnt = {}
        self.final = {}
        for e in ENGS:
            self._new_esem(e)

    def alloc_sem(self):
        i = self.nsem
        self.nsem += 1
        assert i < len(self.sems), "out of semaphores"
        return i

    def _new_esem(self, e):
        i = self.alloc_sem()
        self.esem[e] = i
        self.owned[e].add(i)
        self.cnt[e] = 0

    def _deps(self, eng, reads, writes):
        need = {}

        def add(t, raw=False):
            if t is None:
                return
            s, v = t
            if s in self.owned[eng] and not (raw and eng != 'tensor'):
                return
            if need.get(s, 0) < v:
                need[s] = v
        for b in reads:
            add(b.w, raw=True)
        for b in writes:
            add(b.w)
            for s, v in b.r.items():
                add((s, v))
        out = []
        seen = self.seen[eng]
        for s, v in need.items():
            if seen.get(s, 0) < v:
                seen[s] = v
                out.append((s, v))
        return out

    def op(self, eng, fn, reads=(), writes=()):
        if any(b.excl for b in reads):
            writes = list(writes) + [b for b in reads if b.excl]
            reads = [b for b in reads if not b.excl]
        waits = self._deps(eng, reads, writes)
        if self.cnt[eng] >= SEM_ROT:
            self._new_esem(eng)
        self.cnt[eng] += 1
        s, v = self.esem[eng], self.cnt[eng]
        for b in reads:
            if b.r.get(s, 0) < v:
                b.r[s] = v
        for b in writes:
            b.w = (s, v)
            b.r = {}
        self.q[eng].append((waits, fn, s, 1))

    def dma(self, qeng, fn, reads=(), writes=(), sembuf=None):
        waits = self._deps(qeng, reads, writes)
        sb = sembuf if sembuf is not None else (writes[0] if writes else reads[0])
        if sb.dsem is None or self.dcnt[sb.dsem] >= SEM_ROT:
            sb.dsem = self.alloc_sem()
            self.dcnt[sb.dsem] = 0
        s = sb.dsem
        self.dcnt[s] += 16
        v = self.dcnt[s]
        for b in reads:
            if b.r.get(s, 0) < v:
                b.r[s] = v
        for b in writes:
            b.w = (s, v)
            b.r = {}
        self.q[qeng].append((waits, fn, s, 16))
        return (s, v)

    def wait_all(self, eng, ticks):
        need = {}
        for s, v in ticks:
            if need.get(s, 0) < v:
                need[s] = v
        self.q[eng].append((list(need.items()), None, None, 0))

    def barrier(self):
        ticks = [(self.esem[o], self.cnt[o]) for o in ENGS if self.cnt[o] > 0]
        ticks += [(s, v) for s, v in self.dcnt.items()]
        for e in ENGS:
            need = []
            for s, v in ticks:
                if s in self.owned[e]:
                    continue
                if self.seen[e].get(s, 0) < v:
                    self.seen[e][s] = v
                    need.append((s, v))
            if need:
                self.q[e].append((need, None, None, 0))

    def replay(self, engobj, eng):
        sems = self.sems
        for waits, fn, s, inc in self.q[eng]:
            for ws, wv in waits:
                engobj.wait_ge(sems[ws], wv)
            if fn is not None:
                inst = fn(engobj)
                inst.then_inc(sems[s], inc)

    def emit(self, block):
        @block.tensor
        def _(e):
            self.replay(e, 'tensor')

        @block.vector
        def _(e):
            self.replay(e, 'vector')

        @block.scalar
        def _(e):
            self.replay(e, 'scalar')

        @block.gpsimd
        def _(e):
            self.replay(e, 'gpsimd')

        @block.sync
        def _(e):
            self.replay(e, 'sync')


def AP(base, off, dims):
    return bass.AP(base.tensor, off, [list(d) for d in dims])


S = 4096
D = 1024
NFFT = 8192
PI = float(np.pi)


def host_tables():
    import ml_dtypes
    bf = ml_dtypes.bfloat16
    T = {}
    T['ident_bf'] = np.eye(128, dtype=np.float32).astype(bf)
    T['ident_f'] = np.eye(128, dtype=np.float32)
    p = np.arange(128, dtype=np.float64)
    k1 = np.arange(128, dtype=np.float64)
    j = np.arange(32, dtype=np.float64)
    k2 = np.arange(32, dtype=np.float64)
    ang = 2 * np.pi * (k1[None, :] + 0.5) * p[:, None] / 256.0
    T['F1'] = np.concatenate([np.cos(ang), -np.sin(ang)], axis=1).astype(np.float32).astype(bf)
    jj = np.repeat(j, 4)
    a1 = 2 * np.pi * (k1[None, :] + 0.5) * jj[:, None] / NFFT
    T['Tw1'] = np.stack([np.cos(a1), -np.sin(a1)], axis=1).astype(np.float32)
    th = 2 * np.pi * np.outer(j, k2) / 32.0
    I4 = np.eye(4)
    C = np.kron(np.cos(th), I4)
    Sm = np.kron(np.sin(th), I4)
    T['CS4'] = np.stack([C, -C, Sm, -Sm], axis=1).astype(np.float32).astype(bf)
    W1 = np.concatenate([C, Sm], axis=1)
    W2 = np.concatenate([-C, -Sm], axis=1)
    W34 = np.concatenate([-Sm, C], axis=1)
    T['W3'] = np.stack([W1, W2, W34], axis=1).astype(np.float32).astype(bf)
    a2 = 2 * np.pi * (k1[:, None] + 0.5) * j[None, :] / NFFT
    T['Tw2'] = np.stack([np.cos(a2), -np.sin(a2)], axis=1).astype(np.float32)
    ph = 2 * np.pi * (k1[:, None] + 0.5) * p[None, :] / 256.0
    cphi = (2.0 / NFFT) * np.cos(ph)
    sphi = (2.0 / NFFT) * np.sin(ph)
    T['PHI'] = np.stack([cphi, -sphi, sphi], axis=1).astype(np.float32).astype(bf)
    bands = 16
    t = np.linspace(0.0, 1.0, S, dtype=np.float32)[:, None]
    w = (np.float32(2.0 * np.pi / S) * np.arange(S, dtype=np.float32))[:, None]
    f = np.linspace(1e-4, bands - 1, bands, dtype=np.float32)[None, :]
    fw_ = (f * w).astype(np.float32)
    z = np.concatenate([t, np.cos(fw_), -np.sin(fw_)], axis=-1).astype(np.float32)
    T['zT'] = np.ascontiguousarray(z.T)
    tt = np.linspace(0.0, 1.0, S, dtype=np.float32).reshape(128, 32)
    T['negt'] = (-tt).astype(np.float32)
    import math
    max_decay = math.log(1e-2) / 0.3
    min_decay = math.log(1e-2) / 1.5
    deltas = np.abs(np.linspace(min_decay, max_decay, 512, dtype=np.float32))
    T['absdelta'] = np.ascontiguousarray(np.broadcast_to(deltas[None, :], (128, 512))).astype(np.float32)
    inv = (10000.0 ** (-np.arange(0, 64, 2, dtype=np.float32) / 64)).astype(np.float32)
    angr = (np.arange(S, dtype=np.float32)[:, None] * inv[None, :]).astype(np.float32)
    angr = np.concatenate([angr, angr], axis=-1)
    cosr = np.cos(angr).astype(np.float32)
    sinr = np.sin(angr).astype(np.float32)
    sgn = np.concatenate([-np.ones(32, np.float32), np.ones(32, np.float32)])
    sins = sinr * sgn[None, :]
    T['ropec'] = np.ascontiguousarray(np.concatenate([cosr.T, cosr.T], axis=0))
    T['ropes'] = np.ascontiguousarray(np.concatenate([sins.T, sins.T], axis=0))
    return T


KB = 1024
CONST0 = 0
HT0 = 14 * KB
MIX0 = 78 * KB
WORK0 = 142 * KB
HW0 = 118 * KB
ARENA_F32 = 53100


class Ctx:
    pass


def V(region, off, dims, p0=0, np_=128):
    pstep = region.ap[0][0]
    return bass.AP(region.tensor, region.offset + p0 * pstep + off,
                   [[pstep, np_]] + [list(d) for d in dims])


class K:
  def __init__(self, dumps=(), small_out=False):
    nc = bass.Bass("TRN2", target_bir_lowering=False)

    def din(name, shape, dt=F32):
        return nc.dram_tensor(name, list(shape), dt, kind="ExternalInput").ap()

    x = din('x', [S, D])
    w_in = din('w_in', [D, 3072])
    w_qkp = din('w_qkp', [D, 1024])
    cw = din('cw', [128, 48])
    fw1 = din('fw1', [33, 64])
    fw2 = din('fw2', [64, 64])
    fw3 = din('fw3', [64, 64])
    fw4 = din('fw4', [64, 1024])
    fcol = din('fcol', [64, 4])
    fbias = din('fbias', [128, 4])
    lamv = din('lamv', [1, 256])
    subg = din('subg', [128, 1])
    w_out = din('w_out', [D, D])
    w_up = din('w_up', [D, 4096])
    w_down = din('w_down', [4096, D])
    gcols = din('gcols', [128, 16])
    gpost = din('gpost', [2, 1024])
    t_ident_bf = din('ident_bf', [128, 128], BF16)
    t_ident_f = din('ident_f', [128, 128])
    t_F1 = din('F1', [128, 256], BF16)
    t_Tw1 = din('Tw1', [128, 256])
    t_CS4 = din('CS4', [128, 512], BF16)
    t_W3 = din('W3', [128, 768], BF16)
    t_Tw2 = din('Tw2', [128, 64])
    t_PHI = din('PHI', [128, 384], BF16)
    t_zT = din('zT', [33, S])
    t_negt = din('negt', [128, 32])
    t_absd = din('absdelta', [128, 512])
    t_ropec = din('ropec', [128, S])
    t_ropes = din('ropes', [128, S])
    out = nc.dram_tensor('out', [128 if small_out else S, D], F32, kind='ExternalOutput').ap()
    xa_scr = nc.dram_tensor('xa_scr', [S, D], F32, kind='Internal').ap()
    dump_aps = {}
    for (nm, shape, dt) in dumps:
        dump_aps[nm] = nc.dram_tensor('dbg_' + nm, list(shape), dt, kind='ExternalOutput').ap()

    es = ExitStack()
    sems = [es.enter_context(nc.semaphore(f"s{i}")) for i in range(88)]
    arena_t = es.enter_context(nc.sbuf_tensor("arena", [128, ARENA_F32], F32))
    pst = [es.enter_context(nc.psum_tensor(f"ps{i}", [128, 512], F32)) for i in range(8)]
    ps = [t[:] for t in pst]
    psb = [t[:].bitcast(BF16) for t in pst]
    AF = arena_t[:]
    AB = arena_t[:].bitcast(BF16)

    def f32r(boff, n):
        assert boff % 4 == 0 and boff + 4 * n <= ARENA_F32 * 4, (boff, n)
        return AF[:, boff // 4: boff // 4 + n]

    def bfr(boff, n):
        assert boff % 4 == 0 and boff + 2 * n <= ARENA_F32 * 4, (boff, n)
        return AB[:, boff // 2: boff // 2 + n]

    P = Prog(nc, sems)
    bufs = {}

    def B(name):
        b = bufs.get(name)
        if b is None:
            b = Buf(name, excl=name.startswith('ps'))
            bufs[name] = b
        return b

    def mm(out, lhsT, rhs, start=True, stop=True, reads=(), writes=(), **kw):
        P.op('tensor', lambda e: e.matmul(out, lhsT=lhsT, rhs=rhs, start=start, stop=stop, **kw), reads, writes)

    def tr(out, in_, ident, reads=(), writes=()):
        P.op('tensor', lambda e: e.transpose(out=out, in_=in_, identity=ident), reads, writes)

    def act(out, in_, func, reads=(), writes=(), **kw):
        P.op('scalar', lambda e: e.activation(out=out, in_=in_, func=func, **kw), reads, writes)

    def tt(eng, out, in0, in1, op, reads=(), writes=()):
        P.op(eng, lambda e: e.tensor_tensor(out=out, in0=in0, in1=in1, op=op), reads, writes)

    def ts(eng, out, in0, s1, s2, op0, op1=None, reads=(), writes=(), **kw):
        if op1 is None:
            P.op(eng, lambda e: e.tensor_scalar(out=out, in0=in0, scalar1=s1, scalar2=None, op0=op0, **kw), reads, writes)
        else:
            P.op(eng, lambda e: e.tensor_scalar(out=out, in0=in0, scalar1=s1, scalar2=s2, op0=op0, op1=op1, **kw), reads, writes)

    def stt(eng, out, in0, scalar, in1, op0, op1, reads=(), writes=(), **kw):
        P.op(eng, lambda e: e.scalar_tensor_tensor(out=out, in0=in0, scalar=scalar, in1=in1, op0=op0, op1=op1, **kw), reads, writes)

    def cp(eng, out, in_, reads=(), writes=()):
        P.op(eng, lambda e: e.tensor_copy(out=out, in_=in_), reads, writes)

    def mset(eng, ap, val, writes=()):
        P.op(eng, lambda e: e.memset(ap, val), (), writes)

    def dma(q, out, in_, reads=(), writes=(), sembuf=None):
        return P.dma(q, lambda e: e.dma_start(out=out, in_=in_), reads, writes, sembuf)

    def barrier():
        P.barrier()

    cpos = [CONST0]

    def calloc(nbytes):
        o = cpos[0]
        cpos[0] += (nbytes + 63) // 64 * 64
        assert cpos[0] <= HT0, cpos[0]
        return o

    c_ident_bf = bfr(calloc(256), 128)
    c_ident_f = f32r(calloc(512), 128)
    c_F1 = bfr(calloc(512), 256)
    c_Tw1 = f32r(calloc(1024), 256)
    c_CS4 = bfr(calloc(1024), 512)
    c_W3 = bfr(calloc(1536), 768)
    c_Tw2 = f32r(calloc(256), 64)
    c_PHI = bfr(calloc(768), 384)
    c_gcols = f32r(calloc(64), 16)
    c_cw = f32r(calloc(192), 48)
    c_fbias = f32r(calloc(16), 4)
    c_subg = f32r(calloc(4), 1)
    c_lamv = f32r(calloc(1024), 256)
    c_negt = f32r(calloc(128), 32)
    c_absd = f32r(calloc(2048), 512)
    c_fw1 = f32r(calloc(256), 64)
    c_fw2 = f32r(calloc(256), 64)
    c_fw3 = f32r(calloc(256), 64)
    c_fcol = f32r(calloc(16), 4)
    c_fw4 = bfr(calloc(2048), 1024)
    c_cols = f32r(calloc(1024), 256)
    c_lam = f32r(calloc(64), 16)
    KC = B('consts')
    for dst, src in [(c_ident_f, t_ident_f), (c_Tw1, t_Tw1), (c_Tw2, t_Tw2), (c_gcols, gcols), (c_cw, cw),
                     (c_fbias, fbias), (c_subg, subg), (c_negt, t_negt), (c_absd, t_absd), (c_fcol[0:64, :], fcol),
                     (c_fw1[0:33, :], fw1), (c_fw2[0:64, :], fw2), (c_fw3[0:64, :], fw3),
                     (c_ident_bf, t_ident_bf), (c_F1, t_F1), (c_CS4, t_CS4), (c_W3, t_W3), (c_PHI, t_PHI)]:
        dma('sync', dst, src, writes=[KC])
    dma('sync', c_lamv, bass.AP(lamv.tensor, 0, [[0, 128], [1, 256]]), writes=[KC])
    KCG = B('consts_g')
    dma('gpsimd', c_fw4[0:64, :], fw4, writes=[KCG])

    d = dict(locals())
    d.pop('self')
    self.__dict__.update(d)


def rstd_from_ss(k, ss, ms, ln, rstd, n, eps, bufs):
    k.ts('vector', ms, ss, 1.0 / n, eps, ALU.mult, ALU.add, reads=bufs, writes=bufs)
    k.act(ln, ms, ACT.Ln, reads=bufs, writes=bufs)
    k.act(rstd, ln, ACT.Exp, reads=bufs, writes=bufs, scale=-0.5)


def phase0(k):
    B = k.B
    hT = k.bfr(HT0, 8 * S).rearrange("p (c t) -> p c t", c=8)
    k.hT = hT
    XT = [k.f32r(WORK0 + i * 4096, 1024) for i in range(3)]
    XN = [k.bfr(WORK0 + 12288 + i * 2048, 1024) for i in range(2)]
    JK = k.bfr(WORK0 + 16384, 1024)
    gb = V(k.c_gcols, 0, [[1, 8], [0, 128]])
    nt = getattr(k, 'dev_ntiles', 32)

    def stage_a(i):
        s3, s2 = i % 3, i % 2
        xt, xn = XT[s3], XN[s2]
        bx, bn, bc = B(f'p0x{s3}'), B(f'p0n{s2}'), B(f'p0c{s2}')
        cols = k.c_cols[:, s2 * 8: s2 * 8 + 8]
        ss, ms, ln, rs = cols[:, 0:1], cols[:, 1:2], cols[:, 2:3], cols[:, 3:4]
        k.dma('sync', xt, k.x[i * 128:(i + 1) * 128, :], writes=[bx])
        k.stt('vector', JK, xt, 1.0, xt, ALU.mult, ALU.mult, reads=[bx], writes=[B('p0jk'), bc], accum_out=ss)
        rstd_from_ss(k, ss, ms, ln, rs, 1024.0, 1e-6, [bc])
        k.act(xn, xt, ACT.Identity, reads=[bx, bc], writes=[bn], scale=rs)

    def stage_b(i):
        s2 = i % 2
        xn, bn = XN[s2], B(f'p0n{s2}')
        bank = s2
        bp = B(f'ps{bank}')
        for c in range(8):
            k.tr(k.psb[bank][:, c * 128:(c + 1) * 128], xn[:, c * 128:(c + 1) * 128], k.c_ident_bf,
                 reads=[bn, k.KC], writes=[bp])
        pin = k.psb[bank][:, 0:1024].rearrange("p (c t) -> p c t", c=8)
        k.tt('vector', hT[:, :, i * 128:(i + 1) * 128], pin, gb, ALU.mult, reads=[bp, k.KC], writes=[B(f'hT{i}')])

    if nt > 0:
        stage_a(0)
    for i in range(nt):
        if i + 1 < nt:
            stage_a(i + 1)
        stage_b(i)


H3T0 = MIX0 + 32 * KB


def phaseF(k):
    B = k.B
    ZT = k.f32r(HW0, S)
    ARG = k.f32r(HW0 + 16 * KB, S)
    WT = k.f32r(HW0 + 32 * KB, S)
    HA = k.f32r(HW0 + 48 * KB, S)
    HB = k.f32r(HW0 + 64 * KB, S)
    h3T = k.bfr(H3T0, S)
    k.h3T = h3T
    bz, ba, bw = B('fz'), B('farg'), B('fwt')
    k.dma('sync', ZT[0:33, :], k.t_zT, writes=[bz])
    freq = k.c_fcol[0:64, 3:4]
    layers = [(k.c_fw1[0:33, :], ZT[0:33, :], HA, bz, B('fha')),
              (k.c_fw2[0:64, :], HA[0:64, :], HB, B('fha'), B('fhb')),
              (k.c_fw3[0:64, :], HB[0:64, :], h3T, B('fhb'), B('h3T'))]
    for L, (w, inp, outp, bin_, bout) in enumerate(layers):
        fb = k.c_cols[0:64, 32 + L: 33 + L]
        bfb = B(f'ffb{L}')
        k.tt('vector', fb, k.c_fcol[0:64, L:L + 1], freq, ALU.mult, reads=[k.KC], writes=[bfb])
        for ch in range(8):
            bank = ch % 2
            bp = B(f'ps{bank}')
            k.mm(k.ps[bank][0:64, :], w, inp[:, ch * 512:(ch + 1) * 512], reads=[k.KC, bin_], writes=[bp])
            k.ts('vector', ARG[0:64, ch * 512:(ch + 1) * 512], k.ps[bank][0:64, :], freq, fb, ALU.mult, ALU.add,
                 reads=[bp, k.KC, bfb], writes=[ba])
        a = ARG[0:64, :]
        wt = WT[0:64, :]
        k.ts('vector', wt, a, PI, -2 * PI, ALU.is_gt, ALU.mult, reads=[ba], writes=[bw])
        k.tt('vector', a, a, wt, ALU.add, reads=[ba, bw], writes=[ba])
        k.ts('vector', wt, a, -PI, 2 * PI, ALU.is_lt, ALU.mult, reads=[ba], writes=[bw])
        k.tt('vector', a, a, wt, ALU.add, reads=[ba, bw], writes=[ba])
        k.act(outp[0:64, :], a, ACT.Sin, reads=[ba], writes=[bout])


def load_w(k, dst3, src_cols, bw):
    k.dma('gpsimd', dst3, src_cols.rearrange("(c p) n -> p c n", p=128), writes=[bw])


def proj_fm(k, wt3, col0, ncols, dst_fn, bw, tag, banks=(6, 7)):
    B = k.B
    for ch in range(8):
        bank = banks[ch % len(banks)]
        bp = B(f'ps{bank}')
        hb = [B(f'hT{i}') for i in range(ch * 4, ch * 4 + 4)]
        for c in range(8):
            k.mm(k.ps[bank][:, :], wt3[:, c, col0:col0 + ncols], k.hT[:, c, ch * 512:(ch + 1) * 512],
                 start=(c == 0), stop=(c == 7), reads=[bw] + hb, writes=[bp])
        dst_fn(ch, k.ps[bank][:, :], bp)


def short_conv(k, raw, t1, dst, ti, braw, bt1, bdst, eng='vector'):
    w0, w1, w2, bb = [k.c_cw[:, ti * 4 + i: ti * 4 + i + 1] for i in range(4)]
    k.act(t1, raw[:, 1:S + 1], ACT.Identity, reads=[braw, k.KC], writes=[bt1], scale=w1, bias=bb)
    k.stt(eng, t1, raw[:, 0:S], w0, t1, ALU.mult, ALU.add, reads=[braw, bt1, k.KC], writes=[bt1])
    k.stt(eng, dst, raw[:, 2:S + 2], w2, t1, ALU.mult, ALU.add, reads=[braw, bt1, k.KC], writes=[bdst])


def fft_fwd_batch(k, d0_groups, pslots, a_slots, tag):
    P_reg, bP = pslots
    for gi, (lh, bd) in enumerate(d0_groups):
        pa, ba = a_slots[gi]
        k.mm(pa, lh, k.c_F1, reads=[bd, k.KC], writes=[ba])
    for gi, (lh, bd) in enumerate(d0_groups):
        pa, ba = a_slots[gi]
        o = V(P_reg, gi * 128, [[2 * 512, 2], [512, 2], [1, 128]])
        i0 = bass.AP(pa.tensor, pa.offset, [list(pa.ap[0]), [0, 2], [128, 2], [1, 128]])
        i1 = V(k.c_Tw1, 0, [[128, 2], [0, 2], [1, 128]])
        k.tt('vector', o, i0, i1, ALU.mult, reads=[ba, k.KC], writes=[bP])


def phaseH(k):
    B = k.B
    mixT = k.bfr(MIX0, 8 * S).rearrange("p (c t) -> p c t", c=8)
    k.mixT = mixT
    U = k.bfr(HW0, S)
    G = k.bfr(HW0 + 8 * KB, 2 * 32 * 128)
    R = k.bfr(HW0 + 24 * KB, 4 * 32 * 128)
    D0F = k.bfr(HW0 + 24 * KB, 2 * 32 * 128)
    PF = [k.bfr(HW0 + 40 * KB + i * 4 * KB, 2048) for i in range(4)]
    X0 = HW0 + 56 * KB
    RAWS = [k.f32r(HW0 + 24 * KB, S + 4), k.f32r(X0 + KB, S + 4)]
    T1 = k.f32r(HW0 + 24 * KB + 16416, S)
    D0 = k.bfr(X0, 32 * 128)
    PU = [k.bfr(X0 + 8 * KB + i * 4 * KB, 2048) for i in range(2)]
    QS = [k.bfr(X0 + 16 * KB + i * 4 * KB, 2048) for i in range(2)]
    DEC = [k.f32r(X0 + i * 512, 128) for i in range(2)]
    WSL = [k.bfr(HW0 + 8 * KB + i * 2 * KB, 1024).rearrange("p (c n) -> p c n", c=8) for i in range(3)]
    X0C = k.bfr(X0 + 24 * KB, S)
    TMP = [k.f32r(HW0 + 8 * KB + 10 * KB + i * 2 * KB, 512) for i in range(2)]
    hT = k.hT
    ASL = [(k.ps[b][:, h * 256:(h + 1) * 256], B(f'ps{b}')) for b in range(2) for h in range(2)]
    CSm = lambda i: k.c_CS4[:, i * 128:(i + 1) * 128]
    W3m = lambda i: k.c_W3[:, i * 256:(i + 1) * 256]
    PHIm = lambda i: k.c_PHI[:, i * 128:(i + 1) * 128]
    XRE = [0, 2, 2, 1]
    XIM = [3, 0, 0, 2]
    XIMN = [2, 1, 1, 3]
    QW = [0, 2, 2, 1]
    RW = [0, 1, 2, 0]
    braws, bt1, bu, bx0c = [B('hraw0'), B('hraw1')], B('ht1'), B('hu'), B('hx0c')

    for ct in getattr(k, 'dev_cts', range(4)):
        for RAW, braw in zip(RAWS, braws):
            k.mset('vector', RAW[:, 0:1], 0.0, writes=[braw])
            k.mset('vector', RAW[:, S + 1:S + 2], 0.0, writes=[braw])
        cols3 = (512 + ct * 128, 1024 + ct * 128, ct * 128)
        for wi, col in enumerate(cols3):
            load_w(k, WSL[wi], k.w_in[:, col:col + 128], B(f'hw{wi}'))
        for wi, col in enumerate(cols3):
            bw = B(f'hw{wi}')
            RAW, braw = RAWS[wi % 2], braws[wi % 2]

            def evac(ch, pap, bp, RAW=RAW, braw=braw):
                k.act(RAW[:, 1 + ch * 512: 1 + (ch + 1) * 512], pap, ACT.Copy, reads=[bp], writes=[braw])
            proj_fm(k, WSL[wi], 0, 128, evac, bw, f'h{wi}')
            ti = col // 128
            if wi == 0:
                short_conv(k, RAW, T1, U, ti, braw, bt1, bu)
            elif wi == 1:
                short_conv(k, RAW, T1, T1, ti, braw, bt1, bt1)
                k.tt('gpsimd', U, T1, U, ALU.mult, reads=[bt1, bu], writes=[bu])
            else:
                short_conv(k, RAW, T1, X0C, ti, braw, bt1, bx0c)
        k.barrier()
        bd0f = B('hd0f')
        bG = B('hG')
        for j in range(32):
            bank = 6 + j % 2
            bp = B(f'ps{bank}')
            s2 = j % 2
            bdec = B(f'hdec{s2}')
            lh = V(k.h3T, j, [[32, 128]], p0=0, np_=64)
            rh = V(k.c_fw4, ct * 128, [[512, 2], [1, 128]], p0=0, np_=64)
            k.mm(k.ps[bank][:, 0:256], lh, rh, reads=[B('h3T'), k.KCG], writes=[bp])
            k.act(DEC[s2], k.c_absd[:, ct * 128:(ct + 1) * 128], ACT.Exp, reads=[k.KC], writes=[bdec],
                  scale=k.c_negt[:, j:j + 1])
            o = V(D0F, j * 4, [[32 * 128, 2], [128, 32], [1, 4]])
            i0 = bass.AP(k.ps[bank].tensor, k.ps[bank].offset, [list(k.ps[bank].ap[0]), [128, 2], [4, 32], [1, 4]])
            i1 = V(DEC[s2], 0, [[0, 2], [4, 32], [1, 4]])
            k.tt('vector', o, i0, i1, ALU.mult, reads=[bp, bdec], writes=[bd0f])
        def f12(bt):
            pf, pb_ = (PF[(bt % 2) * 2], B(f'hpf{bt % 2}')), (PF[(bt % 2) * 2 + 1], B(f'hpb{bt % 2}'))
            for di, pslot in enumerate((pf, pb_)):
                groups = [(V(D0F, di * 4096 + (bt * 4 + gi) * 128, [[1, 128]]), bd0f) for gi in range(4)]
                fft_fwd_batch(k, groups, pslot, ASL, 'f')

        f12(0)
        for bt in range(8):
            if bt + 1 < 8:
                f12(bt + 1)
            pf, pb_ = (PF[(bt % 2) * 2], B(f'hpf{bt % 2}')), (PF[(bt % 2) * 2 + 1], B(f'hpb{bt % 2}'))
            bre, bim = 2 + (bt % 2) * 2, 3 + (bt % 2) * 2
            bpr, bpi = B(f'ps{bre}'), B(f'ps{bim}')
            for (bank, bp, cf, cb) in ((bre, bpr, XRE, XRE), (bim, bpi, XIM, XIMN)):
                n = 0
                for (preg, bP), coef in ((pf, cf), (pb_, cb)):
                    for comp in range(4):
                        k.mm(k.ps[bank][:, :], CSm(coef[comp]), preg[:, comp * 512:(comp + 1) * 512],
                             start=(n == 0), stop=(n == 7), reads=[bP, k.KC], writes=[bp])
                        n += 1
            for ri, (bank, bp) in enumerate(((bre, bpr), (bim, bpi))):
                k.act(G[:, ri * 4096 + bt * 512: ri * 4096 + (bt + 1) * 512], k.ps[bank][:, :], ACT.Copy,
                      reads=[bp], writes=[bG])
        k.barrier()
        bd0 = B('hd0')
        bR = B('hR')
        for jb in range(4):
            bank = 6 + jb % 2
            bp = B(f'ps{bank}')
            for jj in range(8):
                j = jb * 8 + jj
                k.tr(k.psb[bank][:, jj * 128:(jj + 1) * 128], V(U, j, [[32, 128]]), k.c_ident_bf,
                     reads=[bu, k.KC], writes=[bp])
            o = V(D0, jb * 8 * 4, [[4, 8], [128, 32], [1, 4]])
            i0 = bass.AP(k.psb[bank].tensor, k.psb[bank].offset, [list(k.psb[bank].ap[0]), [128, 8], [4, 32], [1, 4]])
            k.cp('vector', o, i0, reads=[bp], writes=[bd0])
        def u12(bt):
            pu = (PU[bt % 2], B(f'hpu{bt % 2}'))
            groups = [(V(D0, (bt * 4 + gi) * 128, [[1, 128]]), bd0) for gi in range(4)]
            fft_fwd_batch(k, groups, pu, ASL, 'u')

        def u3q(bt):
            pu = (PU[bt % 2], B(f'hpu{bt % 2}'))
            bre, bim = 2 + (bt % 2) * 2, 3 + (bt % 2) * 2
            bpr, bpi = B(f'ps{bre}'), B(f'ps{bim}')
            for (bank, bp, cf) in ((bre, bpr, XRE), (bim, bpi, XIM)):
                for comp in range(4):
                    k.mm(k.ps[bank][:, :], CSm(cf[comp]), pu[0][:, comp * 512:(comp + 1) * 512],
                         start=(comp == 0), stop=(comp == 3), reads=[pu[1], k.KC], writes=[bp])
            qreg, bq = QS[bt % 2], B(f'hq{bt % 2}')
            gsl = V(G, bt * 512, [[4096, 2], [1, 512]])
            for half, (bank, bp) in enumerate(((bre, bpr), (bim, bpi))):
                if half == 0:
                    o = V(qreg, 0, [[512, 2], [1, 512]])
                else:
                    o = V(qreg, 1024, [[512, 2], [1, 512]])
                i0 = bass.AP(k.ps[bank].tensor, k.ps[bank].offset, [list(k.ps[bank].ap[0]), [0, 2], [1, 512]])
                k.tt('vector', o, i0, gsl, ALU.mult, reads=[bp, bG], writes=[bq])
        def u3p(bt):
            qreg, bq = QS[bt % 2], B(f'hq{bt % 2}')
            for gi in range(4):
                pa, ba = ASL[gi]
                for comp in range(4):
                    k.mm(pa, qreg[:, comp * 512 + gi * 128: comp * 512 + (gi + 1) * 128], W3m(QW[comp]),
                         start=(comp == 0), stop=(comp == 3), reads=[bq, k.KC], writes=[ba], skip_group_check=True)
            for gi in range(4):
                pa, ba = ASL[gi]
                g = bt * 4 + gi
                for a in range(2):
                    o = V(R, (a * 2) * 4096 + g * 4, [[4096, 2], [128, 32], [1, 4]])
                    i0 = bass.AP(pa.tensor, pa.offset, [list(pa.ap[0]), [128, 2], [4, 32], [1, 4]])
                    i1 = V(k.c_Tw2, a * 32, [[0, 2], [1, 32], [0, 4]])
                    k.tt('vector', o, i0, i1, ALU.mult, reads=[ba, k.KC], writes=[bR])

        u12(0)
        for bt in range(9):
            if bt + 1 < 8:
                u12(bt + 1)
            if bt < 8:
                u3q(bt)
            if bt >= 1:
                u3p(bt - 1)
        k.barrier()
        bmix = B(f'mix{ct}')
        for jb in range(8):
            bank = 4 + jb % 2
            bp = B(f'ps{bank}')
            for jj in range(4):
                j = jb * 4 + jj
                for comp in range(4):
                    k.mm(k.ps[bank][:, jj * 128:(jj + 1) * 128], R[:, comp * 4096 + j * 128: comp * 4096 + (j + 1) * 128],
                         PHIm(RW[comp]), start=(comp == 0), stop=(comp == 3), reads=[bR, k.KC], writes=[bp],
                         skip_group_check=True)
            tmp, btmp = TMP[jb % 2], B(f'htmp{jb % 2}')
            uv = V(U, jb * 4, [[32, 128], [1, 4]])
            pv = bass.AP(k.ps[bank].tensor, k.ps[bank].offset, [list(k.ps[bank].ap[0]), [1, 128], [128, 4]])
            tv = V(tmp, 0, [[4, 128], [1, 4]])
            k.stt('vector', tv, uv, k.c_fbias[:, ct:ct + 1], pv, ALU.mult, ALU.add, reads=[bu, bp, k.KC], writes=[btmp])
            xv = V(X0C, jb * 4, [[32, 128], [1, 4]])
            mv = bass.AP(mixT.tensor, mixT.offset + ct * S + jb * 4, [list(mixT.ap[0]), [32, 128], [1, 4]])
            k.tt('gpsimd', mv, tv, xv, ALU.mult, reads=[btmp, bx0c], writes=[bmix])
        k.barrier()


def phaseA(k):
    B = k.B
    hT, mixT = k.hT, k.mixT
    W = WORK0
    QT = k.bfr(W, S)
    KT = k.bfr(W + 8 * KB, S)
    VA = k.bfr(W + 16 * KB, 32 * 130).rearrange("p (s e) -> p s e", s=32)
    RC = [k.f32r(W + 25 * KB + i * 2 * KB, 512) for i in range(2)]
    RS = [k.f32r(W + 29 * KB + i * 2 * KB, 512) for i in range(2)]
    WSL = [k.bfr(W + 33 * KB + i * 2 * KB, 1024).rearrange("p (c n) -> p c n", c=8) for i in range(5)]
    PT = [k.bfr(W + 43 * KB + i * KB, 512) for i in range(4)]
    TA = [k.f32r(W + 47 * KB + i * 2 * KB, 512) for i in range(4)]
    OSB = [k.f32r(W + 55 * KB + i * 512, 128) for i in range(2)]
    YN = [k.bfr(W + 56 * KB + i * 256, 128) for i in range(2)]
    JK = k.bfr(W + 57 * KB, 128)
    lam = k.c_lam
    bl = B('lam')
    lv = k.c_lamv
    k.stt('vector', JK[:, 0:64], lv[:, 0:64], 1.0, lv[:, 64:128], ALU.mult, ALU.mult, reads=[k.KC], writes=[bl, B('ajk')],
          accum_out=lam[:, 0:1])
    k.stt('vector', JK[:, 0:64], lv[:, 128:192], 1.0, lv[:, 192:256], ALU.mult, ALU.mult, reads=[k.KC], writes=[bl, B('ajk')],
          accum_out=lam[:, 1:2])
    k.act(lam[:, 2:4], lam[:, 0:2], ACT.Exp, reads=[bl], writes=[bl])
    k.tt('vector', lam[:, 4:5], lam[:, 2:3], lam[:, 3:4], ALU.subtract, reads=[bl], writes=[bl])
    k.ts('vector', lam[:, 5:6], lam[:, 4:5], 0.2, -1.0, ALU.add, ALU.mult, reads=[bl], writes=[bl])
    neglam = lam[:, 5:6]
    bQ, bK, bV = B('aQT'), B('aKT'), B('aVA')

    def oreg(r):
        return k.ps[4 + r // 3][:, (r % 3) * 129:(r % 3) * 129 + 129], B(f'ps{4 + r // 3}')

    for hd in getattr(k, 'dev_heads', range(4)):
        srcs = [k.w_in[:, 1536 + hd * 128: 1536 + (hd + 1) * 128], k.w_qkp[:, hd * 128:(hd + 1) * 128],
                k.w_in[:, 2048 + hd * 128: 2048 + (hd + 1) * 128], k.w_qkp[:, 512 + hd * 128: 512 + (hd + 1) * 128],
                k.w_in[:, 2560 + hd * 128: 2560 + (hd + 1) * 128]]
        bws = [B(f'aw{i}') for i in range(5)]
        for i in range(5):
            load_w(k, WSL[i], srcs[i], bws[i])
        k.mset('vector', VA[:, :, 128:129], 1.0, writes=[bV])
        for sb in range(8):
            bank = 6 + sb % 2
            bp = B(f'ps{bank}')
            for s4 in range(4):
                st = sb * 4 + s4
                for c in range(8):
                    k.mm(k.ps[bank][:, s4 * 128:(s4 + 1) * 128], hT[:, c, st * 128:(st + 1) * 128], WSL[4][:, c, :],
                         start=(c == 0), stop=(c == 7), reads=[bws[4], B(f'hT{st}')], writes=[bp], skip_group_check=True)
            pin = k.ps[bank][:, :].rearrange("p (s e) -> p s e", s=4)
            k.act(VA[:, sb * 4:(sb + 1) * 4, 0:128], pin, ACT.Copy, reads=[bp], writes=[bV])
        for ch in range(8):
            s2 = ch % 2
            brc, brs = B(f'arc{s2}'), B(f'ars{s2}')
            k.dma('sync', RC[s2], k.t_ropec[:, ch * 512:(ch + 1) * 512], writes=[brc])
            k.dma('sync', RS[s2], k.t_ropes[:, ch * 512:(ch + 1) * 512], writes=[brs])
            hb = [B(f'hT{i}') for i in range(ch * 4, ch * 4 + 4)]
            for wi in range(4):
                bank = 2 + wi
                for c in range(8):
                    k.mm(k.ps[bank][:, :], WSL[wi][:, c, :], hT[:, c, ch * 512:(ch + 1) * 512],
                         start=(c == 0), stop=(c == 7), reads=[bws[wi]] + hb, writes=[B(f'ps{bank}')])
            for qi, (dst, bdst, sc) in enumerate(((QT, bQ, 0.125), (KT, bK, 1.0))):
                ta, tb = TA[qi * 2], TA[qi * 2 + 1]
                bta, btb = B(f'ata{qi}'), B(f'atb{qi}')
                k.stt('vector', ta, k.ps[2 + qi * 2][:, :], sc, RC[s2], ALU.mult, ALU.mult,
                      reads=[B(f'ps{2 + qi * 2}'), brc], writes=[bta])
                k.stt('vector', tb, k.ps[3 + qi * 2][:, :], sc, RS[s2], ALU.mult, ALU.mult,
                      reads=[B(f'ps{3 + qi * 2}'), brs], writes=[btb])
                k.tt('gpsimd', dst[:, ch * 512:(ch + 1) * 512], ta, tb, ALU.add, reads=[bta, btb], writes=[bdst])
        def qk_exp(qc, st):
            par = st % 2
            for c in range(2):
                bank = par * 2 + c
                k.mm(k.ps[bank][:, :], KT[c * 64:(c + 1) * 64, st * 128:(st + 1) * 128],
                     QT[c * 64:(c + 1) * 64, qc * 512:(qc + 1) * 512], reads=[bK, bQ], writes=[B(f'ps{bank}')])
            for c in range(2):
                bank = par * 2 + c
                k.act(PT[par * 2 + c], k.ps[bank][:, :], ACT.Exp, reads=[B(f'ps{bank}')], writes=[B(f'apt{par * 2 + c}')])

        for qc in range(8):
            qk_exp(qc, 0)
            for st in range(32):
                par = st % 2
                if st + 1 < 32:
                    qk_exp(qc, st + 1)
                for c in range(2):
                    for qs in range(4):
                        r = c * 4 + qs
                        oap, bo = oreg(r)
                        k.mm(oap, PT[par * 2 + c][:, qs * 128:(qs + 1) * 128], VA[:, st, 0:129],
                             start=(st == 0 and r % 3 == 0), stop=(st == 31),
                             reads=[B(f'apt{par * 2 + c}'), bV], writes=[bo], skip_group_check=True)
            for qs in range(4):
                o0, bo0 = oreg(qs)
                o1, bo1 = oreg(4 + qs)
                s2 = qs % 2
                cols = k.c_cols[:, 64 + s2 * 16: 64 + s2 * 16 + 16]
                bc = B(f'acol{s2}')
                osb, bos = OSB[s2], B(f'aosb{s2}')
                yn, byn = YN[s2], B(f'ayn{s2}')
                P_ = k.P
                P_.op('vector', lambda e, a=cols[:, 0:1], b=o0[:, 128:129]: e.reciprocal(out=a, in_=b), [bo0], [bc])
                P_.op('vector', lambda e, a=cols[:, 1:2], b=o1[:, 128:129]: e.reciprocal(out=a, in_=b), [bo1], [bc])
                k.tt('vector', cols[:, 2:3], cols[:, 1:2], neglam, ALU.mult, reads=[bc, bl], writes=[bc])
                k.ts('vector', osb, o0[:, 0:128], cols[:, 0:1], None, ALU.mult, reads=[bo0, bc], writes=[bos])
                k.stt('vector', osb, o1[:, 0:128], cols[:, 2:3], osb, ALU.mult, ALU.add, reads=[bo1, bc, bos], writes=[bos])
                k.stt('vector', JK, osb, 1.0, osb, ALU.mult, ALU.mult, reads=[bos], writes=[bc, B('ajk')], accum_out=cols[:, 3:4])
                rstd_from_ss(k, cols[:, 3:4], cols[:, 4:5], cols[:, 5:6], cols[:, 6:7], 128.0, 1e-5, [bc])
                k.ts('vector', yn, osb, cols[:, 6:7], 0.8, ALU.mult, ALU.mult, reads=[bos, bc], writes=[byn])
                k.tr(k.psb[7][:, 0:128], yn, k.c_ident_bf, reads=[byn, k.KC], writes=[B('ps7')])
                q0 = qc * 512 + qs * 128
                k.act(mixT[:, 4 + hd, q0:q0 + 128], k.psb[7][:, 0:128], ACT.Identity, reads=[B('ps7'), k.KC],
                      writes=[B(f'mix{4 + hd}')], scale=k.c_subg[:, 0:1])


def load_mlp_w(k):
    B = k.B
    WUP = k.bfr(HT0, 8 * 4096).rearrange("p (c n) -> p c n", c=8)
    k.WUP = WUP
    for c in range(8):
        k.dma('gpsimd', WUP[:, c, :], k.w_up[c * 128:(c + 1) * 128, :], writes=[B('wup')])


def phaseO(k):
    B = k.B
    mixT = k.mixT
    W = WORK0
    load_mlp_w(k)
    WO = k.bfr(W, 8 * 1024).rearrange("p (c n) -> p c n", c=8)
    GP = k.f32r(W + 16 * KB, 1024)
    XT = [k.f32r(W + 20 * KB + i * 4 * KB, 1024) for i in range(3)]
    MS = [k.f32r(W + 32 * KB + i * 4 * KB, 1024) for i in range(2)]
    TMP = [k.f32r(W + 40 * KB + i * 4 * KB, 1024) for i in range(2)]
    JK = k.bfr(W + 48 * KB, 1024)
    bwo, bgp = B('owo'), B('ogp')
    for half in range(2):
        k.dma('gpsimd', WO[:, :, half * 512:(half + 1) * 512],
              k.w_out[:, half * 512:(half + 1) * 512].rearrange("(c p) n -> p c n", p=128), writes=[bwo])
    k.dma('sync', GP, bass.AP(k.gpost.tensor, 0, [[0, 128], [1, 1024]]), writes=[bgp])
    mixb = [B(f'mix{c}') for c in range(8)]
    def stage_a(i):
        s3, s2 = i % 3, i % 2
        xt, bx = XT[s3], B(f'ox{s3}')
        ms, bms = MS[s2], B(f'oms{s2}')
        k.dma('sync', xt, k.x[i * 128:(i + 1) * 128, :], writes=[bx])
        for half in range(2):
            bank = 2 * s2 + half
            bp = B(f'ps{bank}')
            for c in range(8):
                k.mm(k.ps[bank][:, :], mixT[:, c, i * 128:(i + 1) * 128], WO[:, c, half * 512:(half + 1) * 512],
                     start=(c == 0), stop=(c == 7), reads=[bwo] + mixb, writes=[bp])
            k.act(ms[:, half * 512:(half + 1) * 512], k.ps[bank][:, :], ACT.Copy, reads=[bp], writes=[bms])

    def stage_b(i):
        s3, s2 = i % 3, i % 2
        xt, bx = XT[s3], B(f'ox{s3}')
        ms, bms = MS[s2], B(f'oms{s2}')
        tmp, btmp = TMP[s2], B(f'otmp{s2}')
        cols = k.c_cols[:, 96 + s2 * 8: 96 + s2 * 8 + 8]
        bc = B(f'ocol{s2}')
        k.stt('vector', JK, ms, 1.0, ms, ALU.mult, ALU.mult, reads=[bms], writes=[bc, B('ojk')], accum_out=cols[:, 0:1])
        rstd_from_ss(k, cols[:, 0:1], cols[:, 1:2], cols[:, 2:3], cols[:, 3:4], 1024.0, 1e-6, [bc])
        k.stt('vector', tmp, ms, cols[:, 3:4], GP, ALU.mult, ALU.mult, reads=[bms, bc, bgp], writes=[btmp])
        k.tt('gpsimd', tmp, tmp, xt, ALU.add, reads=[btmp, bx], writes=[btmp])
        k.dma('sync', k.xa_scr[i * 128:(i + 1) * 128, :], tmp, reads=[btmp], writes=[B(f'xa{i}')], sembuf=btmp)

    stage_a(0)
    for i in range(32):
        if i + 1 < 32:
            stage_a(i + 1)
        stage_b(i)


def phaseM(k):
    B = k.B
    WUP = k.WUP
    WDN = k.bfr(MIX0, 32 * 1024).rearrange("p (f n) -> p f n", f=32)
    for f0 in range(0, 32, 4):
        k.dma('gpsimd', WDN[:, f0:f0 + 4, :],
              k.w_down[f0 * 128:(f0 + 4) * 128, :].rearrange("(f p) n -> p f n", p=128), writes=[B('wdn')])
    W = WORK0
    GP = k.f32r(W, 1024)
    XA = [[k.f32r(W + 4 * KB + (a * 2 + b) * 4 * KB, 1024) for b in range(2)] for a in range(2)]
    XN = [k.bfr(W + 20 * KB + i * 2 * KB, 1024) for i in range(2)]
    H2T = [k.bfr(W + 24 * KB + i * 4 * KB, 8 * 256).rearrange("p (c t) -> p c t", c=8) for i in range(2)]
    UPT = k.bfr(W + 32 * KB, 32 * 256).rearrange("p (f t) -> p f t", f=32)
    RL = [k.f32r(W + 48 * KB + i * KB, 256) for i in range(2)]
    MS = [k.f32r(W + 50 * KB + i * 4 * KB, 1024) for i in range(2)]
    JK = k.bfr(W + 58 * KB, 1024)
    bgp = B('mgp')
    k.dma('sync', GP, bass.AP(k.gpost.tensor, 1024, [[0, 128], [1, 1024]]), writes=[bgp])
    gb = V(k.c_gcols, 8, [[1, 8], [0, 128]])
    bwup, bwdn, bupt = B('wup'), B('wdn'), B('mupt')
    k.out_ticks = []
    nck = getattr(k, 'dev_nck', 16)

    def prologue(ck):
        a = ck % 2
        bh2 = B(f'mh2{a}')
        for tl in range(2):
            i = ck * 2 + tl
            xa, bxa = XA[a][tl], B(f'mxa{a}{tl}')
            xn, bxn = XN[tl], B(f'mxn{tl}')
            cols = k.c_cols[:, 128 + tl * 8: 128 + tl * 8 + 8]
            bc = B(f'mcol{tl}')
            k.dma('sync', xa, k.xa_scr[i * 128:(i + 1) * 128, :], reads=[B(f'xa{i}')], writes=[bxa])
            k.stt('vector', JK, xa, 1.0, xa, ALU.mult, ALU.mult, reads=[bxa], writes=[bc, B('mjk')], accum_out=cols[:, 0:1])
            rstd_from_ss(k, cols[:, 0:1], cols[:, 1:2], cols[:, 2:3], cols[:, 3:4], 1024.0, 1e-6, [bc])
            k.act(xn, xa, ACT.Identity, reads=[bxa, bc], writes=[bxn], scale=cols[:, 3:4])
            bank = tl
            bp = B(f'ps{bank}')
            for c in range(8):
                k.tr(k.psb[bank][:, c * 128:(c + 1) * 128], xn[:, c * 128:(c + 1) * 128], k.c_ident_bf,
                     reads=[bxn, k.KC], writes=[bp])
            pin = k.psb[bank][:, 0:1024].rearrange("p (c t) -> p c t", c=8)
            k.tt('vector', H2T[a][:, :, tl * 128:(tl + 1) * 128], pin, gb, ALU.mult, reads=[bp, k.KC], writes=[bh2])

    prologue(0)
    for ck in range(nck):
        a = ck % 2
        bh2 = B(f'mh2{a}')
        for f in range(32):
            bank = 2 + f % 2
            bp = B(f'ps{bank}')
            rl, brl = RL[f % 2], B(f'mrl{f % 2}')
            for c in range(8):
                k.mm(k.ps[bank][:, 0:256], WUP[:, c, f * 128:(f + 1) * 128], H2T[a][:, c, :],
                     start=(c == 0), stop=(c == 7), reads=[bwup, bh2], writes=[bp])
            k.act(rl, k.ps[bank][:, 0:256], ACT.Relu, reads=[bp], writes=[brl])
            k.tt('gpsimd', UPT[:, f, :], rl, rl, ALU.mult, reads=[brl], writes=[bupt])
        if ck + 1 < nck:
            prologue(ck + 1)
        for tl in range(2):
            i = ck * 2 + tl
            xa, bxa = XA[a][tl], B(f'mxa{a}{tl}')
            ms, bms = MS[tl], B(f'mms{tl}')
            cols = k.c_cols[:, 144 + tl * 8: 144 + tl * 8 + 8]
            bc = B(f'mcol2{tl}')
            for half in range(2):
                bank = 4 + tl * 2 + half
                bp = B(f'ps{bank}')
                for f in range(32):
                    k.mm(k.ps[bank][:, :], UPT[:, f, tl * 128:(tl + 1) * 128], WDN[:, f, half * 512:(half + 1) * 512],
                         start=(f == 0), stop=(f == 31), reads=[bupt, bwdn], writes=[bp])
                k.act(ms[:, half * 512:(half + 1) * 512], k.ps[bank][:, :], ACT.Copy, reads=[bp], writes=[bms])
            k.stt('vector', JK, ms, 1.0, ms, ALU.mult, ALU.mult, reads=[bms], writes=[bc, B('mjk')], accum_out=cols[:, 0:1])
            rstd_from_ss(k, cols[:, 0:1], cols[:, 1:2], cols[:, 2:3], cols[:, 3:4], 1024.0, 1e-6, [bc])
            k.stt('vector', ms, ms, cols[:, 3:4], GP, ALU.mult, ALU.mult, reads=[bms, bc, bgp], writes=[bms])
            k.tt('gpsimd', ms, ms, xa, ALU.add, reads=[bms, bxa], writes=[bms])
            k.out_ticks.append(k.dma('sync', k.out[i * 128:(i + 1) * 128, :], ms, reads=[bms], sembuf=bms))


def host_inputs(inputs):
    T = host_tables()

    def g(k):
        return np.asarray(inputs[k], dtype=np.float32)
    w_in = np.ascontiguousarray(g('w_in')[0])
    qk = w_in[:, 1536:2560]
    idx = np.arange(1024)
    d = idx % 64
    perm = (idx - d) + (d + 32) % 64
    sh = {}
    sh['w_in'] = w_in
    sh['w_qkp'] = np.ascontiguousarray(qk[:, perm])
    cwb = np.concatenate([g('conv_w')[0], g('conv_b')[0][None, :]], axis=0)
    sh['cw'] = np.ascontiguousarray(cwb.reshape(4, 12, 128).transpose(2, 1, 0).reshape(128, 48))
    sh['fw1'] = np.ascontiguousarray(g('filt_w1')[0])
    sh['fw2'] = np.ascontiguousarray(g('filt_w2')[0])
    sh['fw3'] = np.ascontiguousarray(g('filt_w3')[0])
    sh['fw4'] = np.ascontiguousarray(g('filt_w4')[0])
    sh['fcol'] = np.ascontiguousarray(np.stack([g('filt_b1')[0], g('filt_b2')[0], g('filt_b3')[0], g('filt_freq')[0]], axis=1))
    sh['fbias'] = np.ascontiguousarray(g('filt_bias')[0].reshape(4, 128).T)
    sh['lamv'] = np.concatenate([g('lam_q1')[0], g('lam_k1')[0], g('lam_q2')[0], g('lam_k2')[0]])[None, :].copy()
    sh['subg'] = np.ascontiguousarray(g('subln_gain')[0][:, None])
    sh['w_out'] = np.ascontiguousarray(g('w_out')[0])
    sh['w_up'] = np.ascontiguousarray(g('w_up')[0])
    sh['w_down'] = np.ascontiguousarray(g('w_down')[0])
    sh['gcols'] = np.ascontiguousarray(np.stack([g('attn_pre_gain')[0].reshape(8, 128).T,
                                                 g('mlp_pre_gain')[0].reshape(8, 128).T], axis=1).reshape(128, 16))
    sh['gpost'] = np.ascontiguousarray(np.stack([g('attn_post_gain')[0], g('mlp_post_gain')[0]], axis=0))
    sh['ident_bf'] = T['ident_bf']
    sh['ident_f'] = T['ident_f']
    sh['F1'] = T['F1']
    sh['Tw1'] = T['Tw1'].reshape(128, 256)
    sh['CS4'] = T['CS4'].reshape(128, 512)
    sh['W3'] = T['W3'].reshape(128, 768)
    sh['Tw2'] = T['Tw2'].reshape(128, 64)
    sh['PHI'] = T['PHI'].reshape(128, 384)
    sh['zT'] = T['zT']
    sh['negt'] = T['negt']
    sh['absdelta'] = T['absdelta']
    sh['ropec'] = T['ropec']
    sh['ropes'] = T['ropes']
    return sh


PHASES = ['p0', 'pF', 'pH', 'pA', 'pO', 'pM']


def make_program(upto='pM', dumps=(), dump_fn=None, small_out=False, **attrs):
    k = K(dumps=dumps, small_out=small_out)
    k.__dict__.update(attrs)
    fns = {'p0': phase0, 'pF': phaseF, 'pH': phaseH, 'pA': phaseA, 'pO': phaseO, 'pM': phaseM}
    for ph in PHASES:
        fns[ph](k)
        k.barrier()
        if ph == upto:
            break
    ticks = []
    if dump_fn is not None:
        ticks += dump_fn(k)
    ticks += getattr(k, 'out_ticks', [])
    k.P.wait_all('sync', ticks)
    with k.nc.Block() as block:
        k.P.emit(block)
    k.es.close()
    return k.nc


def kernel(**inputs):
    sh = host_inputs(inputs)
    x = np.asarray(inputs['x'], dtype=np.float32)
    nc = make_program()
    in_maps = []
    for c in range(8):
        m = dict(sh)
        m['x'] = np.ascontiguousarray(x[c])
        in_maps.append(m)
    res = run_bass_kernel_spmd(nc, in_maps, core_ids=list(range(8)))
    return np.stack([np.asarray(r['out'], dtype=np.float32) for r in res.results], axis=0)
```

```python
import numpy as np
import concourse.bass as bass
import concourse.mybir as mybir
from concourse.bass_utils import run_bass_kernel_spmd
from contextlib import ExitStack

F32 = mybir.dt.float32
BF16 = mybir.dt.bfloat16
ALU = mybir.AluOpType
ACT = mybir.ActivationFunctionType
AX = mybir.AxisListType

ENGS = ('tensor', 'vector', 'scalar', 'gpsimd', 'sync')
SEM_ROT = 16000


class Buf:
    __slots__ = ('name', 'w', 'r', 'dsem', 'excl')

    def __init__(self, name, excl=False):
        self.name = name
        self.excl = excl
        self.w = None
        self.r = {}
        self.dsem = None


class Prog:
    def __init__(self, nc, sems):
        self.nc = nc
        self.sems = sems
        self.nsem = 0
        self.q = {e: [] for e in ENGS}
        self.cnt = {e: 0 for e in ENGS}
        self.esem = {}
        self.owned = {e: set() for e in ENGS}
        self.seen = {e: {} for e in ENGS}
        self.dcnt = {}
        self.final = {}
        for e in ENGS:
            self._new_esem(e)

    def alloc_sem(self):
        i = self.nsem
        self.nsem += 1
        assert i < len(self.sems), "out of semaphores"
        return i

    def _new_esem(self, e):
        i = self.alloc_sem()
        self.esem[e] = i
        self.owned[e].add(i)
        self.cnt[e] = 0

    def _deps(self, eng, reads, writes, is_dma=False):
        need = {}

        def add(t, raw=False):
            if t is None:
                return
            s, v = t
            if s in self.owned[eng] and not is_dma and not (raw and eng != 'tensor'):
                return
            if need.get(s, 0) < v:
                need[s] = v
        for b in reads:
            add(b.w, raw=True)
        for b in writes:
            add(b.w)
            for s, v in b.r.items():
                add((s, v))
        out = []
        seen = self.seen[eng]
        for s, v in need.items():
            if seen.get(s, 0) < v:
                seen[s] = v
                out.append((s, v))
        return out

    def op(self, eng, fn, reads=(), writes=()):
        if any(b.excl for b in reads):
            writes = list(writes) + [b for b in reads if b.excl]
            reads = [b for b in reads if not b.excl]
        waits = self._deps(eng, reads, writes)
        if self.cnt[eng] >= SEM_ROT:
            self._new_esem(eng)
        self.cnt[eng] += 1
        s, v = self.esem[eng], self.cnt[eng]
        for b in reads:
            if b.r.get(s, 0) < v:
                b.r[s] = v
        for b in writes:
            b.w = (s, v)
            b.r = {}
        self.q[eng].append((waits, fn, s, 1))

    def dma(self, qeng, fn, reads=(), writes=(), sembuf=None):
        waits = self._deps(qeng, reads, writes, is_dma=True)
        sb = sembuf if sembuf is not None else (writes[0] if writes else reads[0])
        if sb.dsem is None or self.dcnt[sb.dsem] >= SEM_ROT:
            sb.dsem = self.alloc_sem()
            self.dcnt[sb.dsem] = 0
        s = sb.dsem
        self.dcnt[s] += 16
        v = self.dcnt[s]
        for b in reads:
            if b.r.get(s, 0) < v:
                b.r[s] = v
        for b in writes:
            b.w = (s, v)
            b.r = {}
        self.q[qeng].append((waits, fn, s, 16))
        return (s, v)

    def wait_all(self, eng, ticks):
        need = {}
        for s, v in ticks:
            if need.get(s, 0) < v:
                need[s] = v
        self.q[eng].append((list(need.items()), None, None, 0))

    def barrier(self):
        ticks = [(self.esem[o], self.cnt[o]) for o in ENGS if self.cnt[o] > 0]
        ticks += [(s, v) for s, v in self.dcnt.items()]
        for e in ENGS:
            need = []
            for s, v in ticks:
                if s in self.owned[e] and e != 'gpsimd':
                    continue
                if self.seen[e].get(s, 0) < v:
                    self.seen[e][s] = v
                    need.append((s, v))
            if need:
                self.q[e].append((need, None, None, 0))

    def replay(self, engobj, eng):
        sems = self.sems
        for waits, fn, s, inc in self.q[eng]:
            for ws, wv in waits:
                engobj.wait_ge(sems[ws], wv)
            if fn is not None:
                inst = fn(engobj)
                inst.then_inc(sems[s], inc)

    def emit(self, block):
        @block.tensor
        def _(e):
            self.replay(e, 'tensor')

        @block.vector
        def _(e):
            self.replay(e, 'vector')

        @block.scalar
        def _(e):
            self.replay(e, 'scalar')

        @block.gpsimd
        def _(e):
            self.replay(e, 'gpsimd')

        @block.sync
        def _(e):
            self.replay(e, 'sync')


def AP(base, off, dims):
    return bass.AP(base.tensor, off, [list(d) for d in dims])


S = 4096
D = 1024
NFFT = 8192
PI = float(np.pi)


def host_tables():
    import ml_dtypes
    bf = ml_dtypes.bfloat16
    T = {}
    T['ident_bf'] = np.eye(128, dtype=np.float32).astype(bf)
    T['ident_f'] = np.eye(128, dtype=np.float32)
    p = np.arange(128, dtype=np.float64)
    k1 = np.arange(128, dtype=np.float64)
    j = np.arange(32, dtype=np.float64)
    k2 = np.arange(32, dtype=np.float64)
    ang = 2 * np.pi * (k1[None, :] + 0.5) * p[:, None] / 256.0
    T['F1'] = np.concatenate([np.cos(ang), -np.sin(ang)], axis=1).astype(np.float32).astype(bf)
    jj = np.repeat(j, 4)
    a1 = 2 * np.pi * (k1[None, :] + 0.5) * jj[:, None] / NFFT
    T['Tw1'] = np.stack([np.cos(a1), -np.sin(a1)], axis=1).astype(np.float32)
    th = 2 * np.pi * np.outer(j, k2) / 32.0
    I4 = np.eye(4)
    C = np.kron(np.cos(th), I4)
    Sm = np.kron(np.sin(th), I4)
    T['CS4'] = np.stack([C, -C, Sm, -Sm], axis=1).astype(np.float32).astype(bf)
    W1 = np.concatenate([C, Sm], axis=1)
    W2 = np.concatenate([-C, -Sm], axis=1)
    W34 = np.concatenate([-Sm, C], axis=1)
    T['W3'] = np.stack([W1, W2, W34], axis=1).astype(np.float32).astype(bf)
    a2 = 2 * np.pi * (k1[:, None] + 0.5) * j[None, :] / NFFT
    T['Tw2'] = np.stack([np.cos(a2), -np.sin(a2)], axis=1).astype(np.float32)
    ph = 2 * np.pi * (k1[:, None] + 0.5) * p[None, :] / 256.0
    cphi = (2.0 / NFFT) * np.cos(ph)
    sphi = (2.0 / NFFT) * np.sin(ph)
    T['PHI'] = np.stack([cphi, -sphi, sphi], axis=1).astype(np.float32).astype(bf)
    bands = 16
    t = np.linspace(0.0, 1.0, S, dtype=np.float32)[:, None]
    w = (np.float32(2.0 * np.pi / S) * np.arange(S, dtype=np.float32))[:, None]
    f = np.linspace(1e-4, bands - 1, bands, dtype=np.float32)[None, :]
    fw_ = (f * w).astype(np.float32)
    z = np.concatenate([t, np.cos(fw_), -np.sin(fw_)], axis=-1).astype(np.float32)
    T['zT'] = np.ascontiguousarray(z.T)
    tt = np.linspace(0.0, 1.0, S, dtype=np.float32).reshape(128, 32)
    T['negt'] = (-tt).astype(np.float32)
    import math
    max_decay = math.log(1e-2) / 0.3
    min_decay = math.log(1e-2) / 1.5
    deltas = np.abs(np.linspace(min_decay, max_decay, 512, dtype=np.float32))
    T['absdelta'] = np.ascontiguousarray(np.broadcast_to(deltas[None, :], (128, 512))).astype(np.float32)
    inv = (10000.0 ** (-np.arange(0, 64, 2, dtype=np.float32) / 64)).astype(np.float32)
    angr = (np.arange(S, dtype=np.float32)[:, None] * inv[None, :]).astype(np.float32)
    angr = np.concatenate([angr, angr], axis=-1)
    cosr = np.cos(angr).astype(np.float32)
    sinr = np.sin(angr).astype(np.float32)
    sgn = np.concatenate([-np.ones(32, np.float32), np.ones(32, np.float32)])
    sins = sinr * sgn[None, :]
    T['ropec'] = np.ascontiguousarray(np.concatenate([cosr.T, cosr.T], axis=0))
    T['ropes'] = np.ascontiguousarray(np.concatenate([sins.T, sins.T], axis=0))
    return T


KB = 1024
CONST0 = 0
HT0 = 14 * KB
MIX0 = 78 * KB
WORK0 = 142 * KB
HW0 = 118 * KB
ARENA_F32 = 53100


class Ctx:
    pass


def V(region, off, dims, p0=0, np_=128):
    pstep = region.ap[0][0]
    return bass.AP(region.tensor, region.offset + p0 * pstep + off,
                   [[pstep, np_]] + [list(d) for d in dims])


class K:
  def __init__(self, dumps=(), small_out=False):
    nc = bass.Bass("TRN2", target_bir_lowering=False)

    def din(name, shape, dt=F32):
        return nc.dram_tensor(name, list(shape), dt, kind="ExternalInput").ap()

    x = din('x', [S, D])
    w_in = din('w_in', [D, 3072])
    w_qkp = din('w_qkp', [D, 1024])
    cw = din('cw', [128, 48])
    fw1 = din('fw1', [33, 64])
    fw2 = din('fw2', [64, 64])
    fw3 = din('fw3', [64, 64])
    fw4 = din('fw4', [64, 1024])
    fcol = din('fcol', [64, 4])
    fbias = din('fbias', [128, 4])
    lamv = din('lamv', [1, 256])
    subg = din('subg', [128, 1])
    w_out = din('w_out', [D, D])
    w_up = din('w_up', [D, 4096])
    w_down = din('w_down', [4096, D])
    gcols = din('gcols', [128, 16])
    gpost = din('gpost', [2, 1024])
    t_ident_bf = din('ident_bf', [128, 128], BF16)
    t_ident_f = din('ident_f', [128, 128])
    t_F1 = din('F1', [128, 256], BF16)
    t_Tw1 = din('Tw1', [128, 256])
    t_CS4 = din('CS4', [128, 512], BF16)
    t_W3 = din('W3', [128, 768], BF16)
    t_Tw2 = din('Tw2', [128, 64])
    t_PHI = din('PHI', [128, 384], BF16)
    t_zT = din('zT', [33, S])
    t_negt = din('negt', [128, 32])
    t_absd = din('absdelta', [128, 512])
    t_ropec = din('ropec', [128, S])
    t_ropes = din('ropes', [128, S])
    out = nc.dram_tensor('out', [128 if small_out else S, D], F32, kind='ExternalOutput').ap()
    xa_scr = nc.dram_tensor('xa_scr', [S, D], F32, kind='Internal').ap()
    dump_aps = {}
    for (nm, shape, dt) in dumps:
        dump_aps[nm] = nc.dram_tensor('dbg_' + nm, list(shape), dt, kind='ExternalOutput').ap()

    es = ExitStack()
    sems = [es.enter_context(nc.semaphore(f"s{i}")) for i in range(88)]
    arena_t = es.enter_context(nc.sbuf_tensor("arena", [128, ARENA_F32], F32))
    psall_t = es.enter_context(nc.psum_tensor("psall", [128, 4096], F32))
    psall = psall_t[:]
    psall_b = psall_t[:].bitcast(BF16)
    ps = [psall[:, i * 512:(i + 1) * 512] for i in range(8)]
    psb = [psall_b[:, i * 1024:(i + 1) * 1024] for i in range(8)]
    AF = arena_t[:]
    AB = arena_t[:].bitcast(BF16)

    def f32r(boff, n):
        assert boff % 4 == 0 and boff + 4 * n <= ARENA_F32 * 4, (boff, n)
        return AF[:, boff // 4: boff // 4 + n]

    def bfr(boff, n):
        assert boff % 4 == 0 and boff + 2 * n <= ARENA_F32 * 4, (boff, n)
        return AB[:, boff // 2: boff // 2 + n]

    P = Prog(nc, sems)
    bufs = {}

    def B(name):
        b = bufs.get(name)
        if b is None:
            b = Buf(name, excl=name.startswith('ps'))
            bufs[name] = b
        return b

    def mm(out, lhsT, rhs, start=True, stop=True, reads=(), writes=(), **kw):
        P.op('tensor', lambda e: e.matmul(out, lhsT=lhsT, rhs=rhs, start=start, stop=stop, **kw), reads, writes)

    def tr(out, in_, ident, reads=(), writes=()):
        P.op('tensor', lambda e: e.transpose(out=out, in_=in_, identity=ident), reads, writes)

    def act(out, in_, func, reads=(), writes=(), **kw):
        P.op('scalar', lambda e: e.activation(out=out, in_=in_, func=func, **kw), reads, writes)

    def tt(eng, out, in0, in1, op, reads=(), writes=()):
        P.op(eng, lambda e: e.tensor_tensor(out=out, in0=in0, in1=in1, op=op), reads, writes)

    def ts(eng, out, in0, s1, s2, op0, op1=None, reads=(), writes=(), **kw):
        if op1 is None:
            P.op(eng, lambda e: e.tensor_scalar(out=out, in0=in0, scalar1=s1, scalar2=None, op0=op0, **kw), reads, writes)
        else:
            P.op(eng, lambda e: e.tensor_scalar(out=out, in0=in0, scalar1=s1, scalar2=s2, op0=op0, op1=op1, **kw), reads, writes)

    def stt(eng, out, in0, scalar, in1, op0, op1, reads=(), writes=(), **kw):
        P.op(eng, lambda e: e.scalar_tensor_tensor(out=out, in0=in0, scalar=scalar, in1=in1, op0=op0, op1=op1, **kw), reads, writes)

    def cp(eng, out, in_, reads=(), writes=()):
        P.op(eng, lambda e: e.tensor_copy(out=out, in_=in_), reads, writes)

    def mset(eng, ap, val, writes=()):
        P.op(eng, lambda e: e.memset(ap, val), (), writes)

    def dma(q, out, in_, reads=(), writes=(), sembuf=None):
        return P.dma(q, lambda e: e.dma_start(out=out, in_=in_), reads, writes, sembuf)

    def barrier():
        P.barrier()

    cpos = [CONST0]

    def calloc(nbytes):
        o = cpos[0]
        cpos[0] += (nbytes + 63) // 64 * 64
        assert cpos[0] <= HT0, cpos[0]
        return o

    c_ident_bf = bfr(calloc(256), 128)
    c_ident_f = f32r(calloc(512), 128)
    c_F1 = bfr(calloc(512), 256)
    c_Tw1 = f32r(calloc(1024), 256)
    c_CS4 = bfr(calloc(1024), 512)
    c_W3 = bfr(calloc(1536), 768)
    c_Tw2 = f32r(calloc(256), 64)
    c_PHI = bfr(calloc(768), 384)
    c_gcols = f32r(calloc(64), 16)
    c_cw = f32r(calloc(192), 48)
    c_fbias = f32r(calloc(16), 4)
    c_subg = f32r(calloc(4), 1)
    c_lamv = f32r(calloc(1024), 256)
    c_negt = f32r(calloc(128), 32)
    c_absd = f32r(calloc(2048), 512)
    c_fw1 = f32r(calloc(256), 64)
    c_fw2 = f32r(calloc(256), 64)
    c_fw3 = f32r(calloc(256), 64)
    c_fcol = f32r(calloc(16), 4)
    c_fw4 = bfr(calloc(2048), 1024)
    c_cols = f32r(calloc(1024), 256)
    c_lam = f32r(calloc(64), 16)
    KC = B('consts')
    for dst, src in [(c_ident_f, t_ident_f), (c_Tw1, t_Tw1), (c_Tw2, t_Tw2), (c_gcols, gcols), (c_cw, cw),
                     (c_fbias, fbias), (c_subg, subg), (c_negt, t_negt), (c_absd, t_absd), (c_fcol[0:64, :], fcol),
                     (c_fw1[0:33, :], fw1), (c_fw2[0:64, :], fw2), (c_fw3[0:64, :], fw3),
                     (c_ident_bf, t_ident_bf), (c_F1, t_F1), (c_CS4, t_CS4), (c_W3, t_W3), (c_PHI, t_PHI)]:
        dma('sync', dst, src, writes=[KC])
    dma('sync', c_lamv, bass.AP(lamv.tensor, 0, [[0, 128], [1, 256]]), writes=[KC])
    KCG = B('consts_g')
    dma('gpsimd', c_fw4[0:64, :], fw4, writes=[KCG])

    d = dict(locals())
    d.pop('self')
    self.__dict__.update(d)


def rstd_from_ss(k, ss, ms, ln, rstd, n, eps, bufs):
    k.ts('vector', ms, ss, 1.0 / n, eps, ALU.mult, ALU.add, reads=bufs, writes=bufs)
    k.act(ln, ms, ACT.Ln, reads=bufs, writes=bufs)
    k.act(rstd, ln, ACT.Exp, reads=bufs, writes=bufs, scale=-0.5)


def phase0(k):
    B = k.B
    hT = k.bfr(HT0, 8 * S).rearrange("p (c t) -> p c t", c=8)
    k.hT = hT
    XT = [k.f32r(WORK0 + i * 4096, 1024) for i in range(3)]
    XN = [k.bfr(WORK0 + 12288 + i * 2048, 1024) for i in range(2)]
    JK = k.bfr(WORK0 + 16384, 1024)
    gb = V(k.c_gcols, 0, [[1, 8], [0, 128]])
    nt = getattr(k, 'dev_ntiles', 32)

    def stage_a(i):
        s3, s2 = i % 3, i % 2
        xt, xn = XT[s3], XN[s2]
        bx, bn, bc = B(f'p0x{s3}'), B(f'p0n{s2}'), B(f'p0c{s2}')
        cols = k.c_cols[:, s2 * 8: s2 * 8 + 8]
        ss, ms, ln, rs = cols[:, 0:1], cols[:, 1:2], cols[:, 2:3], cols[:, 3:4]
        k.dma('sync', xt, k.x[i * 128:(i + 1) * 128, :], writes=[bx])
        k.stt('vector', JK, xt, 1.0, xt, ALU.mult, ALU.mult, reads=[bx], writes=[B('p0jk'), bc], accum_out=ss)
        rstd_from_ss(k, ss, ms, ln, rs, 1024.0, 1e-6, [bc])
        k.act(xn, xt, ACT.Identity, reads=[bx, bc], writes=[bn], scale=rs)

    def stage_b(i):
        s2 = i % 2
        xn, bn = XN[s2], B(f'p0n{s2}')
        bank = s2
        bp = B(f'ps{bank}')
        for c in range(8):
            k.tr(k.psb[bank][:, c * 128:(c + 1) * 128], xn[:, c * 128:(c + 1) * 128], k.c_ident_bf,
                 reads=[bn, k.KC], writes=[bp])
        pin = k.psb[bank][:, 0:1024].rearrange("p (c t) -> p c t", c=8)
        k.tt('vector', hT[:, :, i * 128:(i + 1) * 128], pin, gb, ALU.mult, reads=[bp, k.KC], writes=[B(f'hT{i}')])

    if nt > 0:
        stage_a(0)
    for i in range(nt):
        if i + 1 < nt:
            stage_a(i + 1)
        stage_b(i)


H3T0 = MIX0 + 32 * KB


def phaseF(k):
    B = k.B
    ZT = k.f32r(HW0, S)
    ARG = k.f32r(HW0 + 16 * KB, S)
    WT = k.f32r(HW0 + 32 * KB, S)
    HA = k.f32r(HW0 + 48 * KB, S)
    HB = k.f32r(HW0 + 64 * KB, S)
    h3T = k.bfr(H3T0, S)
    k.h3T = h3T
    bz, ba, bw = B('fz'), B('farg'), B('fwt')
    k.dma('sync', ZT[0:33, :], k.t_zT, writes=[bz])
    freq = k.c_fcol[0:64, 3:4]
    layers = [(k.c_fw1[0:33, :], ZT[0:33, :], HA, bz, B('fha')),
              (k.c_fw2[0:64, :], HA[0:64, :], HB, B('fha'), B('fhb')),
              (k.c_fw3[0:64, :], HB[0:64, :], h3T, B('fhb'), B('h3T'))]
    for L, (w, inp, outp, bin_, bout) in enumerate(layers):
        fb = k.c_cols[0:64, 32 + L: 33 + L]
        bfb = B(f'ffb{L}')
        k.tt('vector', fb, k.c_fcol[0:64, L:L + 1], freq, ALU.mult, reads=[k.KC], writes=[bfb])
        for ch in range(8):
            bank = ch % 2
            bp = B(f'ps{bank}')
            k.mm(k.ps[bank][0:64, :], w, inp[:, ch * 512:(ch + 1) * 512], reads=[k.KC, bin_], writes=[bp])
            k.ts('vector', ARG[0:64, ch * 512:(ch + 1) * 512], k.ps[bank][0:64, :], freq, fb, ALU.mult, ALU.add,
                 reads=[bp, k.KC, bfb], writes=[ba])
        a = ARG[0:64, :]
        wt = WT[0:64, :]
        k.ts('vector', wt, a, PI, -2 * PI, ALU.is_gt, ALU.mult, reads=[ba], writes=[bw])
        k.tt('vector', a, a, wt, ALU.add, reads=[ba, bw], writes=[ba])
        k.ts('vector', wt, a, -PI, 2 * PI, ALU.is_lt, ALU.mult, reads=[ba], writes=[bw])
        k.tt('vector', a, a, wt, ALU.add, reads=[ba, bw], writes=[ba])
        k.act(outp[0:64, :], a, ACT.Sin, reads=[ba], writes=[bout])


def load_w(k, dst3, src_cols, bw):
    k.dma('gpsimd', dst3, src_cols.rearrange("(c p) n -> p c n", p=128), writes=[bw])


def proj_fm(k, wt3, col0, ncols, dst_fn, bw, tag, banks=(6, 7)):
    B = k.B
    for ch in range(8):
        bank = banks[ch % len(banks)]
        bp = B(f'ps{bank}')
        hb = [B(f'hT{i}') for i in range(ch * 4, ch * 4 + 4)]
        for c in range(8):
            k.mm(k.ps[bank][:, :], wt3[:, c, col0:col0 + ncols], k.hT[:, c, ch * 512:(ch + 1) * 512],
                 start=(c == 0), stop=(c == 7), reads=[bw] + hb, writes=[bp])
        dst_fn(ch, k.ps[bank][:, :], bp)


def short_conv(k, raw, t1, dst, ti, braw, bt1, bdst, eng='vector'):
    w0, w1, w2, bb = [k.c_cw[:, ti * 4 + i: ti * 4 + i + 1] for i in range(4)]
    k.act(t1, raw[:, 1:S + 1], ACT.Identity, reads=[braw, k.KC], writes=[bt1], scale=w1, bias=bb)
    k.stt(eng, t1, raw[:, 0:S], w0, t1, ALU.mult, ALU.add, reads=[braw, bt1, k.KC], writes=[bt1])
    k.stt(eng, dst, raw[:, 2:S + 2], w2, t1, ALU.mult, ALU.add, reads=[braw, bt1, k.KC], writes=[bdst])


def fft_fwd_batch(k, d0_groups, pslots, a_slots, tag):
    P_reg, bP = pslots
    for gi, (lh, bd) in enumerate(d0_groups):
        pa, ba = a_slots[gi]
        k.mm(pa, lh, k.c_F1, reads=[bd, k.KC], writes=[ba])
    for gi, (lh, bd) in enumerate(d0_groups):
        pa, ba = a_slots[gi]
        o = V(P_reg, gi * 128, [[2 * 512, 2], [512, 2], [1, 128]])
        i0 = bass.AP(pa.tensor, pa.offset, [list(pa.ap[0]), [0, 2], [128, 2], [1, 128]])
        i1 = V(k.c_Tw1, 0, [[128, 2], [0, 2], [1, 128]])
        k.tt('vector', o, i0, i1, ALU.mult, reads=[ba, k.KC], writes=[bP])


def phaseH(k):
    B = k.B
    mixT = k.bfr(MIX0, 8 * S).rearrange("p (c t) -> p c t", c=8)
    k.mixT = mixT
    U = k.bfr(HW0, S)
    G = k.bfr(HW0 + 8 * KB, 2 * 32 * 128)
    R = k.bfr(HW0 + 24 * KB, 4 * 32 * 128)
    D0F = k.bfr(HW0 + 24 * KB, 2 * 32 * 128)
    PF = [k.bfr(HW0 + 40 * KB + i * 4 * KB, 2048) for i in range(4)]
    X0 = HW0 + 56 * KB
    RAWS = [k.f32r(HW0 + 24 * KB, S + 4), k.f32r(X0 + KB, S + 4)]
    T1 = k.f32r(HW0 + 24 * KB + 16416, S)
    D0 = k.bfr(X0, 32 * 128)
    PU = [k.bfr(X0 + 8 * KB + i * 4 * KB, 2048) for i in range(2)]
    QS = [k.bfr(X0 + 16 * KB + i * 4 * KB, 2048) for i in range(2)]
    DEC = [k.f32r(X0 + i * 512, 128) for i in range(2)]
    WSL = [k.bfr(HW0 + 8 * KB + i * 2 * KB, 1024).rearrange("p (c n) -> p c n", c=8) for i in range(3)]
    X0C = k.bfr(X0 + 24 * KB, S)
    TMP = [k.f32r(HW0 + 8 * KB + 10 * KB + i * 2 * KB, 512) for i in range(2)]
    hT = k.hT
    ASL = [(k.ps[b][:, h * 256:(h + 1) * 256], B(f'ps{b}')) for b in range(2) for h in range(2)]
    CSm = lambda i: k.c_CS4[:, i * 128:(i + 1) * 128]
    W3m = lambda i: k.c_W3[:, i * 256:(i + 1) * 256]
    PHIm = lambda i: k.c_PHI[:, i * 128:(i + 1) * 128]
    XRE = [0, 2, 2, 1]
    XIM = [3, 0, 0, 2]
    XIMN = [2, 1, 1, 3]
    QW = [0, 2, 2, 1]
    RW = [0, 1, 2, 0]
    braws, bt1, bu, bx0c = [B('hraw0'), B('hraw1')], B('ht1'), B('hu'), B('hx0c')

    for ct in getattr(k, 'dev_cts', range(4)):
        for RAW, braw in zip(RAWS, braws):
            k.mset('vector', RAW[:, 0:1], 0.0, writes=[braw])
            k.mset('vector', RAW[:, S + 1:S + 2], 0.0, writes=[braw])
        cols3 = (512 + ct * 128, 1024 + ct * 128, ct * 128)
        for wi, col in enumerate(cols3):
            load_w(k, WSL[wi], k.w_in[:, col:col + 128], B(f'hw{wi}'))
        for wi, col in enumerate(cols3):
            bw = B(f'hw{wi}')
            RAW, braw = RAWS[wi % 2], braws[wi % 2]

            def evac(ch, pap, bp, RAW=RAW, braw=braw):
                k.act(RAW[:, 1 + ch * 512: 1 + (ch + 1) * 512], pap, ACT.Copy, reads=[bp], writes=[braw])
            proj_fm(k, WSL[wi], 0, 128, evac, bw, f'h{wi}')
            ti = col // 128
            if wi == 0:
                short_conv(k, RAW, T1, U, ti, braw, bt1, bu)
            elif wi == 1:
                short_conv(k, RAW, T1, T1, ti, braw, bt1, bt1)
                k.tt('gpsimd', U, T1, U, ALU.mult, reads=[bt1, bu], writes=[bu])
            else:
                short_conv(k, RAW, T1, X0C, ti, braw, bt1, bx0c)
        k.barrier()
        bd0f = B('hd0f')
        bG = B('hG')
        for j in range(32):
            bank = 6 + j % 2
            bp = B(f'ps{bank}')
            s2 = j % 2
            bdec = B(f'hdec{s2}')
            lh = V(k.h3T, j, [[32, 128]], p0=0, np_=64)
            rh = V(k.c_fw4, ct * 128, [[512, 2], [1, 128]], p0=0, np_=64)
            k.mm(k.ps[bank][:, 0:256], lh, rh, reads=[B('h3T'), k.KCG], writes=[bp])
            k.act(DEC[s2], k.c_absd[:, ct * 128:(ct + 1) * 128], ACT.Exp, reads=[k.KC], writes=[bdec],
                  scale=k.c_negt[:, j:j + 1])
            o = V(D0F, j * 4, [[32 * 128, 2], [128, 32], [1, 4]])
            i0 = bass.AP(k.ps[bank].tensor, k.ps[bank].offset, [list(k.ps[bank].ap[0]), [128, 2], [4, 32], [1, 4]])
            i1 = V(DEC[s2], 0, [[0, 2], [4, 32], [1, 4]])
            k.tt('vector', o, i0, i1, ALU.mult, reads=[bp, bdec], writes=[bd0f])
        def f12(bt):
            pf, pb_ = (PF[(bt % 2) * 2], B(f'hpf{bt % 2}')), (PF[(bt % 2) * 2 + 1], B(f'hpb{bt % 2}'))
            for di, pslot in enumerate((pf, pb_)):
                groups = [(V(D0F, di * 4096 + (bt * 4 + gi) * 128, [[1, 128]]), bd0f) for gi in range(4)]
                fft_fwd_batch(k, groups, pslot, ASL, 'f')

        f12(0)
        for bt in range(8):
            if bt + 1 < 8:
                f12(bt + 1)
            pf, pb_ = (PF[(bt % 2) * 2], B(f'hpf{bt % 2}')), (PF[(bt % 2) * 2 + 1], B(f'hpb{bt % 2}'))
            bre, bim = 2 + (bt % 2) * 2, 3 + (bt % 2) * 2
            bpr, bpi = B(f'ps{bre}'), B(f'ps{bim}')
            for (bank, bp, cf, cb) in ((bre, bpr, XRE, XRE), (bim, bpi, XIM, XIMN)):
                n = 0
                for (preg, bP), coef in ((pf, cf), (pb_, cb)):
                    for comp in range(4):
                        k.mm(k.ps[bank][:, :], CSm(coef[comp]), preg[:, comp * 512:(comp + 1) * 512],
                             start=(n == 0), stop=(n == 7), reads=[bP, k.KC], writes=[bp])
                        n += 1
            for ri, (bank, bp) in enumerate(((bre, bpr), (bim, bpi))):
                k.act(G[:, ri * 4096 + bt * 512: ri * 4096 + (bt + 1) * 512], k.ps[bank][:, :], ACT.Copy,
                      reads=[bp], writes=[bG])
        k.barrier()
        bd0 = B('hd0')
        bR = B('hR')
        for jb in range(4):
            bank = 6 + jb % 2
            bp = B(f'ps{bank}')
            for jj in range(8):
                j = jb * 8 + jj
                k.tr(k.psb[bank][:, jj * 128:(jj + 1) * 128], V(U, j, [[32, 128]]), k.c_ident_bf,
                     reads=[bu, k.KC], writes=[bp])
            o = V(D0, jb * 8 * 4, [[4, 8], [128, 32], [1, 4]])
            i0 = bass.AP(k.psb[bank].tensor, k.psb[bank].offset, [list(k.psb[bank].ap[0]), [128, 8], [4, 32], [1, 4]])
            k.cp('vector', o, i0, reads=[bp], writes=[bd0])
        def u12(bt):
            pu = (PU[bt % 2], B(f'hpu{bt % 2}'))
            groups = [(V(D0, (bt * 4 + gi) * 128, [[1, 128]]), bd0) for gi in range(4)]
            fft_fwd_batch(k, groups, pu, ASL, 'u')

        def u3q(bt):
            pu = (PU[bt % 2], B(f'hpu{bt % 2}'))
            bre, bim = 2 + (bt % 2) * 2, 3 + (bt % 2) * 2
            bpr, bpi = B(f'ps{bre}'), B(f'ps{bim}')
            for (bank, bp, cf) in ((bre, bpr, XRE), (bim, bpi, XIM)):
                for comp in range(4):
                    k.mm(k.ps[bank][:, :], CSm(cf[comp]), pu[0][:, comp * 512:(comp + 1) * 512],
                         start=(comp == 0), stop=(comp == 3), reads=[pu[1], k.KC], writes=[bp])
            qreg, bq = QS[bt % 2], B(f'hq{bt % 2}')
            gsl = V(G, bt * 512, [[4096, 2], [1, 512]])
            for half, (bank, bp) in enumerate(((bre, bpr), (bim, bpi))):
                if half == 0:
                    o = V(qreg, 0, [[512, 2], [1, 512]])
                else:
                    o = V(qreg, 1024, [[512, 2], [1, 512]])
                i0 = bass.AP(k.ps[bank].tensor, k.ps[bank].offset, [list(k.ps[bank].ap[0]), [0, 2], [1, 512]])
                k.tt('vector', o, i0, gsl, ALU.mult, reads=[bp, bG], writes=[bq])
        def u3p(bt):
            qreg, bq = QS[bt % 2], B(f'hq{bt % 2}')
            for gi in range(4):
                pa, ba = ASL[gi]
                for comp in range(4):
                    k.mm(pa, qreg[:, comp * 512 + gi * 128: comp * 512 + (gi + 1) * 128], W3m(QW[comp]),
                         start=(comp == 0), stop=(comp == 3), reads=[bq, k.KC], writes=[ba], skip_group_check=True)
            for gi in range(4):
                pa, ba = ASL[gi]
                g = bt * 4 + gi
                for a in range(2):
                    o = V(R, (a * 2) * 4096 + g * 4, [[4096, 2], [128, 32], [1, 4]])
                    i0 = bass.AP(pa.tensor, pa.offset, [list(pa.ap[0]), [128, 2], [4, 32], [1, 4]])
                    i1 = V(k.c_Tw2, a * 32, [[0, 2], [1, 32], [0, 4]])
                    k.tt('vector', o, i0, i1, ALU.mult, reads=[ba, k.KC], writes=[bR])

        u12(0)
        for bt in range(9):
            if bt + 1 < 8:
                u12(bt + 1)
            if bt < 8:
                u3q(bt)
            if bt >= 1:
                u3p(bt - 1)
        k.barrier()
        bmix = B(f'mix{ct}')
        for jb in range(8):
            bank = 4 + jb % 2
            bp = B(f'ps{bank}')
            for jj in range(4):
                j = jb * 4 + jj
                for comp in range(4):
                    k.mm(k.ps[bank][:, jj * 128:(jj + 1) * 128], R[:, comp * 4096 + j * 128: comp * 4096 + (j + 1) * 128],
                         PHIm(RW[comp]), start=(comp == 0), stop=(comp == 3), reads=[bR, k.KC], writes=[bp],
                         skip_group_check=True)
            tmp, btmp = TMP[jb % 2], B(f'htmp{jb % 2}')
            uv = V(U, jb * 4, [[32, 128], [1, 4]])
            pv = bass.AP(k.ps[bank].tensor, k.ps[bank].offset, [list(k.ps[bank].ap[0]), [1, 128], [128, 4]])
            tv = V(tmp, 0, [[4, 128], [1, 4]])
            k.stt('vector', tv, uv, k.c_fbias[:, ct:ct + 1], pv, ALU.mult, ALU.add, reads=[bu, bp, k.KC], writes=[btmp])
            xv = V(X0C, jb * 4, [[32, 128], [1, 4]])
            mv = bass.AP(mixT.tensor, mixT.offset + ct * S + jb * 4, [list(mixT.ap[0]), [32, 128], [1, 4]])
            k.tt('gpsimd', mv, tv, xv, ALU.mult, reads=[btmp, bx0c], writes=[bmix])
        k.barrier()


def phaseA(k):
    B = k.B
    hT, mixT = k.hT, k.mixT
    W = WORK0
    QT = k.bfr(W, S)
    KT = k.bfr(W + 8 * KB, S)
    VA = k.bfr(W + 16 * KB, 32 * 130).rearrange("p (s e) -> p s e", s=32)
    RC = [k.f32r(W + 25 * KB + i * 2 * KB, 512) for i in range(2)]
    RS = [k.f32r(W + 29 * KB + i * 2 * KB, 512) for i in range(2)]
    WSL = [k.bfr(W + 33 * KB + i * 2 * KB, 1024).rearrange("p (c n) -> p c n", c=8) for i in range(5)]
    PT = [k.bfr(W + 43 * KB + i * KB, 512) for i in range(4)]
    PTALL = k.bfr(W + 43 * KB, 2048)
    TA = [k.f32r(W + 47 * KB + i * 2 * KB, 512) for i in range(4)]
    OSB = [k.f32r(W + 55 * KB + i * 512, 128) for i in range(2)]
    YN = [k.bfr(W + 56 * KB + i * 256, 128) for i in range(2)]
    JK = k.bfr(W + 57 * KB, 128)
    lam = k.c_lam
    bl = B('lam')
    lv = k.c_lamv
    k.stt('vector', JK[:, 0:64], lv[:, 0:64], 1.0, lv[:, 64:128], ALU.mult, ALU.mult, reads=[k.KC], writes=[bl, B('ajk')],
          accum_out=lam[:, 0:1])
    k.stt('vector', JK[:, 0:64], lv[:, 128:192], 1.0, lv[:, 192:256], ALU.mult, ALU.mult, reads=[k.KC], writes=[bl, B('ajk')],
          accum_out=lam[:, 1:2])
    k.act(lam[:, 2:4], lam[:, 0:2], ACT.Exp, reads=[bl], writes=[bl])
    k.tt('vector', lam[:, 4:5], lam[:, 2:3], lam[:, 3:4], ALU.subtract, reads=[bl], writes=[bl])
    k.ts('vector', lam[:, 5:6], lam[:, 4:5], 0.2, -1.0, ALU.add, ALU.mult, reads=[bl], writes=[bl])
    neglam = lam[:, 5:6]
    bQ, bK, bV = B('aQT'), B('aKT'), B('aVA')

    def oreg(r):
        return k.ps[4 + r // 3][:, (r % 3) * 129:(r % 3) * 129 + 129], B(f'ps{4 + r // 3}')

    for hd in getattr(k, 'dev_heads', range(4)):
        srcs = [k.w_in[:, 1536 + hd * 128: 1536 + (hd + 1) * 128], k.w_qkp[:, hd * 128:(hd + 1) * 128],
                k.w_in[:, 2048 + hd * 128: 2048 + (hd + 1) * 128], k.w_qkp[:, 512 + hd * 128: 512 + (hd + 1) * 128],
                k.w_in[:, 2560 + hd * 128: 2560 + (hd + 1) * 128]]
        bws = [B(f'aw{i}') for i in range(5)]
        for i in range(5):
            load_w(k, WSL[i], srcs[i], bws[i])
        k.mset('vector', VA[:, :, 128:129], 1.0, writes=[bV])
        for sb in range(8):
            bank = 6 + sb % 2
            bp = B(f'ps{bank}')
            for s4 in range(4):
                st = sb * 4 + s4
                for c in range(8):
                    k.mm(k.ps[bank][:, s4 * 128:(s4 + 1) * 128], hT[:, c, st * 128:(st + 1) * 128], WSL[4][:, c, :],
                         start=(c == 0), stop=(c == 7), reads=[bws[4], B(f'hT{st}')], writes=[bp], skip_group_check=True)
            pin = k.ps[bank][:, :].rearrange("p (s e) -> p s e", s=4)
            k.act(VA[:, sb * 4:(sb + 1) * 4, 0:128], pin, ACT.Copy, reads=[bp], writes=[bV])
        for ch in range(8):
            s2 = ch % 2
            brc, brs = B(f'arc{s2}'), B(f'ars{s2}')
            k.dma('sync', RC[s2], k.t_ropec[:, ch * 512:(ch + 1) * 512], writes=[brc])
            k.dma('sync', RS[s2], k.t_ropes[:, ch * 512:(ch + 1) * 512], writes=[brs])
            hb = [B(f'hT{i}') for i in range(ch * 4, ch * 4 + 4)]
            for wi in range(4):
                bank = 2 + wi
                for c in range(8):
                    k.mm(k.ps[bank][:, :], WSL[wi][:, c, :], hT[:, c, ch * 512:(ch + 1) * 512],
                         start=(c == 0), stop=(c == 7), reads=[bws[wi]] + hb, writes=[B(f'ps{bank}')])
            for qi, (dst, bdst, sc) in enumerate(((QT, bQ, 0.125), (KT, bK, 1.0))):
                ta, tb = TA[qi * 2], TA[qi * 2 + 1]
                bta, btb = B(f'ata{qi}'), B(f'atb{qi}')
                k.stt('vector', ta, k.ps[2 + qi * 2][:, :], sc, RC[s2], ALU.mult, ALU.mult,
                      reads=[B(f'ps{2 + qi * 2}'), brc], writes=[bta])
                k.stt('vector', tb, k.ps[3 + qi * 2][:, :], sc, RS[s2], ALU.mult, ALU.mult,
                      reads=[B(f'ps{3 + qi * 2}'), brs], writes=[btb])
                k.tt('gpsimd', dst[:, ch * 512:(ch + 1) * 512], ta, tb, ALU.add, reads=[bta, btb], writes=[bdst])
        def qk_exp(qc, st):
            par = st % 2
            for c in range(2):
                bank = par * 2 + c
                k.mm(k.ps[bank][:, :], KT[c * 64:(c + 1) * 64, st * 128:(st + 1) * 128],
                     QT[c * 64:(c + 1) * 64, qc * 512:(qc + 1) * 512], reads=[bK, bQ], writes=[B(f'ps{bank}')])
            for c in range(2):
                bank = par * 2 + c
                k.act(PT[par * 2 + c], k.ps[bank][:, :], ACT.Exp, reads=[B(f'ps{bank}')], writes=[B(f'apt{par * 2 + c}')])

        for qc in range(8):
            qk_exp(qc, 0)
            for st in range(32):
                par = st % 2
                if st + 1 < 32:
                    qk_exp(qc, st + 1)
                for c in range(2):
                    for qs in range(4):
                        r = c * 4 + qs
                        oap, bo = oreg(r)
                        k.mm(oap, PT[par * 2 + c][:, qs * 128:(qs + 1) * 128], VA[:, st, 0:129],
                             start=(st == 0 and r % 3 == 0), stop=(st == 31),
                             reads=[B(f'apt{par * 2 + c}'), bV], writes=[bo], skip_group_check=True)
            for qs in range(4):
                o0, bo0 = oreg(qs)
                o1, bo1 = oreg(4 + qs)
                s2 = qs % 2
                cols = k.c_cols[:, 64 + s2 * 16: 64 + s2 * 16 + 16]
                bc = B(f'acol{s2}')
                osb, bos = OSB[s2], B(f'aosb{s2}')
                yn, byn = YN[s2], B(f'ayn{s2}')
                P_ = k.P
                P_.op('vector', lambda e, a=cols[:, 0:1], b=o0[:, 128:129]: e.reciprocal(out=a, in_=b), [bo0], [bc])
                P_.op('vector', lambda e, a=cols[:, 1:2], b=o1[:, 128:129]: e.reciprocal(out=a, in_=b), [bo1], [bc])
                k.tt('vector', cols[:, 2:3], cols[:, 1:2], neglam, ALU.mult, reads=[bc, bl], writes=[bc])
                k.ts('vector', osb, o0[:, 0:128], cols[:, 0:1], None, ALU.mult, reads=[bo0, bc], writes=[bos])
                k.stt('vector', osb, o1[:, 0:128], cols[:, 2:3], osb, ALU.mult, ALU.add, reads=[bo1, bc, bos], writes=[bos])
                k.stt('vector', JK, osb, 1.0, osb, ALU.mult, ALU.mult, reads=[bos], writes=[bc, B('ajk')], accum_out=cols[:, 3:4])
                rstd_from_ss(k, cols[:, 3:4], cols[:, 4:5], cols[:, 5:6], cols[:, 6:7], 128.0, 1e-5, [bc])
                k.ts('vector', yn, osb, cols[:, 6:7], 0.8, ALU.mult, ALU.mult, reads=[bos, bc], writes=[byn])
                k.tr(k.psb[7][:, 0:128], yn, k.c_ident_bf, reads=[byn, k.KC], writes=[B('ps7')])
                q0 = qc * 512 + qs * 128
                k.act(mixT[:, 4 + hd, q0:q0 + 128], k.psb[7][:, 0:128], ACT.Identity, reads=[B('ps7'), k.KC],
                      writes=[B(f'mix{4 + hd}')], scale=k.c_subg[:, 0:1])


def load_mlp_w(k):
    B = k.B
    WUP = k.bfr(HT0, 8 * 4096).rearrange("p (c n) -> p c n", c=8)
    k.WUP = WUP
    for c in range(8):
        k.dma('gpsimd', WUP[:, c, :], k.w_up[c * 128:(c + 1) * 128, :], writes=[B(f'wup{c}')])


def phaseO(k):
    B = k.B
    mixT = k.mixT
    W = WORK0
    WO = k.bfr(W, 8 * 1024).rearrange("p (c n) -> p c n", c=8)
    GP = k.f32r(W + 16 * KB, 1024)
    XT = [k.f32r(W + 20 * KB + i * 4 * KB, 1024) for i in range(3)]
    MS = [k.f32r(W + 32 * KB + i * 4 * KB, 1024) for i in range(2)]
    TMP = [k.f32r(W + 40 * KB + i * 4 * KB, 1024) for i in range(2)]
    JK = k.bfr(W + 48 * KB, 1024)
    bwo, bgp = B('owo'), B('ogp')
    for half in range(2):
        k.dma('gpsimd', WO[:, :, half * 512:(half + 1) * 512],
              k.w_out[:, half * 512:(half + 1) * 512].rearrange("(c p) n -> p c n", p=128), writes=[bwo])
    k.dma('sync', GP, bass.AP(k.gpost.tensor, 0, [[0, 128], [1, 1024]]), writes=[bgp])
    load_mlp_w(k)
    mixb = [B(f'mix{c}') for c in range(8)]
    def stage_a(i):
        s3, s2 = i % 3, i % 2
        xt, bx = XT[s3], B(f'ox{s3}')
        ms, bms = MS[s2], B(f'oms{s2}')
        k.dma('sync', xt, k.x[i * 128:(i + 1) * 128, :], writes=[bx])
        for half in range(2):
            bank = 2 * s2 + half
            bp = B(f'ps{bank}')
            for c in range(8):
                k.mm(k.ps[bank][:, :], mixT[:, c, i * 128:(i + 1) * 128], WO[:, c, half * 512:(half + 1) * 512],
                     start=(c == 0), stop=(c == 7), reads=[bwo] + mixb, writes=[bp])
            k.act(ms[:, half * 512:(half + 1) * 512], k.ps[bank][:, :], ACT.Copy, reads=[bp], writes=[bms])

    def stage_b(i):
        s3, s2 = i % 3, i % 2
        xt, bx = XT[s3], B(f'ox{s3}')
        ms, bms = MS[s2], B(f'oms{s2}')
        tmp, btmp = TMP[s2], B(f'otmp{s2}')
        cols = k.c_cols[:, 96 + s2 * 8: 96 + s2 * 8 + 8]
        bc = B(f'ocol{s2}')
        k.stt('vector', JK, ms, 1.0, ms, ALU.mult, ALU.mult, reads=[bms], writes=[bc, B('ojk')], accum_out=cols[:, 0:1])
        rstd_from_ss(k, cols[:, 0:1], cols[:, 1:2], cols[:, 2:3], cols[:, 3:4], 1024.0, 1e-6, [bc])
        k.stt('vector', tmp, ms, cols[:, 3:4], GP, ALU.mult, ALU.mult, reads=[bms, bc, bgp], writes=[btmp])
        k.tt('gpsimd', tmp, tmp, xt, ALU.add, reads=[btmp, bx], writes=[btmp])
        k.dma('sync', k.xa_scr[i * 128:(i + 1) * 128, :], tmp, reads=[btmp], writes=[B(f'xa{i}')], sembuf=btmp)

    stage_a(0)
    for i in range(32):
        if i + 1 < 32:
            stage_a(i + 1)
        stage_b(i)


def phaseM(k):
    B = k.B
    WUP = k.WUP
    WDN = k.bfr(MIX0, 32 * 1024).rearrange("p (f n) -> p f n", f=32)
    for f0 in range(0, 32, 4):
        k.dma('gpsimd', WDN[:, f0:f0 + 4, :],
              k.w_down[f0 * 128:(f0 + 4) * 128, :].rearrange("(f p) n -> p f n", p=128), writes=[B(f'wdn{f0 // 4}')])
    W = WORK0
    GP = k.f32r(W, 1024)
    XA = [[k.f32r(W + 4 * KB + (a * 2 + b) * 4 * KB, 1024) for b in range(2)] for a in range(2)]
    XN = [k.bfr(W + 20 * KB + i * 2 * KB, 1024) for i in range(2)]
    H2T = [k.bfr(W + 24 * KB + i * 4 * KB, 8 * 256).rearrange("p (c t) -> p c t", c=8) for i in range(2)]
    UPT = k.bfr(W + 32 * KB, 32 * 256).rearrange("p (f t) -> p f t", f=32)
    RL = [k.f32r(W + 48 * KB + i * KB, 256) for i in range(2)]
    MS = [k.f32r(W + 50 * KB + i * 4 * KB, 1024) for i in range(2)]
    JK = k.bfr(W + 58 * KB, 1024)
    bgp = B('mgp')
    k.dma('sync', GP, bass.AP(k.gpost.tensor, 1024, [[0, 128], [1, 1024]]), writes=[bgp])
    gb = V(k.c_gcols, 8, [[1, 8], [0, 128]])
    bwup, bwdn, bupt = B('wup'), B('wdn'), B('mupt')
    k.out_ticks = []
    nck = getattr(k, 'dev_nck', 16)

    def prologue(ck):
        a = ck % 2
        bh2 = B(f'mh2{a}')
        for tl in range(2):
            i = ck * 2 + tl
            xa, bxa = XA[a][tl], B(f'mxa{a}{tl}')
            xn, bxn = XN[tl], B(f'mxn{tl}')
            cols = k.c_cols[:, 128 + tl * 8: 128 + tl * 8 + 8]
            bc = B(f'mcol{tl}')
            k.dma('sync', xa, k.xa_scr[i * 128:(i + 1) * 128, :], reads=[B(f'xa{i}')], writes=[bxa])
            k.stt('vector', JK, xa, 1.0, xa, ALU.mult, ALU.mult, reads=[bxa], writes=[bc, B('mjk')], accum_out=cols[:, 0:1])
            rstd_from_ss(k, cols[:, 0:1], cols[:, 1:2], cols[:, 2:3], cols[:, 3:4], 1024.0, 1e-6, [bc])
            k.act(xn, xa, ACT.Identity, reads=[bxa, bc], writes=[bxn], scale=cols[:, 3:4])
            bank = tl
            bp = B(f'ps{bank}')
            for c in range(8):
                k.tr(k.psb[bank][:, c * 128:(c + 1) * 128], xn[:, c * 128:(c + 1) * 128], k.c_ident_bf,
                     reads=[bxn, k.KC], writes=[bp])
            pin = k.psb[bank][:, 0:1024].rearrange("p (c t) -> p c t", c=8)
            k.tt('vector', H2T[a][:, :, tl * 128:(tl + 1) * 128], pin, gb, ALU.mult, reads=[bp, k.KC], writes=[bh2])

    prologue(0)
    for ck in range(nck):
        a = ck % 2
        bh2 = B(f'mh2{a}')
        for f in range(32):
            bank = 2 + f % 2
            bp = B(f'ps{bank}')
            rl, brl = RL[f % 2], B(f'mrl{f % 2}')
            for c in range(8):
                k.mm(k.ps[bank][:, 0:256], WUP[:, c, f * 128:(f + 1) * 128], H2T[a][:, c, :],
                     start=(c == 0), stop=(c == 7), reads=[B(f'wup{c}'), bh2], writes=[bp])
            k.act(rl, k.ps[bank][:, 0:256], ACT.Relu, reads=[bp], writes=[brl])
            k.tt('gpsimd', UPT[:, f, :], rl, rl, ALU.mult, reads=[brl], writes=[bupt])
        if ck + 1 < nck:
            prologue(ck + 1)
        for tl in range(2):
            i = ck * 2 + tl
            xa, bxa = XA[a][tl], B(f'mxa{a}{tl}')
            ms, bms = MS[tl], B(f'mms{tl}')
            cols = k.c_cols[:, 144 + tl * 8: 144 + tl * 8 + 8]
            bc = B(f'mcol2{tl}')
            for half in range(2):
                bank = 4 + tl * 2 + half
                bp = B(f'ps{bank}')
                for f in range(32):
                    k.mm(k.ps[bank][:, :], UPT[:, f, tl * 128:(tl + 1) * 128], WDN[:, f, half * 512:(half + 1) * 512],
                         start=(f == 0), stop=(f == 31), reads=[bupt, B(f'wdn{f // 4}')], writes=[bp])
                k.act(ms[:, half * 512:(half + 1) * 512], k.ps[bank][:, :], ACT.Copy, reads=[bp], writes=[bms])
            k.stt('vector', JK, ms, 1.0, ms, ALU.mult, ALU.mult, reads=[bms], writes=[bc, B('mjk')], accum_out=cols[:, 0:1])
            rstd_from_ss(k, cols[:, 0:1], cols[:, 1:2], cols[:, 2:3], cols[:, 3:4], 1024.0, 1e-6, [bc])
            k.stt('vector', ms, ms, cols[:, 3:4], GP, ALU.mult, ALU.mult, reads=[bms, bc, bgp], writes=[bms])
            k.tt('gpsimd', ms, ms, xa, ALU.add, reads=[bms, bxa], writes=[bms])
            k.out_ticks.append(k.dma('sync', k.out[i * 128:(i + 1) * 128, :], ms, reads=[bms], sembuf=bms))


def host_inputs(inputs):
    T = host_tables()

    def g(k):
        return np.asarray(inputs[k], dtype=np.float32)
    w_in = np.ascontiguousarray(g('w_in')[0])
    qk = w_in[:, 1536:2560]
    idx = np.arange(1024)
    d = idx % 64
    perm = (idx - d) + (d + 32) % 64
    sh = {}
    sh['w_in'] = w_in
    sh['w_qkp'] = np.ascontiguousarray(qk[:, perm])
    cwb = np.concatenate([g('conv_w')[0], g('conv_b')[0][None, :]], axis=0)
    sh['cw'] = np.ascontiguousarray(cwb.reshape(4, 12, 128).transpose(2, 1, 0).reshape(128, 48))
    sh['fw1'] = np.ascontiguousarray(g('filt_w1')[0])
    sh['fw2'] = np.ascontiguousarray(g('filt_w2')[0])
    sh['fw3'] = np.ascontiguousarray(g('filt_w3')[0])
    sh['fw4'] = np.ascontiguousarray(g('filt_w4')[0])
    sh['fcol'] = np.ascontiguousarray(np.stack([g('filt_b1')[0], g('filt_b2')[0], g('filt_b3')[0], g('filt_freq')[0]], axis=1))
    sh['fbias'] = np.ascontiguousarray(g('filt_bias')[0].reshape(4, 128).T)
    sh['lamv'] = np.concatenate([g('lam_q1')[0], g('lam_k1')[0], g('lam_q2')[0], g('lam_k2')[0]])[None, :].copy()
    sh['subg'] = np.ascontiguousarray(g('subln_gain')[0][:, None])
    sh['w_out'] = np.ascontiguousarray(g('w_out')[0])
    sh['w_up'] = np.ascontiguousarray(g('w_up')[0])
    sh['w_down'] = np.ascontiguousarray(g('w_down')[0])
    sh['gcols'] = np.ascontiguousarray(np.stack([g('attn_pre_gain')[0].reshape(8, 128).T,
                                                 g('mlp_pre_gain')[0].reshape(8, 128).T], axis=1).reshape(128, 16))
    sh['gpost'] = np.ascontiguousarray(np.stack([g('attn_post_gain')[0], g('mlp_post_gain')[0]], axis=0))
    sh['ident_bf'] = T['ident_bf']
    sh['ident_f'] = T['ident_f']
    sh['F1'] = T['F1']
    sh['Tw1'] = T['Tw1'].reshape(128, 256)
    sh['CS4'] = T['CS4'].reshape(128, 512)
    sh['W3'] = T['W3'].reshape(128, 768)
    sh['Tw2'] = T['Tw2'].reshape(128, 64)
    sh['PHI'] = T['PHI'].reshape(128, 384)
    sh['zT'] = T['zT']
    sh['negt'] = T['negt']
    sh['absdelta'] = T['absdelta']
    sh['ropec'] = T['ropec']
    sh['ropes'] = T['ropes']
    return sh


PHASES = ['p0', 'pF', 'pH', 'pA', 'pO', 'pM']


def make_program(upto='pM', dumps=(), dump_fn=None, small_out=False, **attrs):
    k = K(dumps=dumps, small_out=small_out)
    k.__dict__.update(attrs)
    fns = {'p0': phase0, 'pF': phaseF, 'pH': phaseH, 'pA': phaseA, 'pO': phaseO, 'pM': phaseM}
    for ph in PHASES:
        fns[ph](k)
        k.barrier()
        if ph == upto:
            break
    ticks = []
    if dump_fn is not None:
        ticks += dump_fn(k)
    ticks += getattr(k, 'out_ticks', [])
    k.P.wait_all('sync', ticks)
    with k.nc.Block() as block:
        k.P.emit(block)
    k.es.close()
    return k.nc


def kernel(**inputs):
    sh = host_inputs(inputs)
    x = np.asarray(inputs['x'], dtype=np.float32)
    nc = make_program()
    in_maps = []
    for c in range(8):
        m = dict(sh)
        m['x'] = np.ascontiguousarray(x[c])
        in_maps.append(m)
    res = run_bass_kernel_spmd(nc, in_maps, core_ids=list(range(8)))
    return np.stack([np.asarray(r['out'], dtype=np.float32) for r in res.results], axis=0)
```

```python
import numpy as np
import concourse.bass as bass
import concourse.mybir as mybir
from concourse.bass_utils import run_bass_kernel_spmd
from contextlib import ExitStack

F32 = mybir.dt.float32
BF16 = mybir.dt.bfloat16
ALU = mybir.AluOpType
ACT = mybir.ActivationFunctionType
AX = mybir.AxisListType

ENGS = ('tensor', 'vector', 'scalar', 'gpsimd', 'sync')
SEM_ROT = 16000


class Buf:
    __slots__ = ('name', 'w', 'r', 'dsem', 'excl')

    def __init__(self, name, excl=False):
        self.name = name
        self.excl = excl
        self.w = None
        self.r = {}
        self.dsem = None


class Prog:
    def __init__(self, nc, sems):
        self.nc = nc
        self.sems = sems
        self.nsem = 0
        self.q = {e: [] for e in ENGS}
        self.cnt = {e: 0 for e in ENGS}
        self.esem = {}
        self.owned = {e: set() for e in ENGS}
        self.seen = {e: {} for e in ENGS}
        self.dcnt = {}
        self.final = {}
        for e in ENGS:
            self._new_esem(e)

    def alloc_sem(self):
        i = self.nsem
        self.nsem += 1
        assert i < len(self.sems), "out of semaphores"
        return i

    def _new_esem(self, e):
        i = self.alloc_sem()
        self.esem[e] = i
        self.owned[e].add(i)
        self.cnt[e] = 0

    def _deps(self, eng, reads, writes, is_dma=False):
        need = {}

        def add(t, raw=False):
            if t is None:
                return
            s, v = t
            if s in self.owned[eng] and not is_dma and not (raw and eng != 'tensor'):
                return
            if need.get(s, 0) < v:
                need[s] = v
        for b in reads:
            add(b.w, raw=True)
        for b in writes:
            add(b.w)
            for s, v in b.r.items():
                add((s, v))
        out = []
        seen = self.seen[eng]
        for s, v in need.items():
            if seen.get(s, 0) < v:
                seen[s] = v
                out.append((s, v))
        return out

    def op(self, eng, fn, reads=(), writes=()):
        if any(b.excl for b in reads):
            writes = list(writes) + [b for b in reads if b.excl]
            reads = [b for b in reads if not b.excl]
        waits = self._deps(eng, reads, writes)
        if self.cnt[eng] >= SEM_ROT:
            self._new_esem(eng)
        self.cnt[eng] += 1
        s, v = self.esem[eng], self.cnt[eng]
        for b in reads:
            if b.r.get(s, 0) < v:
                b.r[s] = v
        for b in writes:
            b.w = (s, v)
            b.r = {}
        self.q[eng].append((waits, fn, s, 1))

    def dma(self, qeng, fn, reads=(), writes=(), sembuf=None):
        waits = self._deps(qeng, reads, writes, is_dma=True)
        sb = sembuf if sembuf is not None else (writes[0] if writes else reads[0])
        if sb.dsem is None or self.dcnt[sb.dsem] >= SEM_ROT:
            sb.dsem = self.alloc_sem()
            self.dcnt[sb.dsem] = 0
        s = sb.dsem
        self.dcnt[s] += 16
        v = self.dcnt[s]
        for b in reads:
            if b.r.get(s, 0) < v:
                b.r[s] = v
        for b in writes:
            b.w = (s, v)
            b.r = {}
        self.q[qeng].append((waits, fn, s, 16))
        return (s, v)

    def wait_all(self, eng, ticks):
        need = {}
        for s, v in ticks:
            if need.get(s, 0) < v:
                need[s] = v
        self.q[eng].append((list(need.items()), None, None, 0))

    def barrier(self):
        ticks = [(self.esem[o], self.cnt[o]) for o in ENGS if self.cnt[o] > 0]
        ticks += [(s, v) for s, v in self.dcnt.items()]
        for e in ENGS:
            need = []
            for s, v in ticks:
                if s in self.owned[e] and e != 'gpsimd':
                    continue
                if self.seen[e].get(s, 0) < v:
                    self.seen[e][s] = v
                    need.append((s, v))
            if need:
                self.q[e].append((need, None, None, 0))

    def replay(self, engobj, eng):
        sems = self.sems
        for waits, fn, s, inc in self.q[eng]:
            for ws, wv in waits:
                engobj.wait_ge(sems[ws], wv)
            if fn is not None:
                inst = fn(engobj)
                inst.then_inc(sems[s], inc)

    def emit(self, block):
        @block.tensor
        def _(e):
            self.replay(e, 'tensor')

        @block.vector
        def _(e):
            self.replay(e, 'vector')

        @block.scalar
        def _(e):
            self.replay(e, 'scalar')

        @block.gpsimd
        def _(e):
            self.replay(e, 'gpsimd')

        @block.sync
        def _(e):
            self.replay(e, 'sync')


def AP(base, off, dims):
    return bass.AP(base.tensor, off, [list(d) for d in dims])


S = 4096
D = 1024
NFFT = 8192
PI = float(np.pi)


def host_tables():
    import ml_dtypes
    bf = ml_dtypes.bfloat16
    T = {}
    T['ident_bf'] = np.eye(128, dtype=np.float32).astype(bf)
    T['ident_f'] = np.eye(128, dtype=np.float32)
    p = np.arange(128, dtype=np.float64)
    k1 = np.arange(128, dtype=np.float64)
    j = np.arange(32, dtype=np.float64)
    k2 = np.arange(32, dtype=np.float64)
    ang = 2 * np.pi * (k1[None, :] + 0.5) * p[:, None] / 256.0
    T['F1'] = np.concatenate([np.cos(ang), -np.sin(ang)], axis=1).astype(np.float32).astype(bf)
    jj = np.repeat(j, 4)
    a1 = 2 * np.pi * (k1[None, :] + 0.5) * jj[:, None] / NFFT
    T['Tw1'] = np.stack([np.cos(a1), -np.sin(a1)], axis=1).astype(np.float32)
    th = 2 * np.pi * np.outer(j, k2) / 32.0
    I4 = np.eye(4)
    C = np.kron(np.cos(th), I4)
    Sm = np.kron(np.sin(th), I4)
    T['CS4'] = np.stack([C, -C, Sm, -Sm], axis=1).astype(np.float32).astype(bf)
    W1 = np.concatenate([C, Sm], axis=1)
    W2 = np.concatenate([-C, -Sm], axis=1)
    W34 = np.concatenate([-Sm, C], axis=1)
    T['W3'] = np.stack([W1, W2, W34], axis=1).astype(np.float32).astype(bf)
    a2 = 2 * np.pi * (k1[:, None] + 0.5) * j[None, :] / NFFT
    T['Tw2'] = np.stack([np.cos(a2), -np.sin(a2)], axis=1).astype(np.float32)
    ph = 2 * np.pi * (k1[:, None] + 0.5) * p[None, :] / 256.0
    cphi = (2.0 / NFFT) * np.cos(ph)
    sphi = (2.0 / NFFT) * np.sin(ph)
    T['PHI'] = np.stack([cphi, -sphi, sphi], axis=1).astype(np.float32).astype(bf)
    bands = 16
    t = np.linspace(0.0, 1.0, S, dtype=np.float32)[:, None]
    w = (np.float32(2.0 * np.pi / S) * np.arange(S, dtype=np.float32))[:, None]
    f = np.linspace(1e-4, bands - 1, bands, dtype=np.float32)[None, :]
    fw_ = (f * w).astype(np.float32)
    z = np.concatenate([t, np.cos(fw_), -np.sin(fw_)], axis=-1).astype(np.float32)
    T['zT'] = np.ascontiguousarray(z.T)
    tt = np.linspace(0.0, 1.0, S, dtype=np.float32).reshape(128, 32)
    T['negt'] = (-tt).astype(np.float32)
    import math
    max_decay = math.log(1e-2) / 0.3
    min_decay = math.log(1e-2) / 1.5
    deltas = np.abs(np.linspace(min_decay, max_decay, 512, dtype=np.float32))
    T['absdelta'] = np.ascontiguousarray(np.broadcast_to(deltas[None, :], (128, 512))).astype(np.float32)
    inv = (10000.0 ** (-np.arange(0, 64, 2, dtype=np.float32) / 64)).astype(np.float32)
    angr = (np.arange(S, dtype=np.float32)[:, None] * inv[None, :]).astype(np.float32)
    angr = np.concatenate([angr, angr], axis=-1)
    cosr = np.cos(angr).astype(np.float32)
    sinr = np.sin(angr).astype(np.float32)
    sgn = np.concatenate([-np.ones(32, np.float32), np.ones(32, np.float32)])
    sins = sinr * sgn[None, :]
    T['ropec'] = np.ascontiguousarray(np.concatenate([cosr.T, cosr.T], axis=0))
    T['ropes'] = np.ascontiguousarray(np.concatenate([sins.T, sins.T], axis=0))
    return T


KB = 1024
CONST0 = 0
HT0 = 14 * KB
MIX0 = 78 * KB
WORK0 = 142 * KB
HW0 = 118 * KB
ARENA_F32 = 53100


class Ctx:
    pass


def V(region, off, dims, p0=0, np_=128):
    pstep = region.ap[0][0]
    return bass.AP(region.tensor, region.offset + p0 * pstep + off,
                   [[pstep, np_]] + [list(d) for d in dims])


class K:
  def __init__(self, dumps=(), small_out=False):
    nc = bass.Bass("TRN2", target_bir_lowering=False)

    def din(name, shape, dt=F32):
        return nc.dram_tensor(name, list(shape), dt, kind="ExternalInput").ap()

    x = din('x', [S, D])
    w_in = din('w_in', [D, 3072])
    w_qkp = din('w_qkp', [D, 1024])
    cw = din('cw', [128, 48])
    fw1 = din('fw1', [33, 64])
    fw2 = din('fw2', [64, 64])
    fw3 = din('fw3', [64, 64])
    fw4 = din('fw4', [64, 1024])
    fcol = din('fcol', [64, 4])
    fbias = din('fbias', [128, 4])
    lamv = din('lamv', [1, 256])
    subg = din('subg', [128, 1])
    w_out = din('w_out', [D, D])
    w_up = din('w_up', [D, 4096])
    w_down = din('w_down', [4096, D])
    gcols = din('gcols', [128, 16])
    gpost = din('gpost', [2, 1024])
    t_ident_bf = din('ident_bf', [128, 128], BF16)
    t_ident_f = din('ident_f', [128, 128])
    t_F1 = din('F1', [128, 256], BF16)
    t_Tw1 = din('Tw1', [128, 256])
    t_CS4 = din('CS4', [128, 512], BF16)
    t_W3 = din('W3', [128, 768], BF16)
    t_Tw2 = din('Tw2', [128, 64])
    t_PHI = din('PHI', [128, 384], BF16)
    t_zT = din('zT', [33, S])
    t_negt = din('negt', [128, 32])
    t_absd = din('absdelta', [128, 512])
    t_ropec = din('ropec', [128, S])
    t_ropes = din('ropes', [128, S])
    out = nc.dram_tensor('out', [128 if small_out else S, D], F32, kind='ExternalOutput').ap()
    xa_scr = nc.dram_tensor('xa_scr', [S, D], F32, kind='Internal').ap()
    dump_aps = {}
    for (nm, shape, dt) in dumps:
        dump_aps[nm] = nc.dram_tensor('dbg_' + nm, list(shape), dt, kind='ExternalOutput').ap()

    es = ExitStack()
    sems = [es.enter_context(nc.semaphore(f"s{i}")) for i in range(88)]
    arena_t = es.enter_context(nc.sbuf_tensor("arena", [128, ARENA_F32], F32))
    psall_t = es.enter_context(nc.psum_tensor("psall", [128, 4096], F32))
    psall = psall_t[:]
    psall_b = psall_t[:].bitcast(BF16)
    ps = [psall[:, i * 512:(i + 1) * 512] for i in range(8)]
    psb = [psall_b[:, i * 1024:(i + 1) * 1024] for i in range(8)]
    AF = arena_t[:]
    AB = arena_t[:].bitcast(BF16)

    def f32r(boff, n):
        assert boff % 4 == 0 and boff + 4 * n <= ARENA_F32 * 4, (boff, n)
        return AF[:, boff // 4: boff // 4 + n]

    def bfr(boff, n):
        assert boff % 4 == 0 and boff + 2 * n <= ARENA_F32 * 4, (boff, n)
        return AB[:, boff // 2: boff // 2 + n]

    P = Prog(nc, sems)
    bufs = {}

    def B(name):
        b = bufs.get(name)
        if b is None:
            b = Buf(name, excl=name.startswith('ps'))
            bufs[name] = b
        return b

    def mm(out, lhsT, rhs, start=True, stop=True, reads=(), writes=(), **kw):
        P.op('tensor', lambda e: e.matmul(out, lhsT=lhsT, rhs=rhs, start=start, stop=stop, **kw), reads, writes)

    def tr(out, in_, ident, reads=(), writes=()):
        P.op('tensor', lambda e: e.transpose(out=out, in_=in_, identity=ident), reads, writes)

    def act(out, in_, func, reads=(), writes=(), **kw):
        P.op('scalar', lambda e: e.activation(out=out, in_=in_, func=func, **kw), reads, writes)

    def tt(eng, out, in0, in1, op, reads=(), writes=()):
        P.op(eng, lambda e: e.tensor_tensor(out=out, in0=in0, in1=in1, op=op), reads, writes)

    def ts(eng, out, in0, s1, s2, op0, op1=None, reads=(), writes=(), **kw):
        if op1 is None:
            P.op(eng, lambda e: e.tensor_scalar(out=out, in0=in0, scalar1=s1, scalar2=None, op0=op0, **kw), reads, writes)
        else:
            P.op(eng, lambda e: e.tensor_scalar(out=out, in0=in0, scalar1=s1, scalar2=s2, op0=op0, op1=op1, **kw), reads, writes)

    def stt(eng, out, in0, scalar, in1, op0, op1, reads=(), writes=(), **kw):
        P.op(eng, lambda e: e.scalar_tensor_tensor(out=out, in0=in0, scalar=scalar, in1=in1, op0=op0, op1=op1, **kw), reads, writes)

    def cp(eng, out, in_, reads=(), writes=()):
        P.op(eng, lambda e: e.tensor_copy(out=out, in_=in_), reads, writes)

    def mset(eng, ap, val, writes=()):
        P.op(eng, lambda e: e.memset(ap, val), (), writes)

    def dma(q, out, in_, reads=(), writes=(), sembuf=None):
        return P.dma(q, lambda e: e.dma_start(out=out, in_=in_), reads, writes, sembuf)

    def barrier():
        P.barrier()

    cpos = [CONST0]

    def calloc(nbytes):
        o = cpos[0]
        cpos[0] += (nbytes + 63) // 64 * 64
        assert cpos[0] <= HT0, cpos[0]
        return o

    c_ident_bf = bfr(calloc(256), 128)
    c_ident_f = f32r(calloc(512), 128)
    c_F1 = bfr(calloc(512), 256)
    c_Tw1 = f32r(calloc(1024), 256)
    c_CS4 = bfr(calloc(1024), 512)
    c_W3 = bfr(calloc(1536), 768)
    c_Tw2 = f32r(calloc(256), 64)
    c_PHI = bfr(calloc(768), 384)
    c_gcols = f32r(calloc(64), 16)
    c_cw = f32r(calloc(192), 48)
    c_fbias = f32r(calloc(16), 4)
    c_subg = f32r(calloc(4), 1)
    c_lamv = f32r(calloc(1024), 256)
    c_negt = f32r(calloc(128), 32)
    c_absd = f32r(calloc(2048), 512)
    c_fw1 = f32r(calloc(256), 64)
    c_fw2 = f32r(calloc(256), 64)
    c_fw3 = f32r(calloc(256), 64)
    c_fcol = f32r(calloc(16), 4)
    c_fw4 = bfr(calloc(2048), 1024)
    c_cols = f32r(calloc(1024), 256)
    c_lam = f32r(calloc(64), 16)
    KC = B('consts')
    for dst, src in [(c_ident_f, t_ident_f), (c_Tw1, t_Tw1), (c_Tw2, t_Tw2), (c_gcols, gcols), (c_cw, cw),
                     (c_fbias, fbias), (c_subg, subg), (c_negt, t_negt), (c_absd, t_absd), (c_fcol[0:64, :], fcol),
                     (c_fw1[0:33, :], fw1), (c_fw2[0:64, :], fw2), (c_fw3[0:64, :], fw3),
                     (c_ident_bf, t_ident_bf), (c_F1, t_F1), (c_CS4, t_CS4), (c_W3, t_W3), (c_PHI, t_PHI)]:
        dma('sync', dst, src, writes=[KC])
    dma('sync', c_lamv, bass.AP(lamv.tensor, 0, [[0, 128], [1, 256]]), writes=[KC])
    KCG = B('consts_g')
    dma('gpsimd', c_fw4[0:64, :], fw4, writes=[KCG])

    d = dict(locals())
    d.pop('self')
    self.__dict__.update(d)


def rstd_from_ss(k, ss, ms, ln, rstd, n, eps, bufs):
    k.ts('vector', ms, ss, 1.0 / n, eps, ALU.mult, ALU.add, reads=bufs, writes=bufs)
    k.act(ln, ms, ACT.Ln, reads=bufs, writes=bufs)
    k.act(rstd, ln, ACT.Exp, reads=bufs, writes=bufs, scale=-0.5)


def phase0(k):
    B = k.B
    hT = k.bfr(HT0, 8 * S).rearrange("p (c t) -> p c t", c=8)
    k.hT = hT
    XT = [k.f32r(WORK0 + i * 4096, 1024) for i in range(3)]
    XN = [k.bfr(WORK0 + 12288 + i * 2048, 1024) for i in range(2)]
    JK = k.bfr(WORK0 + 16384, 1024)
    gb = V(k.c_gcols, 0, [[1, 8], [0, 128]])
    nt = getattr(k, 'dev_ntiles', 32)

    def stage_a(i):
        s3, s2 = i % 3, i % 2
        xt, xn = XT[s3], XN[s2]
        bx, bn, bc = B(f'p0x{s3}'), B(f'p0n{s2}'), B(f'p0c{s2}')
        cols = k.c_cols[:, s2 * 8: s2 * 8 + 8]
        ss, ms, ln, rs = cols[:, 0:1], cols[:, 1:2], cols[:, 2:3], cols[:, 3:4]
        k.dma('sync', xt, k.x[i * 128:(i + 1) * 128, :], writes=[bx])
        k.stt('vector', JK, xt, 1.0, xt, ALU.mult, ALU.mult, reads=[bx], writes=[B('p0jk'), bc], accum_out=ss)
        rstd_from_ss(k, ss, ms, ln, rs, 1024.0, 1e-6, [bc])
        k.act(xn, xt, ACT.Identity, reads=[bx, bc], writes=[bn], scale=rs)

    def stage_b(i):
        s2 = i % 2
        xn, bn = XN[s2], B(f'p0n{s2}')
        bank = s2
        bp = B(f'ps{bank}')
        for c in range(8):
            k.tr(k.psb[bank][:, c * 128:(c + 1) * 128], xn[:, c * 128:(c + 1) * 128], k.c_ident_bf,
                 reads=[bn, k.KC], writes=[bp])
        pin = k.psb[bank][:, 0:1024].rearrange("p (c t) -> p c t", c=8)
        k.tt('vector', hT[:, :, i * 128:(i + 1) * 128], pin, gb, ALU.mult, reads=[bp, k.KC], writes=[B(f'hT{i}')])

    if nt > 0:
        stage_a(0)
    for i in range(nt):
        if i + 1 < nt:
            stage_a(i + 1)
        stage_b(i)


H3T0 = MIX0 + 32 * KB


def phaseF(k):
    B = k.B
    ZT = k.f32r(HW0, S)
    ARG = k.f32r(HW0 + 16 * KB, S)
    WT = k.f32r(HW0 + 32 * KB, S)
    HA = k.f32r(HW0 + 48 * KB, S)
    HB = k.f32r(HW0 + 64 * KB, S)
    h3T = k.bfr(H3T0, S)
    k.h3T = h3T
    bz, ba, bw = B('fz'), B('farg'), B('fwt')
    k.dma('sync', ZT[0:33, :], k.t_zT, writes=[bz])
    freq = k.c_fcol[0:64, 3:4]
    layers = [(k.c_fw1[0:33, :], ZT[0:33, :], HA, bz, B('fha')),
              (k.c_fw2[0:64, :], HA[0:64, :], HB, B('fha'), B('fhb')),
              (k.c_fw3[0:64, :], HB[0:64, :], h3T, B('fhb'), B('h3T'))]
    for L, (w, inp, outp, bin_, bout) in enumerate(layers):
        fb = k.c_cols[0:64, 32 + L: 33 + L]
        bfb = B(f'ffb{L}')
        k.tt('vector', fb, k.c_fcol[0:64, L:L + 1], freq, ALU.mult, reads=[k.KC], writes=[bfb])
        for ch in range(8):
            bank = ch % 2
            bp = B(f'ps{bank}')
            k.mm(k.ps[bank][0:64, :], w, inp[:, ch * 512:(ch + 1) * 512], reads=[k.KC, bin_], writes=[bp])
            k.ts('vector', ARG[0:64, ch * 512:(ch + 1) * 512], k.ps[bank][0:64, :], freq, fb, ALU.mult, ALU.add,
                 reads=[bp, k.KC, bfb], writes=[ba])
        a = ARG[0:64, :]
        wt = WT[0:64, :]
        k.ts('vector', wt, a, PI, -2 * PI, ALU.is_gt, ALU.mult, reads=[ba], writes=[bw])
        k.tt('vector', a, a, wt, ALU.add, reads=[ba, bw], writes=[ba])
        k.ts('vector', wt, a, -PI, 2 * PI, ALU.is_lt, ALU.mult, reads=[ba], writes=[bw])
        k.tt('vector', a, a, wt, ALU.add, reads=[ba, bw], writes=[ba])
        k.act(outp[0:64, :], a, ACT.Sin, reads=[ba], writes=[bout])


def load_w(k, dst3, src_cols, bw):
    k.dma('gpsimd', dst3, src_cols.rearrange("(c p) n -> p c n", p=128), writes=[bw])


def proj_fm(k, wt3, col0, ncols, dst_fn, bw, tag, banks=(6, 7)):
    B = k.B
    for ch in range(8):
        bank = banks[ch % len(banks)]
        bp = B(f'ps{bank}')
        hb = [B(f'hT{i}') for i in range(ch * 4, ch * 4 + 4)]
        for c in range(8):
            k.mm(k.ps[bank][:, :], wt3[:, c, col0:col0 + ncols], k.hT[:, c, ch * 512:(ch + 1) * 512],
                 start=(c == 0), stop=(c == 7), reads=[bw] + hb, writes=[bp])
        dst_fn(ch, k.ps[bank][:, :], bp)


def short_conv(k, raw, t1, dst, ti, braw, bt1, bdst, eng='vector'):
    w0, w1, w2, bb = [k.c_cw[:, ti * 4 + i: ti * 4 + i + 1] for i in range(4)]
    k.act(t1, raw[:, 1:S + 1], ACT.Identity, reads=[braw, k.KC], writes=[bt1], scale=w1, bias=bb)
    k.stt(eng, t1, raw[:, 0:S], w0, t1, ALU.mult, ALU.add, reads=[braw, bt1, k.KC], writes=[bt1])
    k.stt(eng, dst, raw[:, 2:S + 2], w2, t1, ALU.mult, ALU.add, reads=[braw, bt1, k.KC], writes=[bdst])


def fft_fwd_batch(k, d0_groups, pslots, a_slots, tag):
    P_reg, bP = pslots
    for gi, (lh, bd) in enumerate(d0_groups):
        pa, ba = a_slots[gi]
        k.mm(pa, lh, k.c_F1, reads=[bd, k.KC], writes=[ba])
    for gi, (lh, bd) in enumerate(d0_groups):
        pa, ba = a_slots[gi]
        o = V(P_reg, gi * 128, [[2 * 512, 2], [512, 2], [1, 128]])
        i0 = bass.AP(pa.tensor, pa.offset, [list(pa.ap[0]), [0, 2], [128, 2], [1, 128]])
        i1 = V(k.c_Tw1, 0, [[128, 2], [0, 2], [1, 128]])
        k.tt('vector', o, i0, i1, ALU.mult, reads=[ba, k.KC], writes=[bP])


def phaseH(k):
    B = k.B
    mixT = k.bfr(MIX0, 8 * S).rearrange("p (c t) -> p c t", c=8)
    k.mixT = mixT
    U = k.bfr(HW0, S)
    G = k.bfr(HW0 + 8 * KB, 2 * 32 * 128)
    R = k.bfr(HW0 + 24 * KB, 4 * 32 * 128)
    D0F = k.bfr(HW0 + 24 * KB, 2 * 32 * 128)
    PF = [k.bfr(HW0 + 40 * KB + i * 4 * KB, 2048) for i in range(4)]
    X0 = HW0 + 56 * KB
    RAWS = [k.f32r(HW0 + 24 * KB, S + 4), k.f32r(X0 + KB, S + 4)]
    T1 = k.f32r(HW0 + 24 * KB + 16416, S)
    D0 = k.bfr(X0, 32 * 128)
    PU = [k.bfr(X0 + 8 * KB + i * 4 * KB, 2048) for i in range(2)]
    QS = [k.bfr(X0 + 16 * KB + i * 4 * KB, 2048) for i in range(2)]
    DEC = [k.f32r(X0 + i * 512, 128) for i in range(2)]
    WSL = [k.bfr(HW0 + 8 * KB + i * 2 * KB, 1024).rearrange("p (c n) -> p c n", c=8) for i in range(3)]
    X0C = k.bfr(X0 + 24 * KB, S)
    TMP = [k.f32r(HW0 + 8 * KB + 10 * KB + i * 2 * KB, 512) for i in range(2)]
    hT = k.hT
    ASL = [(k.ps[b][:, h * 256:(h + 1) * 256], B(f'ps{b}')) for b in range(2) for h in range(2)]
    CSm = lambda i: k.c_CS4[:, i * 128:(i + 1) * 128]
    W3m = lambda i: k.c_W3[:, i * 256:(i + 1) * 256]
    PHIm = lambda i: k.c_PHI[:, i * 128:(i + 1) * 128]
    XRE = [0, 2, 2, 1]
    XIM = [3, 0, 0, 2]
    XIMN = [2, 1, 1, 3]
    QW = [0, 2, 2, 1]
    RW = [0, 1, 2, 0]
    braws, bt1, bu, bx0c = [B('hraw0'), B('hraw1')], B('ht1'), B('hu'), B('hx0c')

    for ct in getattr(k, 'dev_cts', range(4)):
        for RAW, braw in zip(RAWS, braws):
            k.mset('vector', RAW[:, 0:1], 0.0, writes=[braw])
            k.mset('vector', RAW[:, S + 1:S + 2], 0.0, writes=[braw])
        cols3 = (512 + ct * 128, 1024 + ct * 128, ct * 128)
        for wi, col in enumerate(cols3):
            load_w(k, WSL[wi], k.w_in[:, col:col + 128], B(f'hw{wi}'))
        for wi, col in enumerate(cols3):
            bw = B(f'hw{wi}')
            RAW, braw = RAWS[wi % 2], braws[wi % 2]

            def evac(ch, pap, bp, RAW=RAW, braw=braw):
                k.act(RAW[:, 1 + ch * 512: 1 + (ch + 1) * 512], pap, ACT.Copy, reads=[bp], writes=[braw])
            proj_fm(k, WSL[wi], 0, 128, evac, bw, f'h{wi}')
            ti = col // 128
            if wi == 0:
                short_conv(k, RAW, T1, U, ti, braw, bt1, bu)
            elif wi == 1:
                short_conv(k, RAW, T1, T1, ti, braw, bt1, bt1)
                k.tt('gpsimd', U, T1, U, ALU.mult, reads=[bt1, bu], writes=[bu])
            else:
                short_conv(k, RAW, T1, X0C, ti, braw, bt1, bx0c)
        k.barrier()
        bd0f = B('hd0f')
        bG = B('hG')
        for j in range(32):
            bank = 6 + j % 2
            bp = B(f'ps{bank}')
            s2 = j % 2
            bdec = B(f'hdec{s2}')
            lh = V(k.h3T, j, [[32, 128]], p0=0, np_=64)
            rh = V(k.c_fw4, ct * 128, [[512, 2], [1, 128]], p0=0, np_=64)
            k.mm(k.ps[bank][:, 0:256], lh, rh, reads=[B('h3T'), k.KCG], writes=[bp])
            k.act(DEC[s2], k.c_absd[:, ct * 128:(ct + 1) * 128], ACT.Exp, reads=[k.KC], writes=[bdec],
                  scale=k.c_negt[:, j:j + 1])
            o = V(D0F, j * 4, [[32 * 128, 2], [128, 32], [1, 4]])
            i0 = bass.AP(k.ps[bank].tensor, k.ps[bank].offset, [list(k.ps[bank].ap[0]), [128, 2], [4, 32], [1, 4]])
            i1 = V(DEC[s2], 0, [[0, 2], [4, 32], [1, 4]])
            k.tt('vector', o, i0, i1, ALU.mult, reads=[bp, bdec], writes=[bd0f])
        def f12(bt):
            pf, pb_ = (PF[(bt % 2) * 2], B(f'hpf{bt % 2}')), (PF[(bt % 2) * 2 + 1], B(f'hpb{bt % 2}'))
            for di, pslot in enumerate((pf, pb_)):
                groups = [(V(D0F, di * 4096 + (bt * 4 + gi) * 128, [[1, 128]]), bd0f) for gi in range(4)]
                fft_fwd_batch(k, groups, pslot, ASL, 'f')

        f12(0)
        for bt in range(8):
            if bt + 1 < 8:
                f12(bt + 1)
            pf, pb_ = (PF[(bt % 2) * 2], B(f'hpf{bt % 2}')), (PF[(bt % 2) * 2 + 1], B(f'hpb{bt % 2}'))
            bre, bim = 2 + (bt % 2) * 2, 3 + (bt % 2) * 2
            bpr, bpi = B(f'ps{bre}'), B(f'ps{bim}')
            for (bank, bp, cf, cb) in ((bre, bpr, XRE, XRE), (bim, bpi, XIM, XIMN)):
                n = 0
                for (preg, bP), coef in ((pf, cf), (pb_, cb)):
                    for comp in range(4):
                        k.mm(k.ps[bank][:, :], CSm(coef[comp]), preg[:, comp * 512:(comp + 1) * 512],
                             start=(n == 0), stop=(n == 7), reads=[bP, k.KC], writes=[bp])
                        n += 1
            for ri, (bank, bp) in enumerate(((bre, bpr), (bim, bpi))):
                k.act(G[:, ri * 4096 + bt * 512: ri * 4096 + (bt + 1) * 512], k.ps[bank][:, :], ACT.Copy,
                      reads=[bp], writes=[bG])
        k.barrier()
        bd0 = B('hd0')
        bR = B('hR')
        for jb in range(4):
            bank = 6 + jb % 2
            bp = B(f'ps{bank}')
            for jj in range(8):
                j = jb * 8 + jj
                k.tr(k.psb[bank][:, jj * 128:(jj + 1) * 128], V(U, j, [[32, 128]]), k.c_ident_bf,
                     reads=[bu, k.KC], writes=[bp])
            o = V(D0, jb * 8 * 4, [[4, 8], [128, 32], [1, 4]])
            i0 = bass.AP(k.psb[bank].tensor, k.psb[bank].offset, [list(k.psb[bank].ap[0]), [128, 8], [4, 32], [1, 4]])
            k.cp('vector', o, i0, reads=[bp], writes=[bd0])
        def u12(bt):
            pu = (PU[bt % 2], B(f'hpu{bt % 2}'))
            groups = [(V(D0, (bt * 4 + gi) * 128, [[1, 128]]), bd0) for gi in range(4)]
            fft_fwd_batch(k, groups, pu, ASL, 'u')

        def u3q(bt):
            pu = (PU[bt % 2], B(f'hpu{bt % 2}'))
            bre, bim = 2 + (bt % 2) * 2, 3 + (bt % 2) * 2
            bpr, bpi = B(f'ps{bre}'), B(f'ps{bim}')
            for (bank, bp, cf) in ((bre, bpr, XRE), (bim, bpi, XIM)):
                for comp in range(4):
                    k.mm(k.ps[bank][:, :], CSm(cf[comp]), pu[0][:, comp * 512:(comp + 1) * 512],
                         start=(comp == 0), stop=(comp == 3), reads=[pu[1], k.KC], writes=[bp])
            qreg, bq = QS[bt % 2], B(f'hq{bt % 2}')
            gsl = V(G, bt * 512, [[4096, 2], [1, 512]])
            for half, (bank, bp) in enumerate(((bre, bpr), (bim, bpi))):
                if half == 0:
                    o = V(qreg, 0, [[512, 2], [1, 512]])
                else:
                    o = V(qreg, 1024, [[512, 2], [1, 512]])
                i0 = bass.AP(k.ps[bank].tensor, k.ps[bank].offset, [list(k.ps[bank].ap[0]), [0, 2], [1, 512]])
                k.tt('vector', o, i0, gsl, ALU.mult, reads=[bp, bG], writes=[bq])
        def u3p(bt):
            qreg, bq = QS[bt % 2], B(f'hq{bt % 2}')
            for gi in range(4):
                pa, ba = ASL[gi]
                for comp in range(4):
                    k.mm(pa, qreg[:, comp * 512 + gi * 128: comp * 512 + (gi + 1) * 128], W3m(QW[comp]),
                         start=(comp == 0), stop=(comp == 3), reads=[bq, k.KC], writes=[ba], skip_group_check=True)
            for gi in range(4):
                pa, ba = ASL[gi]
                g = bt * 4 + gi
                for a in range(2):
                    o = V(R, (a * 2) * 4096 + g * 4, [[4096, 2], [128, 32], [1, 4]])
                    i0 = bass.AP(pa.tensor, pa.offset, [list(pa.ap[0]), [128, 2], [4, 32], [1, 4]])
                    i1 = V(k.c_Tw2, a * 32, [[0, 2], [1, 32], [0, 4]])
                    k.tt('vector', o, i0, i1, ALU.mult, reads=[ba, k.KC], writes=[bR])

        u12(0)
        for bt in range(9):
            if bt + 1 < 8:
                u12(bt + 1)
            if bt < 8:
                u3q(bt)
            if bt >= 1:
                u3p(bt - 1)
        k.barrier()
        bmix = B(f'mix{ct}')
        for jb in range(8):
            bank = 4 + jb % 2
            bp = B(f'ps{bank}')
            for jj in range(4):
                j = jb * 4 + jj
                for comp in range(4):
                    k.mm(k.ps[bank][:, jj * 128:(jj + 1) * 128], R[:, comp * 4096 + j * 128: comp * 4096 + (j + 1) * 128],
                         PHIm(RW[comp]), start=(comp == 0), stop=(comp == 3), reads=[bR, k.KC], writes=[bp],
                         skip_group_check=True)
            tmp, btmp = TMP[jb % 2], B(f'htmp{jb % 2}')
            uv = V(U, jb * 4, [[32, 128], [1, 4]])
            pv = bass.AP(k.ps[bank].tensor, k.ps[bank].offset, [list(k.ps[bank].ap[0]), [1, 128], [128, 4]])
            tv = V(tmp, 0, [[4, 128], [1, 4]])
            k.stt('vector', tv, uv, k.c_fbias[:, ct:ct + 1], pv, ALU.mult, ALU.add, reads=[bu, bp, k.KC], writes=[btmp])
            xv = V(X0C, jb * 4, [[32, 128], [1, 4]])
            mv = bass.AP(mixT.tensor, mixT.offset + ct * S + jb * 4, [list(mixT.ap[0]), [32, 128], [1, 4]])
            k.tt('gpsimd', mv, tv, xv, ALU.mult, reads=[btmp, bx0c], writes=[bmix])
        k.barrier()


def phaseA(k):
    B = k.B
    hT, mixT = k.hT, k.mixT
    W = WORK0
    QT = k.bfr(W, S)
    KT = k.bfr(W + 8 * KB, S)
    VA = k.bfr(W + 16 * KB, 32 * 130).rearrange("p (s e) -> p s e", s=32)
    RC = [k.f32r(W + 25 * KB + i * 2 * KB, 512) for i in range(2)]
    RS = [k.f32r(W + 29 * KB + i * 2 * KB, 512) for i in range(2)]
    WSL = [k.bfr(W + 33 * KB + i * 2 * KB, 1024).rearrange("p (c n) -> p c n", c=8) for i in range(5)]
    PT = [k.bfr(W + 43 * KB + i * KB, 512) for i in range(4)]
    PTALL = k.bfr(W + 43 * KB, 2048)
    TA = [k.f32r(W + 47 * KB + i * 2 * KB, 512) for i in range(4)]
    OSB = [k.f32r(W + 58 * KB + i * 512, 128) for i in range(4)]
    YN = [k.bfr(W + 56 * KB + i * 256, 128) for i in range(4)]
    JK = k.bfr(W + 57 * KB, 128)
    lam = k.c_lam
    bl = B('lam')
    lv = k.c_lamv
    k.stt('vector', JK[:, 0:64], lv[:, 0:64], 1.0, lv[:, 64:128], ALU.mult, ALU.mult, reads=[k.KC], writes=[bl, B('ajk')],
          accum_out=lam[:, 0:1])
    k.stt('vector', JK[:, 0:64], lv[:, 128:192], 1.0, lv[:, 192:256], ALU.mult, ALU.mult, reads=[k.KC], writes=[bl, B('ajk')],
          accum_out=lam[:, 1:2])
    k.act(lam[:, 2:4], lam[:, 0:2], ACT.Exp, reads=[bl], writes=[bl])
    k.tt('vector', lam[:, 4:5], lam[:, 2:3], lam[:, 3:4], ALU.subtract, reads=[bl], writes=[bl])
    k.ts('vector', lam[:, 5:6], lam[:, 4:5], 0.2, -1.0, ALU.add, ALU.mult, reads=[bl], writes=[bl])
    neglam = lam[:, 5:6]
    bQ, bK, bV = B('aQT'), B('aKT'), B('aVA')

    def oreg(r):
        return k.ps[4 + r // 3][:, (r % 3) * 129:(r % 3) * 129 + 129], B(f'ps{4 + r // 3}')

    for hd in getattr(k, 'dev_heads', range(4)):
        srcs = [k.w_in[:, 1536 + hd * 128: 1536 + (hd + 1) * 128], k.w_qkp[:, hd * 128:(hd + 1) * 128],
                k.w_in[:, 2048 + hd * 128: 2048 + (hd + 1) * 128], k.w_qkp[:, 512 + hd * 128: 512 + (hd + 1) * 128],
                k.w_in[:, 2560 + hd * 128: 2560 + (hd + 1) * 128]]
        bws = [B(f'aw{i}') for i in range(5)]
        for i in range(5):
            load_w(k, WSL[i], srcs[i], bws[i])
        k.mset('vector', VA[:, :, 128:129], 1.0, writes=[bV])
        for sb in range(8):
            bank = 6 + sb % 2
            bp = B(f'ps{bank}')
            for s4 in range(4):
                st = sb * 4 + s4
                for c in range(8):
                    k.mm(k.ps[bank][:, s4 * 128:(s4 + 1) * 128], hT[:, c, st * 128:(st + 1) * 128], WSL[4][:, c, :],
                         start=(c == 0), stop=(c == 7), reads=[bws[4], B(f'hT{st}')], writes=[bp], skip_group_check=True)
            pin = k.ps[bank][:, :].rearrange("p (s e) -> p s e", s=4)
            k.act(VA[:, sb * 4:(sb + 1) * 4, 0:128], pin, ACT.Copy, reads=[bp], writes=[bV])
        for ch in range(8):
            s2 = ch % 2
            brc, brs = B(f'arc{s2}'), B(f'ars{s2}')
            k.dma('sync', RC[s2], k.t_ropec[:, ch * 512:(ch + 1) * 512], writes=[brc])
            k.dma('sync', RS[s2], k.t_ropes[:, ch * 512:(ch + 1) * 512], writes=[brs])
            hb = [B(f'hT{i}') for i in range(ch * 4, ch * 4 + 4)]
            for wi in range(4):
                bank = 2 + wi
                for c in range(8):
                    k.mm(k.ps[bank][:, :], WSL[wi][:, c, :], hT[:, c, ch * 512:(ch + 1) * 512],
                         start=(c == 0), stop=(c == 7), reads=[bws[wi]] + hb, writes=[B(f'ps{bank}')])
            for qi, (dst, bdst, sc) in enumerate(((QT, bQ, 0.125), (KT, bK, 1.0))):
                ta, tb = TA[qi * 2], TA[qi * 2 + 1]
                bta, btb = B(f'ata{qi}'), B(f'atb{qi}')
                k.stt('vector', ta, k.ps[2 + qi * 2][:, :], sc, RC[s2], ALU.mult, ALU.mult,
                      reads=[B(f'ps{2 + qi * 2}'), brc], writes=[bta])
                k.stt('vector', tb, k.ps[3 + qi * 2][:, :], sc, RS[s2], ALU.mult, ALU.mult,
                      reads=[B(f'ps{3 + qi * 2}'), brs], writes=[btb])
                k.tt('gpsimd', dst[:, ch * 512:(ch + 1) * 512], ta, tb, ALU.add, reads=[bta, btb], writes=[bdst])
        def qk_exp(qc, st):
            par = st % 2
            for c in range(2):
                bank = par * 2 + c
                k.mm(k.ps[bank][:, :], KT[c * 64:(c + 1) * 64, st * 128:(st + 1) * 128],
                     QT[c * 64:(c + 1) * 64, qc * 512:(qc + 1) * 512], reads=[bK, bQ], writes=[B(f'ps{bank}')])
            for c in range(2):
                bank = par * 2 + c
                k.act(PT[par * 2 + c], k.ps[bank][:, :], ACT.Exp, reads=[B(f'ps{bank}')], writes=[B(f'apt{par * 2 + c}')])

        def fin_a(qc):
            for qs in range(4):
                o0, bo0 = oreg(qs)
                o1, bo1 = oreg(4 + qs)
                cols = k.c_cols[:, 64 + qs * 16: 64 + qs * 16 + 16]
                bc = B(f'acol{qs}')
                osb, bos = OSB[qs], B(f'aosb{qs}')
                P_ = k.P
                P_.op('vector', lambda e, a=cols[:, 0:1], b=o0[:, 128:129]: e.reciprocal(out=a, in_=b), [bo0], [bc])
                P_.op('vector', lambda e, a=cols[:, 1:2], b=o1[:, 128:129]: e.reciprocal(out=a, in_=b), [bo1], [bc])
                k.tt('vector', cols[:, 2:3], cols[:, 1:2], neglam, ALU.mult, reads=[bc, bl], writes=[bc])
                k.ts('vector', osb, o0[:, 0:128], cols[:, 0:1], None, ALU.mult, reads=[bo0, bc], writes=[bos])
                k.stt('vector', osb, o1[:, 0:128], cols[:, 2:3], osb, ALU.mult, ALU.add, reads=[bo1, bc, bos], writes=[bos])
            for qs in range(4):
                cols = k.c_cols[:, 64 + qs * 16: 64 + qs * 16 + 16]
                bc = B(f'acol{qs}')
                osb, bos = OSB[qs], B(f'aosb{qs}')
                yn, byn = YN[qs], B(f'ayn{qs}')
                k.stt('vector', JK, osb, 1.0, osb, ALU.mult, ALU.mult, reads=[bos], writes=[bc, B('ajk')], accum_out=cols[:, 3:4])
                rstd_from_ss(k, cols[:, 3:4], cols[:, 4:5], cols[:, 5:6], cols[:, 6:7], 128.0, 1e-5, [bc])
                k.ts('vector', yn, osb, cols[:, 6:7], 0.8, ALU.mult, ALU.mult, reads=[bos, bc], writes=[byn])

        def fin_b(qc):
            for qs in range(4):
                k.tr(k.psb[7][:, qs * 128:(qs + 1) * 128], YN[qs], k.c_ident_bf, reads=[B(f'ayn{qs}'), k.KC], writes=[B('ps7')])
            k.act(mixT[:, 4 + hd, qc * 512:(qc + 1) * 512], k.psb[7][:, 0:512], ACT.Identity, reads=[B('ps7'), k.KC],
                  writes=[B(f'mix{4 + hd}')], scale=k.c_subg[:, 0:1])

        pending = None
        for qc in range(8):
            qk_exp(qc, 0)
            for st in range(32):
                par = st % 2
                if st + 1 < 32:
                    qk_exp(qc, st + 1)
                if st == 2 and pending is not None:
                    fin_b(pending)
                    pending = None
                for c in range(2):
                    for qs in range(4):
                        r = c * 4 + qs
                        oap, bo = oreg(r)
                        k.mm(oap, PT[par * 2 + c][:, qs * 128:(qs + 1) * 128], VA[:, st, 0:129],
                             start=(st == 0 and r % 3 == 0), stop=(st == 31),
                             reads=[B(f'apt{par * 2 + c}'), bV], writes=[bo], skip_group_check=True)
            fin_a(qc)
            pending = qc
        fin_b(pending)


def load_mlp_w(k):
    B = k.B
    WUP = k.bfr(HT0, 8 * 4096).rearrange("p (c n) -> p c n", c=8)
    k.WUP = WUP
    for c in range(8):
        k.dma('gpsimd', WUP[:, c, :], k.w_up[c * 128:(c + 1) * 128, :], writes=[B(f'wup{c}')])


def phaseO(k):
    B = k.B
    mixT = k.mixT
    W = WORK0
    WO = k.bfr(W, 8 * 1024).rearrange("p (c n) -> p c n", c=8)
    GP = k.f32r(W + 16 * KB, 1024)
    XT = [k.f32r(W + 20 * KB + i * 4 * KB, 1024) for i in range(3)]
    MS = [k.f32r(W + 32 * KB + i * 4 * KB, 1024) for i in range(2)]
    TMP = [k.f32r(W + 40 * KB + i * 4 * KB, 1024) for i in range(2)]
    JK = k.bfr(W + 48 * KB, 1024)
    bwo, bgp = B('owo'), B('ogp')
    for half in range(2):
        k.dma('gpsimd', WO[:, :, half * 512:(half + 1) * 512],
              k.w_out[:, half * 512:(half + 1) * 512].rearrange("(c p) n -> p c n", p=128), writes=[bwo])
    k.dma('sync', GP, bass.AP(k.gpost.tensor, 0, [[0, 128], [1, 1024]]), writes=[bgp])
    load_mlp_w(k)
    mixb = [B(f'mix{c}') for c in range(8)]
    def stage_a(i):
        s3, s2 = i % 3, i % 2
        xt, bx = XT[s3], B(f'ox{s3}')
        ms, bms = MS[s2], B(f'oms{s2}')
        k.dma('sync', xt, k.x[i * 128:(i + 1) * 128, :], writes=[bx])
        for half in range(2):
            bank = 2 * s2 + half
            bp = B(f'ps{bank}')
            for c in range(8):
                k.mm(k.ps[bank][:, :], mixT[:, c, i * 128:(i + 1) * 128], WO[:, c, half * 512:(half + 1) * 512],
                     start=(c == 0), stop=(c == 7), reads=[bwo] + mixb, writes=[bp])
            k.act(ms[:, half * 512:(half + 1) * 512], k.ps[bank][:, :], ACT.Copy, reads=[bp], writes=[bms])

    def stage_b(i):
        s3, s2 = i % 3, i % 2
        xt, bx = XT[s3], B(f'ox{s3}')
        ms, bms = MS[s2], B(f'oms{s2}')
        tmp, btmp = TMP[s2], B(f'otmp{s2}')
        cols = k.c_cols[:, 96 + s2 * 8: 96 + s2 * 8 + 8]
        bc = B(f'ocol{s2}')
        k.stt('vector', JK, ms, 1.0, ms, ALU.mult, ALU.mult, reads=[bms], writes=[bc, B('ojk')], accum_out=cols[:, 0:1])
        rstd_from_ss(k, cols[:, 0:1], cols[:, 1:2], cols[:, 2:3], cols[:, 3:4], 1024.0, 1e-6, [bc])
        k.stt('vector', tmp, ms, cols[:, 3:4], GP, ALU.mult, ALU.mult, reads=[bms, bc, bgp], writes=[btmp])
        k.tt('gpsimd', tmp, tmp, xt, ALU.add, reads=[btmp, bx], writes=[btmp])
        k.dma('sync', k.xa_scr[i * 128:(i + 1) * 128, :], tmp, reads=[btmp], writes=[B(f'xa{i}')], sembuf=btmp)

    stage_a(0)
    for i in range(32):
        if i + 1 < 32:
            stage_a(i + 1)
        stage_b(i)


def phaseM(k):
    B = k.B
    WUP = k.WUP
    WDN = k.bfr(MIX0, 32 * 1024).rearrange("p (f n) -> p f n", f=32)
    for f0 in range(0, 32, 4):
        k.dma('gpsimd', WDN[:, f0:f0 + 4, :],
              k.w_down[f0 * 128:(f0 + 4) * 128, :].rearrange("(f p) n -> p f n", p=128), writes=[B(f'wdn{f0 // 4}')])
    W = WORK0
    GP = k.f32r(W, 1024)
    XA = [[k.f32r(W + 4 * KB + (a * 2 + b) * 4 * KB, 1024) for b in range(2)] for a in range(2)]
    XN = [k.bfr(W + 20 * KB + i * 2 * KB, 1024) for i in range(2)]
    H2T = [k.bfr(W + 24 * KB + i * 4 * KB, 8 * 256).rearrange("p (c t) -> p c t", c=8) for i in range(2)]
    UPT = k.bfr(W + 32 * KB, 32 * 256).rearrange("p (f t) -> p f t", f=32)
    RL = [k.f32r(W + 48 * KB + i * KB, 256) for i in range(2)]
    MS = [k.f32r(W + 50 * KB + i * 4 * KB, 1024) for i in range(2)]
    JK = k.bfr(W + 58 * KB, 1024)
    bgp = B('mgp')
    k.dma('sync', GP, bass.AP(k.gpost.tensor, 1024, [[0, 128], [1, 1024]]), writes=[bgp])
    gb = V(k.c_gcols, 8, [[1, 8], [0, 128]])
    bwup, bwdn, bupt = B('wup'), B('wdn'), B('mupt')
    k.out_ticks = []
    nck = getattr(k, 'dev_nck', 16)

    def prologue(ck):
        a = ck % 2
        bh2 = B(f'mh2{a}')
        for tl in range(2):
            i = ck * 2 + tl
            xa, bxa = XA[a][tl], B(f'mxa{a}{tl}')
            xn, bxn = XN[tl], B(f'mxn{tl}')
            cols = k.c_cols[:, 128 + tl * 8: 128 + tl * 8 + 8]
            bc = B(f'mcol{tl}')
            k.dma('sync', xa, k.xa_scr[i * 128:(i + 1) * 128, :], reads=[B(f'xa{i}')], writes=[bxa])
            k.stt('vector', JK, xa, 1.0, xa, ALU.mult, ALU.mult, reads=[bxa], writes=[bc, B('mjk')], accum_out=cols[:, 0:1])
            rstd_from_ss(k, cols[:, 0:1], cols[:, 1:2], cols[:, 2:3], cols[:, 3:4], 1024.0, 1e-6, [bc])
            k.act(xn, xa, ACT.Identity, reads=[bxa, bc], writes=[bxn], scale=cols[:, 3:4])
            bank = tl
            bp = B(f'ps{bank}')
            for c in range(8):
                k.tr(k.psb[bank][:, c * 128:(c + 1) * 128], xn[:, c * 128:(c + 1) * 128], k.c_ident_bf,
                     reads=[bxn, k.KC], writes=[bp])
            pin = k.psb[bank][:, 0:1024].rearrange("p (c t) -> p c t", c=8)
            k.tt('vector', H2T[a][:, :, tl * 128:(tl + 1) * 128], pin, gb, ALU.mult, reads=[bp, k.KC], writes=[bh2])

    prologue(0)
    for ck in range(nck):
        a = ck % 2
        bh2 = B(f'mh2{a}')
        for f in range(32):
            bank = 2 + f % 2
            bp = B(f'ps{bank}')
            rl, brl = RL[f % 2], B(f'mrl{f % 2}')
            for c in range(8):
                k.mm(k.ps[bank][:, 0:256], WUP[:, c, f * 128:(f + 1) * 128], H2T[a][:, c, :],
                     start=(c == 0), stop=(c == 7), reads=[B(f'wup{c}'), bh2], writes=[bp])
            k.act(rl, k.ps[bank][:, 0:256], ACT.Relu, reads=[bp], writes=[brl])
            k.tt('gpsimd', UPT[:, f, :], rl, rl, ALU.mult, reads=[brl], writes=[bupt])
        if ck + 1 < nck:
            prologue(ck + 1)
        for tl in range(2):
            i = ck * 2 + tl
            xa, bxa = XA[a][tl], B(f'mxa{a}{tl}')
            ms, bms = MS[tl], B(f'mms{tl}')
            cols = k.c_cols[:, 144 + tl * 8: 144 + tl * 8 + 8]
            bc = B(f'mcol2{tl}')
            for half in range(2):
                bank = 4 + tl * 2 + half
                bp = B(f'ps{bank}')
                for f in range(32):
                    k.mm(k.ps[bank][:, :], UPT[:, f, tl * 128:(tl + 1) * 128], WDN[:, f, half * 512:(half + 1) * 512],
                         start=(f == 0), stop=(f == 31), reads=[bupt, B(f'wdn{f // 4}')], writes=[bp])
                k.act(ms[:, half * 512:(half + 1) * 512], k.ps[bank][:, :], ACT.Copy, reads=[bp], writes=[bms])
            k.stt('vector', JK, ms, 1.0, ms, ALU.mult, ALU.mult, reads=[bms], writes=[bc, B('mjk')], accum_out=cols[:, 0:1])
            rstd_from_ss(k, cols[:, 0:1], cols[:, 1:2], cols[:, 2:3], cols[:, 3:4], 1024.0, 1e-6, [bc])
            k.stt('vector', ms, ms, cols[:, 3:4], GP, ALU.mult, ALU.mult, reads=[bms, bc, bgp], writes=[bms])
            k.tt('gpsimd', ms, ms, xa, ALU.add, reads=[bms, bxa], writes=[bms])
            k.out_ticks.append(k.dma('sync', k.out[i * 128:(i + 1) * 128, :], ms, reads=[bms], sembuf=bms))


def host_inputs(inputs):
    T = host_tables()

    def g(k):
        return np.asarray(inputs[k], dtype=np.float32)
    w_in = np.ascontiguousarray(g('w_in')[0])
    qk = w_in[:, 1536:2560]
    idx = np.arange(1024)
    d = idx % 64
    perm = (idx - d) + (d + 32) % 64
    sh = {}
    sh['w_in'] = w_in
    sh['w_qkp'] = np.ascontiguousarray(qk[:, perm])
    cwb = np.concatenate([g('conv_w')[0], g('conv_b')[0][None, :]], axis=0)
    sh['cw'] = np.ascontiguousarray(cwb.reshape(4, 12, 128).transpose(2, 1, 0).reshape(128, 48))
    sh['fw1'] = np.ascontiguousarray(g('filt_w1')[0])
    sh['fw2'] = np.ascontiguousarray(g('filt_w2')[0])
    sh['fw3'] = np.ascontiguousarray(g('filt_w3')[0])
    sh['fw4'] = np.ascontiguousarray(g('filt_w4')[0])
    sh['fcol'] = np.ascontiguousarray(np.stack([g('filt_b1')[0], g('filt_b2')[0], g('filt_b3')[0], g('filt_freq')[0]], axis=1))
    sh['fbias'] = np.ascontiguousarray(g('filt_bias')[0].reshape(4, 128).T)
    sh['lamv'] = np.concatenate([g('lam_q1')[0], g('lam_k1')[0], g('lam_q2')[0], g('lam_k2')[0]])[None, :].copy()
    sh['subg'] = np.ascontiguousarray(g('subln_gain')[0][:, None])
    sh['w_out'] = np.ascontiguousarray(g('w_out')[0])
    sh['w_up'] = np.ascontiguousarray(g('w_up')[0])
    sh['w_down'] = np.ascontiguousarray(g('w_down')[0])
    sh['gcols'] = np.ascontiguousarray(np.stack([g('attn_pre_gain')[0].reshape(8, 128).T,
                                                 g('mlp_pre_gain')[0].reshape(8, 128).T], axis=1).reshape(128, 16))
    sh['gpost'] = np.ascontiguousarray(np.stack([g('attn_post_gain')[0], g('mlp_post_gain')[0]], axis=0))
    sh['ident_bf'] = T['ident_bf']
    sh['ident_f'] = T['ident_f']
    sh['F1'] = T['F1']
    sh['Tw1'] = T['Tw1'].reshape(128, 256)
    sh['CS4'] = T['CS4'].reshape(128, 512)
    sh['W3'] = T['W3'].reshape(128, 768)
    sh['Tw2'] = T['Tw2'].reshape(128, 64)
    sh['PHI'] = T['PHI'].reshape(128, 384)
    sh['zT'] = T['zT']
    sh['negt'] = T['negt']
    sh['absdelta'] = T['absdelta']
    sh['ropec'] = T['ropec']
    sh['ropes'] = T['ropes']
    return sh


PHASES = ['p0', 'pF', 'pH', 'pA', 'pO', 'pM']


def make_program(upto='pM', dumps=(), dump_fn=None, small_out=False, **attrs):
    k = K(dumps=dumps, small_out=small_out)
    k.__dict__.update(attrs)
    fns = {'p0': phase0, 'pF': phaseF, 'pH': phaseH, 'pA': phaseA, 'pO': phaseO, 'pM': phaseM}
    for ph in PHASES:
        fns[ph](k)
        k.barrier()
        if ph == upto:
            break
    ticks = []
    if dump_fn is not None:
        ticks += dump_fn(k)
    ticks += getattr(k, 'out_ticks', [])
    k.P.wait_all('sync', ticks)
    with k.nc.Block() as block:
        k.P.emit(block)
    k.es.close()
    return k.nc


def kernel(**inputs):
    sh = host_inputs(inputs)
    x = np.asarray(inputs['x'], dtype=np.float32)
    nc = make_program()
    in_maps = []
    for c in range(8):
        m = dict(sh)
        m['x'] = np.ascontiguousarray(x[c])
        in_maps.append(m)
    res = run_bass_kernel_spmd(nc, in_maps, core_ids=list(range(8)))
    return np.stack([np.asarray(r['out'], dtype=np.float32) for r in res.results], axis=0)
```

```python
import numpy as np
import concourse.bass as bass
import concourse.mybir as mybir
from concourse.bass_utils import run_bass_kernel_spmd
from contextlib import ExitStack

F32 = mybir.dt.float32
BF16 = mybir.dt.bfloat16
ALU = mybir.AluOpType
ACT = mybir.ActivationFunctionType
AX = mybir.AxisListType

ENGS = ('tensor', 'vector', 'scalar', 'gpsimd', 'sync')
SEM_ROT = 16000


class Buf:
    __slots__ = ('name', 'w', 'r', 'dsem', 'excl')

    def __init__(self, name, excl=False):
        self.name = name
        self.excl = excl
        self.w = None
        self.r = {}
        self.dsem = None


class Prog:
    def __init__(self, nc, sems):
        self.nc = nc
        self.sems = sems
        self.nsem = 0
        self.q = {e: [] for e in ENGS}
        self.cnt = {e: 0 for e in ENGS}
        self.esem = {}
        self.owned = {e: set() for e in ENGS}
        self.seen = {e: {} for e in ENGS}
        self.dcnt = {}
        self.final = {}
        for e in ENGS:
            self._new_esem(e)

    def alloc_sem(self):
        i = self.nsem
        self.nsem += 1
        assert i < len(self.sems), "out of semaphores"
        return i

    def _new_esem(self, e):
        i = self.alloc_sem()
        self.esem[e] = i
        self.owned[e].add(i)
        self.cnt[e] = 0

    def _deps(self, eng, reads, writes, is_dma=False):
        need = {}

        def add(t, raw=False):
            if t is None:
                return
            s, v = t
            if s in self.owned[eng] and not is_dma and not (raw and eng != 'tensor'):
                return
            if need.get(s, 0) < v:
                need[s] = v
        for b in reads:
            add(b.w, raw=True)
        for b in writes:
            add(b.w)
            for s, v in b.r.items():
                add((s, v))
        out = []
        seen = self.seen[eng]
        for s, v in need.items():
            if seen.get(s, 0) < v:
                seen[s] = v
                out.append((s, v))
        return out

    def op(self, eng, fn, reads=(), writes=()):
        if any(b.excl for b in reads):
            writes = list(writes) + [b for b in reads if b.excl]
            reads = [b for b in reads if not b.excl]
        waits = self._deps(eng, reads, writes)
        if self.cnt[eng] >= SEM_ROT:
            self._new_esem(eng)
        self.cnt[eng] += 1
        s, v = self.esem[eng], self.cnt[eng]
        for b in reads:
            if b.r.get(s, 0) < v:
                b.r[s] = v
        for b in writes:
            b.w = (s, v)
            b.r = {}
        self.q[eng].append((waits, fn, s, 1))

    def dma(self, qeng, fn, reads=(), writes=(), sembuf=None):
        waits = self._deps(qeng, reads, writes, is_dma=True)
        sb = sembuf if sembuf is not None else (writes[0] if writes else reads[0])
        if sb.dsem is None or self.dcnt[sb.dsem] >= SEM_ROT:
            sb.dsem = self.alloc_sem()
            self.dcnt[sb.dsem] = 0
        s = sb.dsem
        self.dcnt[s] += 16
        v = self.dcnt[s]
        for b in reads:
            if b.r.get(s, 0) < v:
                b.r[s] = v
        for b in writes:
            b.w = (s, v)
            b.r = {}
        self.q[qeng].append((waits, fn, s, 16))
        return (s, v)

    def wait_all(self, eng, ticks):
        need = {}
        for s, v in ticks:
            if need.get(s, 0) < v:
                need[s] = v
        self.q[eng].append((list(need.items()), None, None, 0))

    def barrier(self):
        ticks = [(self.esem[o], self.cnt[o]) for o in ENGS if self.cnt[o] > 0]
        ticks += [(s, v) for s, v in self.dcnt.items()]
        for e in ENGS:
            need = []
            for s, v in ticks:
                if s in self.owned[e] and e != 'gpsimd':
                    continue
                if self.seen[e].get(s, 0) < v:
                    self.seen[e][s] = v
                    need.append((s, v))
            if need:
                self.q[e].append((need, None, None, 0))

    def replay(self, engobj, eng):
        sems = self.sems
        for waits, fn, s, inc in self.q[eng]:
            for ws, wv in waits:
                engobj.wait_ge(sems[ws], wv)
            if fn is not None:
                inst = fn(engobj)
                inst.then_inc(sems[s], inc)

    def emit(self, block):
        @block.tensor
        def _(e):
            self.replay(e, 'tensor')

        @block.vector
        def _(e):
            self.replay(e, 'vector')

        @block.scalar
        def _(e):
            self.replay(e, 'scalar')

        @block.gpsimd
        def _(e):
            self.replay(e, 'gpsimd')

        @block.sync
        def _(e):
            self.replay(e, 'sync')


def AP(base, off, dims):
    return bass.AP(base.tensor, off, [list(d) for d in dims])


S = 4096
D = 1024
NFFT = 8192
PI = float(np.pi)


def host_tables():
    import ml_dtypes
    bf = ml_dtypes.bfloat16
    T = {}
    T['ident_bf'] = np.eye(128, dtype=np.float32).astype(bf)
    T['ident_f'] = np.eye(128, dtype=np.float32)
    p = np.arange(128, dtype=np.float64)
    k1 = np.arange(128, dtype=np.float64)
    j = np.arange(32, dtype=np.float64)
    k2 = np.arange(32, dtype=np.float64)
    ang = 2 * np.pi * (k1[None, :] + 0.5) * p[:, None] / 256.0
    T['F1'] = np.concatenate([np.cos(ang), -np.sin(ang)], axis=1).astype(np.float32).astype(bf)
    jj = np.repeat(j, 4)
    a1 = 2 * np.pi * (k1[None, :] + 0.5) * jj[:, None] / NFFT
    T['Tw1'] = np.stack([np.cos(a1), -np.sin(a1)], axis=1).astype(np.float32)
    th = 2 * np.pi * np.outer(j, k2) / 32.0
    I4 = np.eye(4)
    C = np.kron(np.cos(th), I4)
    Sm = np.kron(np.sin(th), I4)
    T['CS4'] = np.stack([C, -C, Sm, -Sm], axis=1).astype(np.float32).astype(bf)
    W1 = np.concatenate([C, Sm], axis=1)
    W2 = np.concatenate([-C, -Sm], axis=1)
    W34 = np.concatenate([-Sm, C], axis=1)
    T['W3'] = np.stack([W1, W2, W34], axis=1).astype(np.float32).astype(bf)
    a2 = 2 * np.pi * (k1[:, None] + 0.5) * j[None, :] / NFFT
    T['Tw2'] = np.stack([np.cos(a2), -np.sin(a2)], axis=1).astype(np.float32)
    ph = 2 * np.pi * (k1[:, None] + 0.5) * p[None, :] / 256.0
    cphi = (2.0 / NFFT) * np.cos(ph)
    sphi = (2.0 / NFFT) * np.sin(ph)
    T['PHI'] = np.stack([cphi, -sphi, sphi], axis=1).astype(np.float32).astype(bf)
    bands = 16
    t = np.linspace(0.0, 1.0, S, dtype=np.float32)[:, None]
    w = (np.float32(2.0 * np.pi / S) * np.arange(S, dtype=np.float32))[:, None]
    f = np.linspace(1e-4, bands - 1, bands, dtype=np.float32)[None, :]
    fw_ = (f * w).astype(np.float32)
    z = np.concatenate([t, np.cos(fw_), -np.sin(fw_)], axis=-1).astype(np.float32)
    T['zT'] = np.ascontiguousarray(z.T)
    tt = np.linspace(0.0, 1.0, S, dtype=np.float32).reshape(128, 32)
    T['negt'] = (-tt).astype(np.float32)
    import math
    max_decay = math.log(1e-2) / 0.3
    min_decay = math.log(1e-2) / 1.5
    deltas = np.abs(np.linspace(min_decay, max_decay, 512, dtype=np.float32))
    T['absdelta'] = np.ascontiguousarray(np.broadcast_to(deltas[None, :], (128, 512))).astype(np.float32)
    inv = (10000.0 ** (-np.arange(0, 64, 2, dtype=np.float32) / 64)).astype(np.float32)
    angr = (np.arange(S, dtype=np.float32)[:, None] * inv[None, :]).astype(np.float32)
    angr = np.concatenate([angr, angr], axis=-1)
    cosr = np.cos(angr).astype(np.float32)
    sinr = np.sin(angr).astype(np.float32)
    sgn = np.concatenate([-np.ones(32, np.float32), np.ones(32, np.float32)])
    sins = sinr * sgn[None, :]
    T['ropec'] = np.ascontiguousarray(np.concatenate([cosr.T, cosr.T], axis=0))
    T['ropes'] = np.ascontiguousarray(np.concatenate([sins.T, sins.T], axis=0))
    return T


KB = 1024
CONST0 = 0
HT0 = 14 * KB
MIX0 = 78 * KB
WORK0 = 142 * KB
HW0 = 118 * KB
ARENA_F32 = 53100


class Ctx:
    pass


def V(region, off, dims, p0=0, np_=128):
    pstep = region.ap[0][0]
    return bass.AP(region.tensor, region.offset + p0 * pstep + off,
                   [[pstep, np_]] + [list(d) for d in dims])


class K:
  def __init__(self, dumps=(), small_out=False):
    nc = bass.Bass("TRN2", target_bir_lowering=False)

    def din(name, shape, dt=F32):
        return nc.dram_tensor(name, list(shape), dt, kind="ExternalInput").ap()

    x = din('x', [S, D])
    w_in = din('w_in', [D, 3072])
    w_qkp = din('w_qkp', [D, 1024])
    cw = din('cw', [128, 48])
    fw1 = din('fw1', [33, 64])
    fw2 = din('fw2', [64, 64])
    fw3 = din('fw3', [64, 64])
    fw4 = din('fw4', [64, 1024])
    fcol = din('fcol', [64, 4])
    fbias = din('fbias', [128, 4])
    lamv = din('lamv', [1, 256])
    subg = din('subg', [128, 1])
    w_out = din('w_out', [D, D])
    w_up = din('w_up', [D, 4096])
    w_down = din('w_down', [4096, D])
    gcols = din('gcols', [128, 16])
    gpost = din('gpost', [2, 1024])
    t_ident_bf = din('ident_bf', [128, 128], BF16)
    t_ident_f = din('ident_f', [128, 128])
    t_F1 = din('F1', [128, 256], BF16)
    t_Tw1 = din('Tw1', [128, 256])
    t_CS4 = din('CS4', [128, 512], BF16)
    t_W3 = din('W3', [128, 768], BF16)
    t_Tw2 = din('Tw2', [128, 64])
    t_PHI = din('PHI', [128, 384], BF16)
    t_zT = din('zT', [33, S])
    t_negt = din('negt', [128, 32])
    t_absd = din('absdelta', [128, 512])
    t_ropec = din('ropec', [128, S])
    t_ropes = din('ropes', [128, S])
    out = nc.dram_tensor('out', [128 if small_out else S, D], F32, kind='ExternalOutput').ap()
    xa_scr = nc.dram_tensor('xa_scr', [S, D], F32, kind='Internal').ap()
    dump_aps = {}
    for (nm, shape, dt) in dumps:
        dump_aps[nm] = nc.dram_tensor('dbg_' + nm, list(shape), dt, kind='ExternalOutput').ap()

    es = ExitStack()
    sems = [es.enter_context(nc.semaphore(f"s{i}")) for i in range(88)]
    arena_t = es.enter_context(nc.sbuf_tensor("arena", [128, ARENA_F32], F32))
    psall_t = es.enter_context(nc.psum_tensor("psall", [128, 4096], F32))
    psall = psall_t[:]
    psall_b = psall_t[:].bitcast(BF16)
    ps = [psall[:, i * 512:(i + 1) * 512] for i in range(8)]
    psb = [psall_b[:, i * 1024:(i + 1) * 1024] for i in range(8)]
    AF = arena_t[:]
    AB = arena_t[:].bitcast(BF16)

    def f32r(boff, n):
        assert boff % 4 == 0 and boff + 4 * n <= ARENA_F32 * 4, (boff, n)
        return AF[:, boff // 4: boff // 4 + n]

    def bfr(boff, n):
        assert boff % 4 == 0 and boff + 2 * n <= ARENA_F32 * 4, (boff, n)
        return AB[:, boff // 2: boff // 2 + n]

    P = Prog(nc, sems)
    bufs = {}

    def B(name):
        b = bufs.get(name)
        if b is None:
            b = Buf(name, excl=name.startswith('ps'))
            bufs[name] = b
        return b

    def mm(out, lhsT, rhs, start=True, stop=True, reads=(), writes=(), **kw):
        P.op('tensor', lambda e: e.matmul(out, lhsT=lhsT, rhs=rhs, start=start, stop=stop, **kw), reads, writes)

    def tr(out, in_, ident, reads=(), writes=()):
        P.op('tensor', lambda e: e.transpose(out=out, in_=in_, identity=ident), reads, writes)

    def act(out, in_, func, reads=(), writes=(), **kw):
        P.op('scalar', lambda e: e.activation(out=out, in_=in_, func=func, **kw), reads, writes)

    def tt(eng, out, in0, in1, op, reads=(), writes=()):
        P.op(eng, lambda e: e.tensor_tensor(out=out, in0=in0, in1=in1, op=op), reads, writes)

    def ts(eng, out, in0, s1, s2, op0, op1=None, reads=(), writes=(), **kw):
        if op1 is None:
            P.op(eng, lambda e: e.tensor_scalar(out=out, in0=in0, scalar1=s1, scalar2=None, op0=op0, **kw), reads, writes)
        else:
            P.op(eng, lambda e: e.tensor_scalar(out=out, in0=in0, scalar1=s1, scalar2=s2, op0=op0, op1=op1, **kw), reads, writes)

    def stt(eng, out, in0, scalar, in1, op0, op1, reads=(), writes=(), **kw):
        P.op(eng, lambda e: e.scalar_tensor_tensor(out=out, in0=in0, scalar=scalar, in1=in1, op0=op0, op1=op1, **kw), reads, writes)

    def cp(eng, out, in_, reads=(), writes=()):
        P.op(eng, lambda e: e.tensor_copy(out=out, in_=in_), reads, writes)

    def mset(eng, ap, val, writes=()):
        P.op(eng, lambda e: e.memset(ap, val), (), writes)

    def dma(q, out, in_, reads=(), writes=(), sembuf=None):
        return P.dma(q, lambda e: e.dma_start(out=out, in_=in_), reads, writes, sembuf)

    def barrier():
        P.barrier()

    cpos = [CONST0]

    def calloc(nbytes):
        o = cpos[0]
        cpos[0] += (nbytes + 63) // 64 * 64
        assert cpos[0] <= HT0, cpos[0]
        return o

    c_ident_bf = bfr(calloc(256), 128)
    c_ident_f = f32r(calloc(512), 128)
    c_F1 = bfr(calloc(512), 256)
    c_Tw1 = f32r(calloc(1024), 256)
    c_CS4 = bfr(calloc(1024), 512)
    c_W3 = bfr(calloc(1536), 768)
    c_Tw2 = f32r(calloc(256), 64)
    c_PHI = bfr(calloc(768), 384)
    c_gcols = f32r(calloc(64), 16)
    c_cw = f32r(calloc(192), 48)
    c_fbias = f32r(calloc(16), 4)
    c_subg = f32r(calloc(4), 1)
    c_lamv = f32r(calloc(1024), 256)
    c_negt = f32r(calloc(128), 32)
    c_absd = f32r(calloc(2048), 512)
    c_fw1 = f32r(calloc(256), 64)
    c_fw2 = f32r(calloc(256), 64)
    c_fw3 = f32r(calloc(256), 64)
    c_fcol = f32r(calloc(16), 4)
    c_fw4 = bfr(calloc(2048), 1024)
    c_cols = f32r(calloc(1024), 256)
    c_lam = f32r(calloc(64), 16)
    KC = B('consts')
    for dst, src in [(c_ident_f, t_ident_f), (c_Tw1, t_Tw1), (c_Tw2, t_Tw2), (c_gcols, gcols), (c_cw, cw),
                     (c_fbias, fbias), (c_subg, subg), (c_negt, t_negt), (c_absd, t_absd), (c_fcol[0:64, :], fcol),
                     (c_fw1[0:33, :], fw1), (c_fw2[0:64, :], fw2), (c_fw3[0:64, :], fw3),
                     (c_ident_bf, t_ident_bf), (c_F1, t_F1), (c_CS4, t_CS4), (c_W3, t_W3), (c_PHI, t_PHI)]:
        dma('sync', dst, src, writes=[KC])
    dma('sync', c_lamv, bass.AP(lamv.tensor, 0, [[0, 128], [1, 256]]), writes=[KC])
    KCG = B('consts_g')
    dma('gpsimd', c_fw4[0:64, :], fw4, writes=[KCG])

    d = dict(locals())
    d.pop('self')
    self.__dict__.update(d)


def rstd_from_ss(k, ss, ms, ln, rstd, n, eps, bufs):
    k.ts('vector', ms, ss, 1.0 / n, eps, ALU.mult, ALU.add, reads=bufs, writes=bufs)
    k.act(ln, ms, ACT.Ln, reads=bufs, writes=bufs)
    k.act(rstd, ln, ACT.Exp, reads=bufs, writes=bufs, scale=-0.5)


def phase0(k):
    B = k.B
    hT = k.bfr(HT0, 8 * S).rearrange("p (c t) -> p c t", c=8)
    k.hT = hT
    XT = [k.f32r(WORK0 + i * 4096, 1024) for i in range(3)]
    XN = [k.bfr(WORK0 + 12288 + i * 2048, 1024) for i in range(2)]
    JK = k.bfr(WORK0 + 16384, 1024)
    gb = V(k.c_gcols, 0, [[1, 8], [0, 128]])
    nt = getattr(k, 'dev_ntiles', 32)

    def stage_a(i):
        s3, s2 = i % 3, i % 2
        xt, xn = XT[s3], XN[s2]
        bx, bn, bc = B(f'p0x{s3}'), B(f'p0n{s2}'), B(f'p0c{s2}')
        cols = k.c_cols[:, s2 * 8: s2 * 8 + 8]
        ss, ms, ln, rs = cols[:, 0:1], cols[:, 1:2], cols[:, 2:3], cols[:, 3:4]
        k.dma('sync', xt, k.x[i * 128:(i + 1) * 128, :], writes=[bx])
        k.stt('vector', JK, xt, 1.0, xt, ALU.mult, ALU.mult, reads=[bx], writes=[B('p0jk'), bc], accum_out=ss)
        rstd_from_ss(k, ss, ms, ln, rs, 1024.0, 1e-6, [bc])
        k.act(xn, xt, ACT.Identity, reads=[bx, bc], writes=[bn], scale=rs)

    def stage_b(i):
        s2 = i % 2
        xn, bn = XN[s2], B(f'p0n{s2}')
        bank = s2
        bp = B(f'ps{bank}')
        for c in range(8):
            k.tr(k.psb[bank][:, c * 128:(c + 1) * 128], xn[:, c * 128:(c + 1) * 128], k.c_ident_bf,
                 reads=[bn, k.KC], writes=[bp])
        pin = k.psb[bank][:, 0:1024].rearrange("p (c t) -> p c t", c=8)
        k.tt('vector', hT[:, :, i * 128:(i + 1) * 128], pin, gb, ALU.mult, reads=[bp, k.KC], writes=[B(f'hT{i}')])

    if nt > 0:
        stage_a(0)
    for i in range(nt):
        if i + 1 < nt:
            stage_a(i + 1)
        stage_b(i)


H3T0 = MIX0 + 32 * KB


def phaseF(k):
    B = k.B
    ZT = k.f32r(HW0, S)
    ARG = k.f32r(HW0 + 16 * KB, S)
    WT = k.f32r(HW0 + 32 * KB, S)
    HA = k.f32r(HW0 + 48 * KB, S)
    HB = k.f32r(HW0 + 64 * KB, S)
    h3T = k.bfr(H3T0, S)
    k.h3T = h3T
    bz, ba, bw = B('fz'), B('farg'), B('fwt')
    k.dma('sync', ZT[0:33, :], k.t_zT, writes=[bz])
    freq = k.c_fcol[0:64, 3:4]
    layers = [(k.c_fw1[0:33, :], ZT[0:33, :], HA, bz, B('fha')),
              (k.c_fw2[0:64, :], HA[0:64, :], HB, B('fha'), B('fhb')),
              (k.c_fw3[0:64, :], HB[0:64, :], h3T, B('fhb'), B('h3T'))]
    for L, (w, inp, outp, bin_, bout) in enumerate(layers):
        fb = k.c_cols[0:64, 32 + L: 33 + L]
        bfb = B(f'ffb{L}')
        k.tt('vector', fb, k.c_fcol[0:64, L:L + 1], freq, ALU.mult, reads=[k.KC], writes=[bfb])
        for ch in range(8):
            bank = ch % 2
            bp = B(f'ps{bank}')
            k.mm(k.ps[bank][0:64, :], w, inp[:, ch * 512:(ch + 1) * 512], reads=[k.KC, bin_], writes=[bp])
            k.ts('vector', ARG[0:64, ch * 512:(ch + 1) * 512], k.ps[bank][0:64, :], freq, fb, ALU.mult, ALU.add,
                 reads=[bp, k.KC, bfb], writes=[ba])
        a = ARG[0:64, :]
        wt = WT[0:64, :]
        k.ts('vector', wt, a, PI, -2 * PI, ALU.is_gt, ALU.mult, reads=[ba], writes=[bw])
        k.tt('vector', a, a, wt, ALU.add, reads=[ba, bw], writes=[ba])
        k.ts('vector', wt, a, -PI, 2 * PI, ALU.is_lt, ALU.mult, reads=[ba], writes=[bw])
        k.tt('vector', a, a, wt, ALU.add, reads=[ba, bw], writes=[ba])
        k.act(outp[0:64, :], a, ACT.Sin, reads=[ba], writes=[bout])


def load_w(k, dst3, src_cols, bw):
    k.dma('gpsimd', dst3, src_cols.rearrange("(c p) n -> p c n", p=128), writes=[bw])


def proj_fm(k, wt3, col0, ncols, dst_fn, bw, tag, banks=(6, 7)):
    B = k.B
    for ch in range(8):
        bank = banks[ch % len(banks)]
        bp = B(f'ps{bank}')
        hb = [B(f'hT{i}') for i in range(ch * 4, ch * 4 + 4)]
        for c in range(8):
            k.mm(k.ps[bank][:, :], wt3[:, c, col0:col0 + ncols], k.hT[:, c, ch * 512:(ch + 1) * 512],
                 start=(c == 0), stop=(c == 7), reads=[bw] + hb, writes=[bp])
        dst_fn(ch, k.ps[bank][:, :], bp)


def short_conv(k, raw, t1, dst, ti, braw, bt1, bdst, eng='vector'):
    w0, w1, w2, bb = [k.c_cw[:, ti * 4 + i: ti * 4 + i + 1] for i in range(4)]
    k.act(t1, raw[:, 1:S + 1], ACT.Identity, reads=[braw, k.KC], writes=[bt1], scale=w1, bias=bb)
    k.stt(eng, t1, raw[:, 0:S], w0, t1, ALU.mult, ALU.add, reads=[braw, bt1, k.KC], writes=[bt1])
    k.stt(eng, dst, raw[:, 2:S + 2], w2, t1, ALU.mult, ALU.add, reads=[braw, bt1, k.KC], writes=[bdst])


def fft_fwd_batch(k, d0_groups, pslots, a_slots, tag):
    P_reg, bP = pslots
    for gi, (lh, bd) in enumerate(d0_groups):
        pa, ba = a_slots[gi]
        k.mm(pa, lh, k.c_F1, reads=[bd, k.KC], writes=[ba])
    for gi, (lh, bd) in enumerate(d0_groups):
        pa, ba = a_slots[gi]
        o = V(P_reg, gi * 128, [[2 * 512, 2], [512, 2], [1, 128]])
        i0 = bass.AP(pa.tensor, pa.offset, [list(pa.ap[0]), [0, 2], [128, 2], [1, 128]])
        i1 = V(k.c_Tw1, 0, [[128, 2], [0, 2], [1, 128]])
        k.tt('vector', o, i0, i1, ALU.mult, reads=[ba, k.KC], writes=[bP])


def phaseH(k):
    B = k.B
    mixT = k.bfr(MIX0, 8 * S).rearrange("p (c t) -> p c t", c=8)
    k.mixT = mixT
    U = k.bfr(HW0, S)
    G = k.bfr(HW0 + 8 * KB, 2 * 32 * 128)
    R = k.bfr(HW0 + 24 * KB, 4 * 32 * 128)
    D0F = k.bfr(HW0 + 24 * KB, 2 * 32 * 128)
    PF = [k.bfr(HW0 + 40 * KB + i * 4 * KB, 2048) for i in range(4)]
    X0 = HW0 + 56 * KB
    RAWS = [k.f32r(HW0 + 24 * KB, S + 4), k.f32r(X0 + KB, S + 4)]
    T1 = k.f32r(HW0 + 24 * KB + 16416, S)
    D0 = k.bfr(X0, 32 * 128)
    PU = [k.bfr(X0 + 8 * KB + i * 4 * KB, 2048) for i in range(2)]
    QS = [k.bfr(X0 + 16 * KB + i * 4 * KB, 2048) for i in range(2)]
    DEC = [k.f32r(X0 + i * 512, 128) for i in range(2)]
    WSL = [k.bfr(HW0 + 8 * KB + i * 2 * KB, 1024).rearrange("p (c n) -> p c n", c=8) for i in range(3)]
    X0C = k.bfr(X0 + 24 * KB, S)
    TMP = [k.f32r(X0 + 8 * KB + i * 4 * KB, 512) for i in range(2)]
    hT = k.hT
    ASL = [(k.ps[b][:, h * 256:(h + 1) * 256], B(f'ps{b}')) for b in range(2) for h in range(2)]
    CSm = lambda i: k.c_CS4[:, i * 128:(i + 1) * 128]
    W3m = lambda i: k.c_W3[:, i * 256:(i + 1) * 256]
    PHIm = lambda i: k.c_PHI[:, i * 128:(i + 1) * 128]
    XRE = [0, 2, 2, 1]
    XIM = [3, 0, 0, 2]
    XIMN = [2, 1, 1, 3]
    QW = [0, 2, 2, 1]
    RW = [0, 1, 2, 0]
    braws, bt1, bu, bx0c = [B('hraw0'), B('hraw1')], B('ht1'), B('hu'), B('hx0c')

    for ct in getattr(k, 'dev_cts', range(4)):
        for RAW, braw in zip(RAWS, braws):
            k.mset('vector', RAW[:, 0:1], 0.0, writes=[braw])
            k.mset('vector', RAW[:, S + 1:S + 2], 0.0, writes=[braw])
        cols3 = (512 + ct * 128, 1024 + ct * 128, ct * 128)
        for wi, col in enumerate(cols3):
            load_w(k, WSL[wi], k.w_in[:, col:col + 128], B(f'hw{wi}'))
        for wi, col in enumerate(cols3):
            bw = B(f'hw{wi}')
            RAW, braw = RAWS[wi % 2], braws[wi % 2]

            def evac(ch, pap, bp, RAW=RAW, braw=braw):
                k.act(RAW[:, 1 + ch * 512: 1 + (ch + 1) * 512], pap, ACT.Copy, reads=[bp], writes=[braw])
            proj_fm(k, WSL[wi], 0, 128, evac, bw, f'h{wi}')
            ti = col // 128
            if wi == 0:
                short_conv(k, RAW, T1, U, ti, braw, bt1, bu)
            elif wi == 1:
                short_conv(k, RAW, T1, T1, ti, braw, bt1, bt1)
                k.tt('gpsimd', U, T1, U, ALU.mult, reads=[bt1, bu], writes=[bu])
            else:
                short_conv(k, RAW, T1, X0C, ti, braw, bt1, bx0c)
        k.barrier()
        bd0f = B('hd0f')
        bG = B('hG')
        for j in range(32):
            bank = 6 + j % 2
            bp = B(f'ps{bank}')
            s2 = j % 2
            bdec = B(f'hdec{s2}')
            lh = V(k.h3T, j, [[32, 128]], p0=0, np_=64)
            rh = V(k.c_fw4, ct * 128, [[512, 2], [1, 128]], p0=0, np_=64)
            k.mm(k.ps[bank][:, 0:256], lh, rh, reads=[B('h3T'), k.KCG], writes=[bp])
            k.act(DEC[s2], k.c_absd[:, ct * 128:(ct + 1) * 128], ACT.Exp, reads=[k.KC], writes=[bdec],
                  scale=k.c_negt[:, j:j + 1])
            o = V(D0F, j * 4, [[32 * 128, 2], [128, 32], [1, 4]])
            i0 = bass.AP(k.ps[bank].tensor, k.ps[bank].offset, [list(k.ps[bank].ap[0]), [128, 2], [4, 32], [1, 4]])
            i1 = V(DEC[s2], 0, [[0, 2], [4, 32], [1, 4]])
            k.tt('vector', o, i0, i1, ALU.mult, reads=[bp, bdec], writes=[bd0f])
        def f12(bt):
            pf, pb_ = (PF[(bt % 2) * 2], B(f'hpf{bt % 2}')), (PF[(bt % 2) * 2 + 1], B(f'hpb{bt % 2}'))
            for di, pslot in enumerate((pf, pb_)):
                groups = [(V(D0F, di * 4096 + (bt * 4 + gi) * 128, [[1, 128]]), bd0f) for gi in range(4)]
                fft_fwd_batch(k, groups, pslot, ASL, 'f')

        f12(0)
        for bt in range(8):
            if bt + 1 < 8:
                f12(bt + 1)
            pf, pb_ = (PF[(bt % 2) * 2], B(f'hpf{bt % 2}')), (PF[(bt % 2) * 2 + 1], B(f'hpb{bt % 2}'))
            bre, bim = 2 + (bt % 2) * 2, 3 + (bt % 2) * 2
            bpr, bpi = B(f'ps{bre}'), B(f'ps{bim}')
            for (bank, bp, cf, cb) in ((bre, bpr, XRE, XRE), (bim, bpi, XIM, XIMN)):
                n = 0
                for (preg, bP), coef in ((pf, cf), (pb_, cb)):
                    for comp in range(4):
                        k.mm(k.ps[bank][:, :], CSm(coef[comp]), preg[:, comp * 512:(comp + 1) * 512],
                             start=(n == 0), stop=(n == 7), reads=[bP, k.KC], writes=[bp])
                        n += 1
            for ri, (bank, bp) in enumerate(((bre, bpr), (bim, bpi))):
                k.act(G[:, ri * 4096 + bt * 512: ri * 4096 + (bt + 1) * 512], k.ps[bank][:, :], ACT.Copy,
                      reads=[bp], writes=[bG])
        k.barrier()
        bd0 = B('hd0')
        bR = B('hR')
        for jb in range(4):
            bank = 6 + jb % 2
            bp = B(f'ps{bank}')
            for jj in range(8):
                j = jb * 8 + jj
                k.tr(k.psb[bank][:, jj * 128:(jj + 1) * 128], V(U, j, [[32, 128]]), k.c_ident_bf,
                     reads=[bu, k.KC], writes=[bp])
            o = V(D0, jb * 8 * 4, [[4, 8], [128, 32], [1, 4]])
            i0 = bass.AP(k.psb[bank].tensor, k.psb[bank].offset, [list(k.psb[bank].ap[0]), [128, 8], [4, 32], [1, 4]])
            k.cp('vector', o, i0, reads=[bp], writes=[bd0])
        def u12(bt):
            pu = (PU[bt % 2], B(f'hpu{bt % 2}'))
            groups = [(V(D0, (bt * 4 + gi) * 128, [[1, 128]]), bd0) for gi in range(4)]
            fft_fwd_batch(k, groups, pu, ASL, 'u')

        def u3q(bt):
            pu = (PU[bt % 2], B(f'hpu{bt % 2}'))
            bre, bim = 2 + (bt % 2) * 2, 3 + (bt % 2) * 2
            bpr, bpi = B(f'ps{bre}'), B(f'ps{bim}')
            for (bank, bp, cf) in ((bre, bpr, XRE), (bim, bpi, XIM)):
                for comp in range(4):
                    k.mm(k.ps[bank][:, :], CSm(cf[comp]), pu[0][:, comp * 512:(comp + 1) * 512],
                         start=(comp == 0), stop=(comp == 3), reads=[pu[1], k.KC], writes=[bp])
            qreg, bq = QS[bt % 2], B(f'hq{bt % 2}')
            gsl = V(G, bt * 512, [[4096, 2], [1, 512]])
            for half, (bank, bp) in enumerate(((bre, bpr), (bim, bpi))):
                if half == 0:
                    o = V(qreg, 0, [[512, 2], [1, 512]])
                else:
                    o = V(qreg, 1024, [[512, 2], [1, 512]])
                i0 = bass.AP(k.ps[bank].tensor, k.ps[bank].offset, [list(k.ps[bank].ap[0]), [0, 2], [1, 512]])
                k.tt('vector', o, i0, gsl, ALU.mult, reads=[bp, bG], writes=[bq])
        def u3p(bt):
            qreg, bq = QS[bt % 2], B(f'hq{bt % 2}')
            for gi in range(4):
                pa, ba = ASL[gi]
                for comp in range(4):
                    k.mm(pa, qreg[:, comp * 512 + gi * 128: comp * 512 + (gi + 1) * 128], W3m(QW[comp]),
                         start=(comp == 0), stop=(comp == 3), reads=[bq, k.KC], writes=[ba], skip_group_check=True)
            for gi in range(4):
                pa, ba = ASL[gi]
                g = bt * 4 + gi
                for a in range(2):
                    o = V(R, (a * 2) * 4096 + g * 4, [[4096, 2], [128, 32], [1, 4]])
                    i0 = bass.AP(pa.tensor, pa.offset, [list(pa.ap[0]), [128, 2], [4, 32], [1, 4]])
                    i1 = V(k.c_Tw2, a * 32, [[0, 2], [1, 32], [0, 4]])
                    k.tt('vector', o, i0, i1, ALU.mult, reads=[ba, k.KC], writes=[bR])

        u12(0)
        for bt in range(9):
            if bt + 1 < 8:
                u12(bt + 1)
            if bt < 8:
                u3q(bt)
            if bt >= 1:
                u3p(bt - 1)
        bmix = B(f'mix{ct}')
        for jb in range(8):
            bank = 4 + jb % 2
            bp = B(f'ps{bank}')
            for jj in range(4):
                j = jb * 4 + jj
                for comp in range(4):
                    k.mm(k.ps[bank][:, jj * 128:(jj + 1) * 128], R[:, comp * 4096 + j * 128: comp * 4096 + (j + 1) * 128],
                         PHIm(RW[comp]), start=(comp == 0), stop=(comp == 3), reads=[bR, k.KC], writes=[bp],
                         skip_group_check=True)
            tmp, btmp = TMP[jb % 2], B(f'hpu{jb % 2}')
            uv = V(U, jb * 4, [[32, 128], [1, 4]])
            pv = bass.AP(k.ps[bank].tensor, k.ps[bank].offset, [list(k.ps[bank].ap[0]), [1, 128], [128, 4]])
            tv = V(tmp, 0, [[4, 128], [1, 4]])
            k.stt('vector', tv, uv, k.c_fbias[:, ct:ct + 1], pv, ALU.mult, ALU.add, reads=[bu, bp, k.KC], writes=[btmp])
            xv = V(X0C, jb * 4, [[32, 128], [1, 4]])
            mv = bass.AP(mixT.tensor, mixT.offset + ct * S + jb * 4, [list(mixT.ap[0]), [32, 128], [1, 4]])
            k.tt('gpsimd', mv, tv, xv, ALU.mult, reads=[btmp, bx0c], writes=[bmix])
        k.barrier()


def phaseA(k):
    B = k.B
    hT, mixT = k.hT, k.mixT
    W = WORK0
    QT = k.bfr(W, S)
    KT = k.bfr(W + 8 * KB, S)
    VA = k.bfr(W + 16 * KB, 32 * 130).rearrange("p (s e) -> p s e", s=32)
    RC = [k.f32r(W + 25 * KB + i * 2 * KB, 512) for i in range(2)]
    RS = [k.f32r(W + 29 * KB + i * 2 * KB, 512) for i in range(2)]
    WSL = [k.bfr(W + 33 * KB + i * 2 * KB, 1024).rearrange("p (c n) -> p c n", c=8) for i in range(5)]
    PT = [k.bfr(W + 43 * KB + i * KB, 512) for i in range(4)]
    PTALL = k.bfr(W + 43 * KB, 2048)
    TA = [k.f32r(W + 47 * KB + i * 2 * KB, 512) for i in range(4)]
    OSB = [k.f32r(W + 58 * KB + i * 512, 128) for i in range(4)]
    YN = [k.bfr(W + 56 * KB + i * 256, 128) for i in range(4)]
    JK = k.bfr(W + 57 * KB, 128)
    lam = k.c_lam
    bl = B('lam')
    lv = k.c_lamv
    k.stt('vector', JK[:, 0:64], lv[:, 0:64], 1.0, lv[:, 64:128], ALU.mult, ALU.mult, reads=[k.KC], writes=[bl, B('ajk')],
          accum_out=lam[:, 0:1])
    k.stt('vector', JK[:, 0:64], lv[:, 128:192], 1.0, lv[:, 192:256], ALU.mult, ALU.mult, reads=[k.KC], writes=[bl, B('ajk')],
          accum_out=lam[:, 1:2])
    k.act(lam[:, 2:4], lam[:, 0:2], ACT.Exp, reads=[bl], writes=[bl])
    k.tt('vector', lam[:, 4:5], lam[:, 2:3], lam[:, 3:4], ALU.subtract, reads=[bl], writes=[bl])
    k.ts('vector', lam[:, 5:6], lam[:, 4:5], 0.2, -1.0, ALU.add, ALU.mult, reads=[bl], writes=[bl])
    neglam = lam[:, 5:6]
    bQ, bK, bV = B('aQT'), B('aKT'), B('aVA')

    def oreg(r):
        return k.ps[4 + r // 3][:, (r % 3) * 129:(r % 3) * 129 + 129], B(f'ps{4 + r // 3}')

    for hd in getattr(k, 'dev_heads', range(4)):
        srcs = [k.w_in[:, 1536 + hd * 128: 1536 + (hd + 1) * 128], k.w_qkp[:, hd * 128:(hd + 1) * 128],
                k.w_in[:, 2048 + hd * 128: 2048 + (hd + 1) * 128], k.w_qkp[:, 512 + hd * 128: 512 + (hd + 1) * 128],
                k.w_in[:, 2560 + hd * 128: 2560 + (hd + 1) * 128]]
        bws = [B(f'aw{i}') for i in range(5)]
        for i in range(5):
            load_w(k, WSL[i], srcs[i], bws[i])
        k.mset('vector', VA[:, :, 128:129], 1.0, writes=[bV])
        for sb in range(8):
            bank = 6 + sb % 2
            bp = B(f'ps{bank}')
            for s4 in range(4):
                st = sb * 4 + s4
                for c in range(8):
                    k.mm(k.ps[bank][:, s4 * 128:(s4 + 1) * 128], hT[:, c, st * 128:(st + 1) * 128], WSL[4][:, c, :],
                         start=(c == 0), stop=(c == 7), reads=[bws[4], B(f'hT{st}')], writes=[bp], skip_group_check=True)
            pin = k.ps[bank][:, :].rearrange("p (s e) -> p s e", s=4)
            k.act(VA[:, sb * 4:(sb + 1) * 4, 0:128], pin, ACT.Copy, reads=[bp], writes=[bV])
        for ch in range(8):
            s2 = ch % 2
            brc, brs = B(f'arc{s2}'), B(f'ars{s2}')
            k.dma('sync', RC[s2], k.t_ropec[:, ch * 512:(ch + 1) * 512], writes=[brc])
            k.dma('sync', RS[s2], k.t_ropes[:, ch * 512:(ch + 1) * 512], writes=[brs])
            hb = [B(f'hT{i}') for i in range(ch * 4, ch * 4 + 4)]
            pbanks = ((2, 3, 4, 5), (0, 1, 6, 7))[s2]
            for wi in range(4):
                bank = pbanks[wi]
                for c in range(8):
                    k.mm(k.ps[bank][:, :], WSL[wi][:, c, :], hT[:, c, ch * 512:(ch + 1) * 512],
                         start=(c == 0), stop=(c == 7), reads=[bws[wi]] + hb, writes=[B(f'ps{bank}')])
            for qi, (dst, bdst, sc) in enumerate(((QT, bQ, 0.125), (KT, bK, 1.0))):
                ta, tb = TA[qi * 2], TA[qi * 2 + 1]
                bta, btb = B(f'ata{qi}'), B(f'atb{qi}')
                b0, b1 = pbanks[qi * 2], pbanks[qi * 2 + 1]
                k.stt('vector', ta, k.ps[b0][:, :], sc, RC[s2], ALU.mult, ALU.mult,
                      reads=[B(f'ps{b0}'), brc], writes=[bta])
                k.stt('vector', tb, k.ps[b1][:, :], sc, RS[s2], ALU.mult, ALU.mult,
                      reads=[B(f'ps{b1}'), brs], writes=[btb])
                k.tt('gpsimd', dst[:, ch * 512:(ch + 1) * 512], ta, tb, ALU.add, reads=[bta, btb], writes=[bdst])
        def qk_exp(qc, st):
            par = st % 2
            for c in range(2):
                bank = par * 2 + c
                k.mm(k.ps[bank][:, :], KT[c * 64:(c + 1) * 64, st * 128:(st + 1) * 128],
                     QT[c * 64:(c + 1) * 64, qc * 512:(qc + 1) * 512], reads=[bK, bQ], writes=[B(f'ps{bank}')])
            for c in range(2):
                bank = par * 2 + c
                k.act(PT[par * 2 + c], k.ps[bank][:, :], ACT.Exp, reads=[B(f'ps{bank}')], writes=[B(f'apt{par * 2 + c}')])

        def fin_a(qc):
            for qs in range(4):
                o0, bo0 = oreg(qs)
                o1, bo1 = oreg(4 + qs)
                cols = k.c_cols[:, 64 + qs * 16: 64 + qs * 16 + 16]
                bc = B(f'acol{qs}')
                osb, bos = OSB[qs], B(f'aosb{qs}')
                P_ = k.P
                P_.op('vector', lambda e, a=cols[:, 0:1], b=o0[:, 128:129]: e.reciprocal(out=a, in_=b), [bo0], [bc])
                P_.op('vector', lambda e, a=cols[:, 1:2], b=o1[:, 128:129]: e.reciprocal(out=a, in_=b), [bo1], [bc])
                k.tt('vector', cols[:, 2:3], cols[:, 1:2], neglam, ALU.mult, reads=[bc, bl], writes=[bc])
                k.ts('vector', osb, o0[:, 0:128], cols[:, 0:1], None, ALU.mult, reads=[bo0, bc], writes=[bos])
                k.stt('vector', osb, o1[:, 0:128], cols[:, 2:3], osb, ALU.mult, ALU.add, reads=[bo1, bc, bos], writes=[bos])
            for qs in range(4):
                cols = k.c_cols[:, 64 + qs * 16: 64 + qs * 16 + 16]
                bc = B(f'acol{qs}')
                osb, bos = OSB[qs], B(f'aosb{qs}')
                yn, byn = YN[qs], B(f'ayn{qs}')
                k.stt('vector', JK, osb, 1.0, osb, ALU.mult, ALU.mult, reads=[bos], writes=[bc, B('ajk')], accum_out=cols[:, 3:4])
                rstd_from_ss(k, cols[:, 3:4], cols[:, 4:5], cols[:, 5:6], cols[:, 6:7], 128.0, 1e-5, [bc])
                k.ts('vector', yn, osb, cols[:, 6:7], 0.8, ALU.mult, ALU.mult, reads=[bos, bc], writes=[byn])

        def fin_b(qc):
            for qs in range(4):
                k.tr(k.psb[7][:, qs * 128:(qs + 1) * 128], YN[qs], k.c_ident_bf, reads=[B(f'ayn{qs}'), k.KC], writes=[B('ps7')])
            k.act(mixT[:, 4 + hd, qc * 512:(qc + 1) * 512], k.psb[7][:, 0:512], ACT.Identity, reads=[B('ps7'), k.KC],
                  writes=[B(f'mix{4 + hd}')], scale=k.c_subg[:, 0:1])

        pending = None
        for qc in range(8):
            qk_exp(qc, 0)
            for st in range(32):
                par = st % 2
                if st + 1 < 32:
                    qk_exp(qc, st + 1)
                if st == 2 and pending is not None:
                    fin_b(pending)
                    pending = None
                for c in range(2):
                    for qs in range(4):
                        r = c * 4 + qs
                        oap, bo = oreg(r)
                        k.mm(oap, PT[par * 2 + c][:, qs * 128:(qs + 1) * 128], VA[:, st, 0:129],
                             start=(st == 0 and r % 3 == 0), stop=(st == 31),
                             reads=[B(f'apt{par * 2 + c}'), bV], writes=[bo], skip_group_check=True)
            fin_a(qc)
            pending = qc
        fin_b(pending)


def load_mlp_w(k):
    B = k.B
    WUP = k.bfr(HT0, 8 * 4096).rearrange("p (c n) -> p c n", c=8)
    k.WUP = WUP
    for c in range(8):
        k.dma('gpsimd', WUP[:, c, :], k.w_up[c * 128:(c + 1) * 128, :], writes=[B(f'wup{c}')])


def phaseO(k):
    B = k.B
    mixT = k.mixT
    W = WORK0
    WO = k.bfr(W, 8 * 1024).rearrange("p (c n) -> p c n", c=8)
    GP = k.f32r(W + 16 * KB, 1024)
    XT = [k.f32r(W + 20 * KB + i * 4 * KB, 1024) for i in range(3)]
    MS = [k.f32r(W + 32 * KB + i * 4 * KB, 1024) for i in range(2)]
    TMP = [k.f32r(W + 40 * KB + i * 4 * KB, 1024) for i in range(2)]
    JK = k.bfr(W + 48 * KB, 1024)
    bwo, bgp = B('owo'), B('ogp')
    for half in range(2):
        k.dma('gpsimd', WO[:, :, half * 512:(half + 1) * 512],
              k.w_out[:, half * 512:(half + 1) * 512].rearrange("(c p) n -> p c n", p=128), writes=[bwo])
    k.dma('sync', GP, bass.AP(k.gpost.tensor, 0, [[0, 128], [1, 1024]]), writes=[bgp])
    load_mlp_w(k)
    mixb = [B(f'mix{c}') for c in range(8)]
    def stage_a(i):
        s3, s2 = i % 3, i % 2
        xt, bx = XT[s3], B(f'ox{s3}')
        ms, bms = MS[s2], B(f'oms{s2}')
        k.dma('sync', xt, k.x[i * 128:(i + 1) * 128, :], writes=[bx])
        for half in range(2):
            bank = 2 * s2 + half
            bp = B(f'ps{bank}')
            for c in range(8):
                k.mm(k.ps[bank][:, :], mixT[:, c, i * 128:(i + 1) * 128], WO[:, c, half * 512:(half + 1) * 512],
                     start=(c == 0), stop=(c == 7), reads=[bwo] + mixb, writes=[bp])
            k.act(ms[:, half * 512:(half + 1) * 512], k.ps[bank][:, :], ACT.Copy, reads=[bp], writes=[bms])

    def stage_b(i):
        s3, s2 = i % 3, i % 2
        xt, bx = XT[s3], B(f'ox{s3}')
        ms, bms = MS[s2], B(f'oms{s2}')
        tmp, btmp = TMP[s2], B(f'otmp{s2}')
        cols = k.c_cols[:, 96 + s2 * 8: 96 + s2 * 8 + 8]
        bc = B(f'ocol{s2}')
        k.stt('vector', JK, ms, 1.0, ms, ALU.mult, ALU.mult, reads=[bms], writes=[bc, B('ojk')], accum_out=cols[:, 0:1])
        rstd_from_ss(k, cols[:, 0:1], cols[:, 1:2], cols[:, 2:3], cols[:, 3:4], 1024.0, 1e-6, [bc])
        k.stt('vector', tmp, ms, cols[:, 3:4], GP, ALU.mult, ALU.mult, reads=[bms, bc, bgp], writes=[btmp])
        k.tt('gpsimd', tmp, tmp, xt, ALU.add, reads=[btmp, bx], writes=[btmp])
        k.dma('sync', k.xa_scr[i * 128:(i + 1) * 128, :], tmp, reads=[btmp], writes=[B(f'xa{i}')], sembuf=btmp)

    stage_a(0)
    for i in range(32):
        if i + 1 < 32:
            stage_a(i + 1)
        stage_b(i)


def phaseM(k):
    B = k.B
    WUP = k.WUP
    WDN = k.bfr(MIX0, 32 * 1024).rearrange("p (f n) -> p f n", f=32)
    for f0 in range(0, 32, 4):
        k.dma('gpsimd', WDN[:, f0:f0 + 4, :],
              k.w_down[f0 * 128:(f0 + 4) * 128, :].rearrange("(f p) n -> p f n", p=128), writes=[B(f'wdn{f0 // 4}')])
    W = WORK0
    GP = k.f32r(W, 1024)
    XA = [[k.f32r(W + 4 * KB + (a * 2 + b) * 4 * KB, 1024) for b in range(2)] for a in range(2)]
    XN = [k.bfr(W + 20 * KB + i * 2 * KB, 1024) for i in range(2)]
    H2T = [k.bfr(W + 24 * KB + i * 4 * KB, 8 * 256).rearrange("p (c t) -> p c t", c=8) for i in range(2)]
    UPT = k.bfr(W + 32 * KB, 32 * 256).rearrange("p (f t) -> p f t", f=32)
    RL = [k.f32r(W + 48 * KB + i * KB, 256) for i in range(2)]
    MS = [k.f32r(W + 50 * KB + i * 4 * KB, 1024) for i in range(2)]
    JK = k.bfr(W + 58 * KB, 1024)
    bgp = B('mgp')
    k.dma('sync', GP, bass.AP(k.gpost.tensor, 1024, [[0, 128], [1, 1024]]), writes=[bgp])
    gb = V(k.c_gcols, 8, [[1, 8], [0, 128]])
    bwup, bwdn, bupt = B('wup'), B('wdn'), B('mupt')
    k.out_ticks = []
    nck = getattr(k, 'dev_nck', 16)

    def prologue(ck):
        a = ck % 2
        bh2 = B(f'mh2{a}')
        for tl in range(2):
            i = ck * 2 + tl
            xa, bxa = XA[a][tl], B(f'mxa{a}{tl}')
            xn, bxn = XN[tl], B(f'mxn{tl}')
            cols = k.c_cols[:, 128 + tl * 8: 128 + tl * 8 + 8]
            bc = B(f'mcol{tl}')
            k.dma('sync', xa, k.xa_scr[i * 128:(i + 1) * 128, :], reads=[B(f'xa{i}')], writes=[bxa])
            k.stt('vector', JK, xa, 1.0, xa, ALU.mult, ALU.mult, reads=[bxa], writes=[bc, B('mjk')], accum_out=cols[:, 0:1])
            rstd_from_ss(k, cols[:, 0:1], cols[:, 1:2], cols[:, 2:3], cols[:, 3:4], 1024.0, 1e-6, [bc])
            k.act(xn, xa, ACT.Identity, reads=[bxa, bc], writes=[bxn], scale=cols[:, 3:4])
            bank = tl
            bp = B(f'ps{bank}')
            for c in range(8):
                k.tr(k.psb[bank][:, c * 128:(c + 1) * 128], xn[:, c * 128:(c + 1) * 128], k.c_ident_bf,
                     reads=[bxn, k.KC], writes=[bp])
            pin = k.psb[bank][:, 0:1024].rearrange("p (c t) -> p c t", c=8)
            k.tt('vector', H2T[a][:, :, tl * 128:(tl + 1) * 128], pin, gb, ALU.mult, reads=[bp, k.KC], writes=[bh2])

    prologue(0)
    for ck in range(nck):
        a = ck % 2
        bh2 = B(f'mh2{a}')
        for f in range(32):
            bank = 2 + f % 2
            bp = B(f'ps{bank}')
            rl, brl = RL[f % 2], B(f'mrl{f % 2}')
            for c in range(8):
                k.mm(k.ps[bank][:, 0:256], WUP[:, c, f * 128:(f + 1) * 128], H2T[a][:, c, :],
                     start=(c == 0), stop=(c == 7), reads=[B(f'wup{c}'), bh2], writes=[bp])
            k.act(rl, k.ps[bank][:, 0:256], ACT.Relu, reads=[bp], writes=[brl])
            k.tt('gpsimd', UPT[:, f, :], rl, rl, ALU.mult, reads=[brl], writes=[bupt])
        if ck + 1 < nck:
            prologue(ck + 1)
        for tl in range(2):
            i = ck * 2 + tl
            xa, bxa = XA[a][tl], B(f'mxa{a}{tl}')
            ms, bms = MS[tl], B(f'mms{tl}')
            cols = k.c_cols[:, 144 + tl * 8: 144 + tl * 8 + 8]
            bc = B(f'mcol2{tl}')
            for half in range(2):
                bank = 4 + tl * 2 + half
                bp = B(f'ps{bank}')
                for f in range(32):
                    k.mm(k.ps[bank][:, :], UPT[:, f, tl * 128:(tl + 1) * 128], WDN[:, f, half * 512:(half + 1) * 512],
                         start=(f == 0), stop=(f == 31), reads=[bupt, B(f'wdn{f // 4}')], writes=[bp])
                k.act(ms[:, half * 512:(half + 1) * 512], k.ps[bank][:, :], ACT.Copy, reads=[bp], writes=[bms])
            k.stt('vector', JK, ms, 1.0, ms, ALU.mult, ALU.mult, reads=[bms], writes=[bc, B('mjk')], accum_out=cols[:, 0:1])
            rstd_from_ss(k, cols[:, 0:1], cols[:, 1:2], cols[:, 2:3], cols[:, 3:4], 1024.0, 1e-6, [bc])
            k.stt('vector', ms, ms, cols[:, 3:4], GP, ALU.mult, ALU.mult, reads=[bms, bc, bgp], writes=[bms])
            k.tt('gpsimd', ms, ms, xa, ALU.add, reads=[bms, bxa], writes=[bms])
            k.out_ticks.append(k.dma('sync', k.out[i * 128:(i + 1) * 128, :], ms, reads=[bms], sembuf=bms))


def host_inputs(inputs):
    T = host_tables()

    def g(k):
        return np.asarray(inputs[k], dtype=np.float32)
    w_in = np.ascontiguousarray(g('w_in')[0])
    qk = w_in[:, 1536:2560]
    idx = np.arange(1024)
    d = idx % 64
    perm = (idx - d) + (d + 32) % 64
    sh = {}
    sh['w_in'] = w_in
    sh['w_qkp'] = np.ascontiguousarray(qk[:, perm])
    cwb = np.concatenate([g('conv_w')[0], g('conv_b')[0][None, :]], axis=0)
    sh['cw'] = np.ascontiguousarray(cwb.reshape(4, 12, 128).transpose(2, 1, 0).reshape(128, 48))
    sh['fw1'] = np.ascontiguousarray(g('filt_w1')[0])
    sh['fw2'] = np.ascontiguousarray(g('filt_w2')[0])
    sh['fw3'] = np.ascontiguousarray(g('filt_w3')[0])
    sh['fw4'] = np.ascontiguousarray(g('filt_w4')[0])
    sh['fcol'] = np.ascontiguousarray(np.stack([g('filt_b1')[0], g('filt_b2')[0], g('filt_b3')[0], g('filt_freq')[0]], axis=1))
    sh['fbias'] = np.ascontiguousarray(g('filt_bias')[0].reshape(4, 128).T)
    sh['lamv'] = np.concatenate([g('lam_q1')[0], g('lam_k1')[0], g('lam_q2')[0], g('lam_k2')[0]])[None, :].copy()
    sh['subg'] = np.ascontiguousarray(g('subln_gain')[0][:, None])
    sh['w_out'] = np.ascontiguousarray(g('w_out')[0])
    sh['w_up'] = np.ascontiguousarray(g('w_up')[0])
    sh['w_down'] = np.ascontiguousarray(g('w_down')[0])
    sh['gcols'] = np.ascontiguousarray(np.stack([g('attn_pre_gain')[0].reshape(8, 128).T,
                                                 g('mlp_pre_gain')[0].reshape(8, 128).T], axis=1).reshape(128, 16))
    sh['gpost'] = np.ascontiguousarray(np.stack([g('attn_post_gain')[0], g('mlp_post_gain')[0]], axis=0))
    sh['ident_bf'] = T['ident_bf']
    sh['ident_f'] = T['ident_f']
    sh['F1'] = T['F1']
    sh['Tw1'] = T['Tw1'].reshape(128, 256)
    sh['CS4'] = T['CS4'].reshape(128, 512)
    sh['W3'] = T['W3'].reshape(128, 768)
    sh['Tw2'] = T['Tw2'].reshape(128, 64)
    sh['PHI'] = T['PHI'].reshape(128, 384)
    sh['zT'] = T['zT']
    sh['negt'] = T['negt']
    sh['absdelta'] = T['absdelta']
    sh['ropec'] = T['ropec']
    sh['ropes'] = T['ropes']
    return sh


PHASES = ['p0', 'pF', 'pH', 'pA', 'pO', 'pM']


def make_program(upto='pM', dumps=(), dump_fn=None, small_out=False, **attrs):
    k = K(dumps=dumps, small_out=small_out)
    k.__dict__.update(attrs)
    fns = {'p0': phase0, 'pF': phaseF, 'pH': phaseH, 'pA': phaseA, 'pO': phaseO, 'pM': phaseM}
    for ph in PHASES:
        fns[ph](k)
        k.barrier()
        if ph == upto:
            break
    ticks = []
    if dump_fn is not None:
        ticks += dump_fn(k)
    ticks += getattr(k, 'out_ticks', [])
    k.P.wait_all('sync', ticks)
    with k.nc.Block() as block:
        k.P.emit(block)
    k.es.close()
    return k.nc


def kernel(**inputs):
    sh = host_inputs(inputs)
    x = np.asarray(inputs['x'], dtype=np.float32)
    nc = make_program()
    in_maps = []
    for c in range(8):
        m = dict(sh)
        m['x'] = np.ascontiguousarray(x[c])
        in_maps.append(m)
    res = run_bass_kernel_spmd(nc, in_maps, core_ids=list(range(8)))
    return np.stack([np.asarray(r['out'], dtype=np.float32) for r in res.results], axis=0)
```
